# Optimizing a Trainium2 kernel written in Bass

```python
import math
import jax, jax.numpy as jnp
from jax import lax
import numpy as np

D_MODEL = 1024
BATCH = 16
SEQ = 4096
DEPTH = 1
DEC_BATCH = 1
DEC_SEQ = 16384
PAST_LEN = 128

ATT_HEADS = 16
ATT_KV_HEADS = 4
ATT_HEAD_DIM = 64
ATT_GROUP = ATT_HEADS // ATT_KV_HEADS
WINDOW = 128
ATT_BLOCK = 128
ROT_DIM = ATT_HEAD_DIM // 4
ROPE_THETA = 500000.0
NEG_BIG = -1e30
HG_HEADS = 8
HG_DK = 128
HG_DV = 128
HG_CHUNK = 64
D_FF = 2816
LN_EPS = 1e-5
RMS_EPS = 1e-6
DN_ALPHA = (2.0 * DEPTH) ** 0.25
DN_BETA = (8.0 * DEPTH) ** -0.25
COL_WIDTHS = (ATT_HEADS * ATT_HEAD_DIM, ATT_KV_HEADS * ATT_HEAD_DIM, ATT_KV_HEADS * ATT_HEAD_DIM,
              HG_HEADS * HG_DK, HG_HEADS * HG_DK, HG_HEADS * HG_DK,
              HG_HEADS * HG_DV, HG_HEADS * HG_DV, D_MODEL, D_MODEL)
SPLIT_POINTS = tuple(int(s) for s in np.cumsum(COL_WIDTHS)[:-1])
D_IN_PROJ = int(sum(COL_WIDTHS))

kernel_name = "hybrid_gated_swa_hgrn2_macaron_deepnorm"


def _layer_norm(x, g, b):
    xf = x.astype(jnp.float32)
    mu = jnp.mean(xf, axis=-1, keepdims=True)
    var = jnp.mean(jnp.square(xf - mu), axis=-1, keepdims=True)
    return ((xf - mu) * lax.rsqrt(var + LN_EPS)).astype(x.dtype) * g + b


def _swiglu(x, w_in, w_out):
    gate, up = jnp.split(x @ w_in, 2, axis=-1)
    return (jax.nn.silu(gate) * up) @ w_out


def _rope_partial(x, pos):
    half = ROT_DIM // 2
    inv = ROPE_THETA ** (-jnp.arange(half, dtype=jnp.float32) / half)
    ang = pos.astype(jnp.float32)[:, None] * inv[None, :]
    cos = jnp.cos(ang)[None, :, None, :].astype(x.dtype)
    sin = jnp.sin(ang)[None, :, None, :].astype(x.dtype)
    x1 = x[..., :half]
    x2 = x[..., half:ROT_DIM]
    return jnp.concatenate([x1 * cos - x2 * sin, x2 * cos + x1 * sin, x[..., ROT_DIM:]], axis=-1)


def _window_attention(q, k, v, sink):
    B, L = q.shape[0], q.shape[1]
    nb = L // ATT_BLOCK
    qb = q.reshape(B, nb, ATT_BLOCK, ATT_KV_HEADS, ATT_GROUP, ATT_HEAD_DIM)
    pad = ((0, 0), (ATT_BLOCK, ATT_BLOCK), (0, 0), (0, 0))
    kb = jnp.pad(k, pad).reshape(B, nb + 2, ATT_BLOCK, ATT_KV_HEADS, ATT_HEAD_DIM)
    vb = jnp.pad(v, pad).reshape(B, nb + 2, ATT_BLOCK, ATT_KV_HEADS, ATT_HEAD_DIM)
    kw = jnp.concatenate([kb[:, :-2], kb[:, 1:-1], kb[:, 2:]], axis=2)
    vw = jnp.concatenate([vb[:, :-2], vb[:, 1:-1], vb[:, 2:]], axis=2)
    qi = jnp.arange(ATT_BLOCK)[:, None]
    kj = jnp.arange(3 * ATT_BLOCK)[None, :] - ATT_BLOCK
    band = jnp.abs(kj - qi) <= WINDOW
    sink_l = sink.astype(jnp.float32).reshape(ATT_KV_HEADS, ATT_GROUP)[None, :, :, None, None]
    scale = ATT_HEAD_DIM ** -0.5

    def one_block(args):
        qn, kn, vn, n = args
        kpos = n * ATT_BLOCK + kj
        valid = band & (kpos >= 0) & (kpos < L)
        s = jnp.einsum('bqkgd,bskd->bkgqs', qn, kn).astype(jnp.float32) * scale
        s = jnp.where(valid, s, NEG_BIG)
        m = jnp.maximum(jnp.max(s, axis=-1, keepdims=True), sink_l)
        e = jnp.exp(s - m)
        p = e / (jnp.sum(e, axis=-1, keepdims=True) + jnp.exp(sink_l - m))
        return jnp.einsum('bkgqs,bskd->bqkgd', p.astype(vn.dtype), vn)

    out = lax.map(one_block, (jnp.swapaxes(qb, 0, 1), jnp.swapaxes(kw, 0, 1),
                              jnp.swapaxes(vw, 0, 1), jnp.arange(nb)))
    return jnp.swapaxes(out, 0, 1).reshape(B, L, ATT_HEADS * ATT_HEAD_DIM)


def _hgrn2_chunk_scan(q, k, v, logf):
    B, H, L, dk = q.shape
    dv = v.shape[-1]
    nc = L // HG_CHUNK
    q = q.reshape(B, H, nc, HG_CHUNK, dk)
    k = k.reshape(B, H, nc, HG_CHUNK, dk)
    v = v.reshape(B, H, nc, HG_CHUNK, dv)
    b = jnp.cumsum(logf.reshape(B, H, nc, HG_CHUNK, dk), axis=3)
    b_last = b[:, :, :, -1:, :]
    q_dec = q * jnp.exp(b)
    k_inv = k * jnp.exp(-b)
    k_end = k * jnp.exp(b_last - b)
    incl = jnp.tril(jnp.ones((HG_CHUNK, HG_CHUNK), dtype=bool))
    a = jnp.where(incl, jnp.einsum('bhnid,bhnjd->bhnij', q_dec, k_inv), 0.0)
    o_intra = jnp.einsum('bhnij,bhnje->bhnie', a, v)
    s_chunk = jnp.einsum('bhncd,bhnce->bhnde', k_end, v)
    decay = jnp.exp(b_last[:, :, :, 0, :])

    def step(s, inp):
        dec, sc = inp
        return dec[..., None] * s + sc, s

    s0 = jnp.zeros((B, H, dk, dv), jnp.float32)
    _, s_prev = lax.scan(step, s0, (jnp.moveaxis(decay, 2, 0), jnp.moveaxis(s_chunk, 2, 0)))
    s_prev = jnp.moveaxis(s_prev, 0, 2)
    o_inter = jnp.einsum('bhncd,bhnde->bhnce', q_dec, s_prev)
    return (o_intra + o_inter).reshape(B, H, L, dv)


def _hgrn2_branch(hq, hf_fwd, hf_bwd, hi, hg, lb_fwd, lb_bwd, norm_g):
    B, L = hq.shape[0], hq.shape[1]

    def heads(t):
        return t.astype(jnp.float32).reshape(B, L, HG_HEADS, -1).transpose(0, 2, 1, 3)

    q = heads(jax.nn.silu(hq))
    v = heads(hi)

    def direction(fz, lb, reverse):
        f = heads(lb + (1.0 - lb) * jax.nn.sigmoid(fz.astype(jnp.float32)))
        qq, kk, vv, lf = q, 1.0 - f, v, jnp.log(f)
        if reverse:
            qq, kk, vv, lf = (jnp.flip(t, axis=2) for t in (qq, kk, vv, lf))
        o = _hgrn2_chunk_scan(qq, kk, vv, lf)
        return jnp.flip(o, axis=2) if reverse else o

    o = direction(hf_fwd, lb_fwd, False) + direction(hf_bwd, lb_bwd, True)
    o = o * lax.rsqrt(jnp.mean(o * o, axis=-1, keepdims=True) + RMS_EPS) * norm_g.astype(jnp.float32)
    o = o.transpose(0, 2, 1, 3).reshape(B, L, HG_HEADS * HG_DV)
    return (o * jax.nn.silu(hg.astype(jnp.float32))).astype(hq.dtype)


def _mixer(x, w_in, sink, lb_fwd, lb_bwd, norm_g, w_o_attn, w_o_hgrn, w_out):
    B, L, _ = x.shape
    aq, ak, av, hq, hff, hfb, hi, hg, ga, gh = jnp.split(x @ w_in, SPLIT_POINTS, axis=-1)
    pos = jnp.arange(L)
    aq = _rope_partial(aq.reshape(B, L, ATT_HEADS, ATT_HEAD_DIM), pos)
    ak = _rope_partial(ak.reshape(B, L, ATT_KV_HEADS, ATT_HEAD_DIM), pos)
    av = av.reshape(B, L, ATT_KV_HEADS, ATT_HEAD_DIM)
    attn = _window_attention(aq, ak, av, sink) @ w_o_attn
    hgrn = _hgrn2_branch(hq, hff, hfb, hi, hg, lb_fwd, lb_bwd, norm_g) @ w_o_hgrn
    merged = jax.nn.sigmoid(ga) * attn + jax.nn.sigmoid(gh) * hgrn
    return merged @ w_out


def setup_inputs(seed: int = 0) -> dict:
    key = jax.random.key(seed)
    ks = jax.random.split(key, 20)
    f32 = jnp.float32
    nrm = lambda k, shape, s: jax.random.normal(k, shape, f32) * s
    return {
        "x_prompt": nrm(ks[0], (BATCH, SEQ, D_MODEL), 1.0),
        "x_sample": nrm(ks[1], (DEC_BATCH, DEC_SEQ, D_MODEL), 1.0),
        "ffn1_w_in": nrm(ks[2], (DEPTH, D_MODEL, 2 * D_FF), D_MODEL ** -0.5),
        "ffn1_w_out": nrm(ks[3], (DEPTH, D_FF, D_MODEL), DN_BETA * D_FF ** -0.5),
        "ln1_g": 1.0 + nrm(ks[4], (DEPTH, D_MODEL), 0.05),
        "ln1_b": nrm(ks[5], (DEPTH, D_MODEL), 0.02),
        "w_in": nrm(ks[6], (DEPTH, D_MODEL, D_IN_PROJ), D_MODEL ** -0.5),
        "attn_sink": nrm(ks[7], (DEPTH, ATT_HEADS), 0.5),
        "hgrn_lb": 1.0 + nrm(ks[8], (2, DEPTH + 1, HG_HEADS * HG_DK), 0.1),
        "hgrn_norm_g": 1.0 + nrm(ks[9], (DEPTH, HG_DV), 0.05),
        "w_o_attn": nrm(ks[10], (DEPTH, ATT_HEADS * ATT_HEAD_DIM, D_MODEL), (ATT_HEADS * ATT_HEAD_DIM) ** -0.5),
        "w_o_hgrn": nrm(ks[11], (DEPTH, HG_HEADS * HG_DV, D_MODEL), (HG_HEADS * HG_DV) ** -0.5),
        "w_out": nrm(ks[12], (DEPTH, D_MODEL, D_MODEL), DN_BETA * D_MODEL ** -0.5),
        "ln2_g": 1.0 + nrm(ks[13], (DEPTH, D_MODEL), 0.05),
        "ln2_b": nrm(ks[14], (DEPTH, D_MODEL), 0.02),
        "ffn2_w_in": nrm(ks[15], (DEPTH, D_MODEL, 2 * D_FF), D_MODEL ** -0.5),
        "ffn2_w_out": nrm(ks[16], (DEPTH, D_FF, D_MODEL), DN_BETA * D_FF ** -0.5),
        "ln3_g": 1.0 + nrm(ks[17], (DEPTH, D_MODEL), 0.05),
        "ln3_b": nrm(ks[18], (DEPTH, D_MODEL), 0.02),
    }


def reference(x_prompt, x_sample, ffn1_w_in, ffn1_w_out, ln1_g, ln1_b, w_in, attn_sink, hgrn_lb,
              hgrn_norm_g, w_o_attn, w_o_hgrn, w_out, ln2_g, ln2_b, ffn2_w_in, ffn2_w_out, ln3_g, ln3_b):
    lb_sched = jnp.cumsum(jax.nn.softmax(hgrn_lb.astype(jnp.float32), axis=1), axis=1)

    def trunk(x):
        for l in range(DEPTH):
            lb_f = lb_sched[0, l].astype(x.dtype)
            lb_b = lb_sched[1, l].astype(x.dtype)
            x = _layer_norm(DN_ALPHA * x + 0.5 * _swiglu(x, ffn1_w_in[l], ffn1_w_out[l]), ln1_g[l], ln1_b[l])
            mix = _mixer(x, w_in[l], attn_sink[l], lb_f, lb_b, hgrn_norm_g[l], w_o_attn[l], w_o_hgrn[l], w_out[l])
            x = _layer_norm(DN_ALPHA * x + mix, ln2_g[l], ln2_b[l])
            x = _layer_norm(DN_ALPHA * x + 0.5 * _swiglu(x, ffn2_w_in[l], ffn2_w_out[l]), ln3_g[l], ln3_b[l])
        return x

    y_prompt = trunk(x_prompt)
    y_sample = trunk(x_sample)
    return (y_prompt, y_sample)
```

```python
import numpy as np
from contextlib import ExitStack
import concourse.bass as bass
import concourse.mybir as mybir
from concourse.bass_utils import run_bass_kernel_spmd

F32 = mybir.dt.float32
BF16 = mybir.dt.bfloat16
AF = mybir.ActivationFunctionType
ALU = mybir.AluOpType
AX = mybir.AxisListType
ENGS = ("tensor", "vector", "scalar", "gpsimd", "sync")

D = 1024
KC = 8
DFF = 2816
FC = 22
ALPHA = float(2.0 ** 0.25)
SCALE = 0.125
NEG = -1.0e30
N_CORES = 8
TF = 512
ARENA_WORDS = 50432
EPOCH = 30000
SAME_ENGINE_SYNC = True
NO_SELF_SYNC = ()
DEPOCH = 1800


class Buf:
    __slots__ = ("last_w", "readers")

    def __init__(self):
        self.last_w = None
        self.readers = {}


class TL:
    __slots__ = ("ap", "b")

    def __init__(self, ap, b=None):
        if type(ap).__name__ != "AP":
            ap = ap[:]
        self.ap = ap
        self.b = b if b is not None else Buf()

    def __getitem__(self, k):
        return TL(self.ap[k], self.b)

    def bc(self, dt):
        return TL(self.ap.bitcast(dt), self.b)

    def r(self, pat, **kw):
        return TL(self.ap.rearrange(pat, **kw), self.b)


class Prog:
    def __init__(self, nc):
        self.nc = nc
        self.streams = {e: [] for e in ENGS}
        self.count = {e: 0 for e in ENGS}
        self.chan_count = {}
        self.waited = {e: {} for e in ENGS}
        self.live = {}
        self.nops = 0

    def op(self, eng, fn, reads=(), writes=(), chan=None):
        deps = {}

        def add(tok):
            if tok is not None and deps.get(tok[0], 0) < tok[1]:
                deps[tok[0]] = tok[1]
        for t in reads:
            add(t.b.last_w)
        for t in writes:
            add(t.b.last_w)
            for kv in t.b.readers.items():
                add(kv)
        waits = []
        w = self.waited[eng]
        for k, v in deps.items():
            if chan is None and k.startswith(eng + "#") and (eng == "tensor" or eng in NO_SELF_SYNC):
                continue
            if w.get(k, 0) >= v:
                continue
            w[k] = v
            waits.append((k, v))
        if chan is None:
            n = self.count[eng]
            self.count[eng] = n + 1
            key = "%s#%d" % (eng, n // EPOCH)
            tok = (key, n % EPOCH + 1)
            inc = (key, 1)
        else:
            n = self.chan_count.get(chan, 0)
            self.chan_count[chan] = n + 1
            key = "dma:%s#%d" % (chan, n // DEPOCH)
            tok = (key, (n % DEPOCH + 1) * 16)
            inc = (key, 16)
        self.live[tok[0]] = tok[1]
        self.streams[eng].append((waits, fn, inc))
        for t in reads:
            if t.b.readers.get(tok[0], 0) < tok[1]:
                t.b.readers[tok[0]] = tok[1]
        for t in writes:
            t.b.last_w = tok
            t.b.readers = {}
        self.nops += 1
        return tok

    def barrier(self):
        allk = list(self.live.items())
        for e in ENGS:
            waits = []
            for k, v in allk:
                if k.startswith(e + "#"):
                    continue
                if self.waited[e].get(k, 0) >= v:
                    continue
                self.waited[e][k] = v
                waits.append((k, v))
            if waits:
                self.streams[e].append((waits, None, None))

    def emit(self, stack):
        nc = self.nc
        keys = set()
        for e in ENGS:
            for waits, fn, inc in self.streams[e]:
                if inc is not None:
                    keys.add(inc[0])
        sems = {k: stack.enter_context(nc.semaphore("s_" + k.replace(":", "_").replace("#", "_"))) for k in sorted(keys)}
        self.nsems = len(sems)
        block = stack.enter_context(nc.Block())

        def make(ename):
            ops = self.streams[ename]

            def body(e):
                for waits, fn, inc in ops:
                    for k, v in waits:
                        e.wait_ge(sems[k], v)
                    if fn is not None:
                        fn(e).then_inc(sems[inc[0]], inc[1])
            return body
        for ename in ENGS:
            if self.streams[ename]:
                getattr(block, ename)(make(ename))


def build(nP, Lp, Ls, debug=False):
    Lse = Ls + 256
    NT = nP * Lp + Lse
    NB = nP * Lp + Ls
    S0 = nP * Lp
    nc = bass.Bass("TRN2", target_bir_lowering=False)
    P = Prog(nc)

    def din(name, shape):
        return nc.dram_tensor(name, list(shape), F32, kind="ExternalInput").ap()

    xT = din("xT", [D, NT])
    ctab = din("ctab", [128, 1536])
    par = din("par", [128, 128])
    ropeP = din("ropeP", [2, 128, Lp])
    ropeS = din("ropeS", [2, 128, Lse])
    w_ffn1_in = din("ffn1_w_in", [1, D, 2 * DFF])
    w_ffn1_out = din("ffn1_w_out", [1, DFF, D])
    w_in = din("w_in", [1, D, 8704])
    w_o_attn = din("w_o_attn", [1, D, D])
    w_o_hgrn = din("w_o_hgrn", [1, D, D])
    w_out = din("w_out", [1, D, D])
    w_ffn2_in = din("ffn2_w_in", [1, D, 2 * DFF])
    w_ffn2_out = din("ffn2_w_out", [1, DFF, D])
    yT = nc.dram_tensor("yT", [D, NB], F32, kind="ExternalOutput").ap()
    skind = "ExternalOutput" if debug else "Internal"
    X1 = nc.dram_tensor("X1s", [D, NT], F32, kind=skind).ap()
    G = nc.dram_tensor("Gs", [D, NT], F32, kind=skind).ap()
    OB = nc.dram_tensor("OBs", [D, NT], F32, kind=skind).ap()
    X2 = nc.dram_tensor("X2s", [D, NT], F32, kind=skind).ap()
    HGs = nc.dram_tensor("HGs", [D, NT], BF16, kind="Internal").ap()
    dX1, dG, dOB, dX2, dHG = TL(X1), TL(G), TL(OB), TL(X2), TL(HGs)

    def fm(ap2d):
        return ap2d.rearrange("(k p) n -> p k n", p=128)

    st = ExitStack()
    with st:
        def sbt(name, shape, dt):
            return st.enter_context(nc.sbuf_tensor(name, list(shape), dt))

        arena = sbt("arena", [128, ARENA_WORDS], F32)
        identb = TL(sbt("identb", [128, 128], BF16))
        identf = TL(sbt("identf", [128, 128], F32))
        onesD = TL(sbt("onesD", [128, 128], F32))
        onesDb = TL(sbt("onesDb", [128, 128], BF16))
        ones128 = TL(sbt("ones128", [128, 128], F32))
        ones1 = TL(sbt("ones1", [128, 64], F32))
        onesT = TL(sbt("onesT", [128, 256], F32))
        cst = TL(sbt("cst", [128, 8], F32))
        CT = TL(sbt("ctabs", [128, 1536], F32))
        PR = TL(sbt("pars", [128, 128], F32))
        DV = TL(sbt("derived", [128, 64], F32))
        banks = [TL(st.enter_context(nc.psum_tensor("pb%d" % i, [128, 512], F32))) for i in range(8)]
        pctr = [0]
        pset = [banks]

        def pool_set(lst):
            pset[0] = list(lst)

        def PS():
            t = pset[0][pctr[0] % len(pset[0])]
            pctr[0] += 1
            return t

        def MM(out, lhsT, rhs, start=True, stop=True):
            P.op("tensor", lambda e: e.matmul(out.ap, lhsT=lhsT.ap, rhs=rhs.ap, start=start, stop=stop),
                 [lhsT, rhs], [out])

        def TR(out, in_):
            P.op("tensor", lambda e: e.transpose(out.ap, in_.ap, identb.ap), [in_, identb], [out])

        def ACT(out, in_, func, bias=None, scale=None, accum=None):
            rd = [in_]
            kw = {}
            wr = [out]
            if accum is not None:
                kw["accum_out"] = accum.ap
                wr.append(accum)
            if bias is not None:
                kw["bias"] = bias.ap
                rd.append(bias)
            if scale is not None:
                if isinstance(scale, TL):
                    kw["scale"] = scale.ap
                    rd.append(scale)
                else:
                    kw["scale"] = float(scale)
            P.op("scalar", lambda e: e.activation(out=out.ap, in_=in_.ap, func=func, **kw), rd, wr)

        def TT(eng, out, in0, in1, op):
            P.op(eng, lambda e: e.tensor_tensor(out=out.ap, in0=in0.ap, in1=in1.ap, op=op), [in0, in1], [out])

        def TS(eng, out, in0, s1, s2, op0, op1=None):
            rd = [in0]
            a1 = s1.ap if isinstance(s1, TL) else float(s1)
            if isinstance(s1, TL):
                rd.append(s1)
            if s2 is None:
                P.op(eng, lambda e: e.tensor_scalar(out=out.ap, in0=in0.ap, scalar1=a1, scalar2=None, op0=op0), rd, [out])
                return
            a2 = s2.ap if isinstance(s2, TL) else float(s2)
            if isinstance(s2, TL):
                rd.append(s2)
            P.op(eng, lambda e: e.tensor_scalar(out=out.ap, in0=in0.ap, scalar1=a1, scalar2=a2, op0=op0, op1=op1), rd, [out])

        def STT(eng, out, in0, s, in1, op0, op1):
            rd = [in0, in1]
            a = s.ap if isinstance(s, TL) else float(s)
            if isinstance(s, TL):
                rd.append(s)
            P.op(eng, lambda e: e.scalar_tensor_tensor(out=out.ap, in0=in0.ap, scalar=a, in1=in1.ap, op0=op0, op1=op1), rd, [out])

        def CP(eng, out, in_):
            P.op(eng, lambda e: e.tensor_copy(out=out.ap, in_=in_.ap), [in_], [out])

        def MS(eng, out, val):
            P.op(eng, lambda e: e.memset(out.ap, val), [], [out])

        chmap = {}

        def DMA(out, in_, eng="sync"):
            ch = chmap.setdefault(id(out.b), "c%d" % len(chmap))
            P.op(eng, lambda e: e.dma_start(out=out.ap, in_=in_.ap), [in_], [out], chan=ch)

        def phase_end():
            P.barrier()
            chmap.clear()

        apos = [0]

        def areset():
            apos[0] = 0

        def af32(shape):
            n = int(np.prod(shape[1:]))
            v = arena[:, apos[0]:apos[0] + n]
            apos[0] += n
            assert apos[0] <= ARENA_WORDS, apos[0]
            if len(shape) == 3:
                v = v.rearrange("p (a b) -> p a b", a=shape[1])
            return TL(v)

        def abf(shape):
            n = int(np.prod(shape[1:]))
            assert n % 2 == 0
            v = arena[:, apos[0]:apos[0] + n // 2].bitcast(BF16)
            apos[0] += n // 2
            assert apos[0] <= ARENA_WORDS, apos[0]
            if len(shape) == 3:
                v = v.rearrange("p (a b) -> p a b", a=shape[1])
            elif len(shape) == 4:
                v = v.rearrange("p (a b c) -> p a b c", a=shape[1], b=shape[2])
            return TL(v)

        def wload(dst, src3, nk):
            for k in range(nk):
                DMA(dst[:, k, :], TL(src3[:, k, :]), eng="gpsimd")

        def tiles_of(c0, c1, T=256):
            out = []
            c = c0
            while c < c1:
                out.append((c, min(T, c1 - c)))
                c += T
            return out

        DMA(CT, TL(ctab))
        DMA(PR, TL(par))
        MS("gpsimd", identf, 0.0)
        P.op("gpsimd", lambda e: e.affine_select(out=identf.ap, in_=identf.ap, pattern=[[-1, 128]],
                                                 compare_op=ALU.not_equal, fill=1.0, base=0, channel_multiplier=1),
             [identf], [identf])
        CP("vector", identb, identf)
        MS("vector", onesD, 1.0 / D)
        MS("vector", onesDb, 1.0 / D)
        MS("vector", ones128, 1.0 / 128)
        MS("vector", ones1, 1.0)
        MS("vector", onesT, 1.0)
        MS("vector", cst[:, 0:1], 1e-5)
        MS("vector", cst[:, 1:2], 1e-6)
        MS("vector", cst[:, 2:3], 1.0)
        MS("vector", cst[:, 3:4], 0.0)
        MS("vector", cst[:, 4:5], 4e-5)
        epsLN, epsRMS, onec = cst[:, 0:1], cst[:, 1:2], cst[:, 2:3]
        TT("vector", DV[:, 0:8], PR[:, 56:64], PR[:, 64:72], ALU.subtract)
        TT("vector", DV[:, 8:16], PR[:, 72:80], PR[:, 80:88], ALU.subtract)
        ACT(DV[:, 0:16], DV[:, 0:16], AF.Sigmoid)
        TS("vector", DV[:, 16:32], DV[:, 0:16], -1.0, 1.0, ALU.mult, ALU.add)
        sinks = PR[:, 88:104]
        flagL, flagR = PR[:, 104:105], PR[:, 105:106]
        permT = CT[:, 0:128]
        triF, triB = CT[:, 128:256], CT[:, 256:384]
        maskMid, maskFirst, maskLast = CT[:, 384:768], CT[:, 768:1152], CT[:, 1152:1536]

        def layer_norm(R, T, gcol, tmp, eps=None):
            eps = epsLN if eps is None else eps
            psM, psV = PS(), PS()
            for c in range(KC):
                sq = tmp["sq"][c % 2].bc(BF16)
                ACT(sq[:, 0:T], R[c], AF.Square)
                MM(psM[:, 0:T], onesD, R[c], start=(c == 0), stop=(c == KC - 1))
                MM(psV[:, 0:T], onesDb, sq[:, 0:T], start=(c == 0), stop=(c == KC - 1))
            mean, m2, rstd = tmp["mean"][:, 0:T], tmp["m2"][:, 0:T], tmp["rstd"][:, 0:T]
            ACT(mean, psM[:, 0:T], AF.Copy)
            ACT(m2, psM[:, 0:T], AF.Square)
            TT("vector", m2, psV[:, 0:T], m2, ALU.subtract)
            ACT(m2, m2, AF.Sqrt, bias=eps, scale=1.0)
            P.op("vector", lambda e: e.reciprocal(out=rstd.ap, in_=m2.ap), [m2], [rstd])
            for c in range(KC):
                TT("vector", R[c], R[c], mean, ALU.subtract)
                TT("vector", R[c], R[c], rstd, ALU.mult)
                ACT(R[c], R[c], AF.Identity, bias=PR[:, gcol + 8 + c:gcol + 9 + c], scale=PR[:, gcol + c:gcol + c + 1])

        def ffn_phase(src, dsrc, dst, ddst, wi_ap, wo_ap, gcol, tlist):
            areset()
            WI = abf([128, KC, 2 * DFF])
            WO = abf([128, FC, D])
            XB = [abf([128, KC, TF]) for _ in range(2)]
            XR = [af32([128, KC, TF])] * 2
            H = abf([128, FC, TF])
            SG = [af32([128, TF])] * 2
            tmp = {"sq": [af32([128, TF])] * 2, "mean": af32([128, TF]),
                   "m2": af32([128, TF]), "rstd": af32([128, TF])}
            wload(WI, wi_ap[0].rearrange("(k p) n -> p k n", p=128), KC)
            wload(WO, wo_ap[0].rearrange("(k p) n -> p k n", p=128), FC)
            s3 = fm(src)
            d3 = fm(dst)
            for it, (sc, dc, T) in enumerate(tlist):
                xb, xr = XB[it % 2], XR[it % 2]
                DMA(xb[:, :, 0:T], TL(s3[:, :, sc:sc + T], dsrc.b), eng="gpsimd")
                DMA(xr[:, :, 0:T], TL(s3[:, :, sc:sc + T], dsrc.b))
                for j in range(FC):
                    pg, pu = PS(), PS()
                    for k in range(KC):
                        MM(pg[:, 0:T], WI[:, k, j * 128:(j + 1) * 128], xb[:, k, 0:T], k == 0, k == KC - 1)
                    for k in range(KC):
                        MM(pu[:, 0:T], WI[:, k, DFF + j * 128:DFF + (j + 1) * 128], xb[:, k, 0:T], k == 0, k == KC - 1)
                    sg = SG[j % 2]
                    ACT(sg[:, 0:T], pg[:, 0:T], AF.Silu)
                    TT("vector", H[:, j, 0:T], sg[:, 0:T], pu[:, 0:T], ALU.mult)
                R = []
                for c in range(KC):
                    po = PS()
                    for j in range(FC):
                        MM(po[:, 0:T], WO[:, j, c * 128:(c + 1) * 128], H[:, j, 0:T], j == 0, j == FC - 1)
                    rc = xr[:, c, 0:T]
                    STT("vector", rc, rc, 2.0 * ALPHA, po[:, 0:T], ALU.mult, ALU.add)
                    R.append(rc)
                layer_norm(R, T, gcol, tmp, eps=cst[:, 4:5])
                DMA(TL(d3[:, :, dc:dc + T], ddst.b), xr[:, :, 0:T])

        tl = [(c, c, T) for (c, T) in tiles_of(0, S0, TF)] + [(S0, S0, 128)] + \
             [(c, c, T) for (c, T) in tiles_of(S0 + 128, S0 + 128 + Ls, TF)] + [(S0 + 128 + Ls, S0 + 128 + Ls, 128)]
        ffn_phase(xT, TL(xT), X1, dX1, w_ffn1_in, w_ffn1_out, 0, tl)
        phase_end()

        segs = [(i * Lp, Lp, False) for i in range(nP)] + [(S0, Lse, True)]
        x13 = fm(X1)
        win3 = w_in[0].rearrange("(k p) n -> p k n", p=128)

        areset()
        Wq = abf([128, KC, 1024])
        Wkd = abf([128, KC, 512])
        Wv = abf([128, KC, 256])
        Wga = abf([128, KC, 1024])
        Woa = abf([128, KC, 1024])
        Lmax = max(Lp, Lse)
        KT = abf([128, 4, Lmax])
        Vt = abf([128, Lmax // 128, 256])
        XB = [abf([128, KC, 256]) for _ in range(2)]
        RC = [af32([128, 256]) for _ in range(2)]
        RS = [af32([128, 256]) for _ in range(2)]
        QT = abf([128, KC, 256])
        QTl = [TL(QT.ap[:, c, :]) for c in range(KC)]
        qf = [af32([128, 256]) for _ in range(3)]
        t1 = [af32([128, 256]) for _ in range(2)]
        t2 = [af32([128, 256]) for _ in range(2)]
        sm = [af32([128, 392]) for _ in range(4)]
        Eb = [abf([128, 392]) for _ in range(4)]
        ETb = [abf([128, 384]) for _ in range(4)]
        cols = [af32([128, 8]) for _ in range(4)]
        hq = [0]
        Otok = [abf([128, 1024]) for _ in range(2)]
        OT = abf([128, KC, 256])
        sga = [af32([128, 256]) for _ in range(2)]
        Gst = [af32([128, KC, 256]) for _ in range(2)]
        wload(Wq, win3[:, :, 0:1024], KC)
        for g in range(4):
            for hf in range(2):
                DMA(Wkd[:, :, g * 128 + hf * 64:g * 128 + hf * 64 + 64],
                    TL(win3[:, :, 1024 + g * 64:1024 + (g + 1) * 64]), eng="gpsimd")
        wload(Wv, win3[:, :, 1280:1536], KC)
        wload(Wga, win3[:, :, 6656:7680], KC)
        wload(Woa, w_o_attn[0].rearrange("(k p) n -> p k n", p=128), KC)

        def rope_a1(ps_in, T, i):
            q32 = qf[i % 3]
            ACT(q32[:, 0:T], ps_in, AF.Copy)
            return [q32, None, i]

        def rope_a2(st_, T):
            pp = PS()
            MM(pp[:, 0:T], permT, st_[0][:, 0:T])
            st_[1] = pp

        def rope_pipe(nchunks, T, rc, rs, proj, outs):
            sts = {}
            for t in range(nchunks + 2):
                if t < nchunks:
                    sts[t] = rope_a1(proj(t), T, t)
                if 0 <= t - 1 < nchunks:
                    rope_a2(sts[t - 1], T)
                if 0 <= t - 2 < nchunks:
                    rope_b(sts[t - 2], T, rc, rs, outs[t - 2])

        def rope_b(st_, T, rc, rs, out_bf):
            q32, pp, i = st_
            a, b_ = t1[i % 2], t2[i % 2]
            TT("gpsimd", a[:, 0:T], q32[:, 0:T], rc[:, 0:T], ALU.mult)
            TT("vector", b_[:, 0:T], pp[:, 0:T], rs[:, 0:T], ALU.mult)
            TT("vector", out_bf, a[:, 0:T], b_[:, 0:T], ALU.add)

        gi = [0]
        for (s0, L, halo) in segs:
            rope3 = ropeS if halo else ropeP
            nblk = L // 128
            seg_tiles = ([(0, 128)] + tiles_of(128, L - 128) + [(L - 128, 128)]) if halo else tiles_of(0, L)
            for it, (c0, T) in enumerate(seg_tiles):
                xb = XB[gi[0] % 2]
                rc, rs = RC[gi[0] % 2], RS[gi[0] % 2]
                gi[0] += 1
                DMA(xb[:, :, 0:T], TL(x13[:, :, s0 + c0:s0 + c0 + T], dX1.b), eng="gpsimd")
                DMA(rc[:, 0:T], TL(rope3[0, :, c0:c0 + T]))
                DMA(rs[:, 0:T], TL(rope3[1, :, c0:c0 + T]))
                def proj_k(g, xb=xb, T=T):
                    pk = PS()
                    for k in range(KC):
                        MM(pk[:, 0:T], Wkd[:, k, g * 128:(g + 1) * 128], xb[:, k, 0:T], k == 0, k == KC - 1)
                    return pk[:, 0:T]
                rope_pipe(4, T, rc, rs, proj_k, [KT[:, g, c0:c0 + T] for g in range(4)])
                for bi in range(T // 128):
                    blk = (c0 // 128) + bi
                    pv = PS()
                    for k in range(KC):
                        MM(pv[:, 0:256], xb[:, k, bi * 128:(bi + 1) * 128], Wv[:, k, :], k == 0, k == KC - 1)
                    if halo and blk == 0:
                        TS("vector", Vt[:, blk, :], pv[:, 0:256], flagL, None, ALU.mult)
                    elif halo and blk == nblk - 1:
                        TS("vector", Vt[:, blk, :], pv[:, 0:256], flagR, None, ALU.mult)
                    else:
                        ACT(Vt[:, blk, :], pv[:, 0:256], AF.Copy)
            body_tiles = tiles_of(128, L - 128) if halo else tiles_of(0, L)
            for it, (c0, T) in enumerate(body_tiles):
                xb = XB[gi[0] % 2]
                rc, rs = RC[gi[0] % 2], RS[gi[0] % 2]
                gst = Gst[gi[0] % 2]
                gi[0] += 1
                DMA(xb[:, :, 0:T], TL(x13[:, :, s0 + c0:s0 + c0 + T], dX1.b), eng="gpsimd")
                DMA(rc[:, 0:T], TL(rope3[0, :, c0:c0 + T]))
                DMA(rs[:, 0:T], TL(rope3[1, :, c0:c0 + T]))
                def proj_q(c, xb=xb, T=T):
                    pq = PS()
                    for k in range(KC):
                        MM(pq[:, 0:T], Wq[:, k, c * 128:(c + 1) * 128], xb[:, k, 0:T], k == 0, k == KC - 1)
                    return pq[:, 0:T]
                rope_pipe(KC, T, rc, rs, proj_q, [QTl[c][:, 0:T] for c in range(KC)])
                pool_set(banks[2:8])
                items = []
                for bi in range(T // 128):
                    n = c0 // 128 + bi
                    kb0, kb1 = max(n - 1, 0), min(n + 1, nblk - 1)
                    nk = kb1 - kb0 + 1
                    if halo and n == 1:
                        mask = maskFirst
                    elif halo and n == nblk - 2:
                        mask = maskLast
                    else:
                        mask = maskMid
                    moff = (kb0 - (n - 1)) * 128
                    for h in range(16):
                        items.append(dict(bi=bi, n=n, kb0=kb0, kb1=kb1, nk=nk, mk=mask[:, moff:moff + nk * 128], h=h, ot=Otok[n % 2]))

                def stage1(it_):
                    h, nk, bi = it_["h"], it_["nk"], it_["bi"]
                    g = h // 4
                    hp = (h % 2) * 64
                    i3 = hq[0] % 4
                    hq[0] += 1
                    it_["i3"] = i3
                    nkw = nk * 128
                    pS = PS()
                    MM(pS[:, 0:nkw], QTl[h // 2][hp:hp + 64, bi * 128:(bi + 1) * 128],
                       KT[hp:hp + 64, g, it_["kb0"] * 128:(it_["kb1"] + 1) * 128])
                    s_ = sm[i3]
                    e_ = Eb[i3]
                    cl = cols[i3]
                    CP("gpsimd", s_[:, nkw:nkw + 1], sinks[:, h:h + 1])
                    STT("vector", s_[:, 0:nkw], pS[:, 0:nkw], SCALE, it_["mk"], ALU.mult, ALU.add)
                    P.op("vector", lambda e, o=cl[:, 1:2], i_=s_[:, 0:nkw + 1]: e.reduce_max(out=o.ap, in_=i_.ap, axis=AX.X, negate=True), [s_], [cl])
                    ACT(e_[:, 0:nkw + 1], s_[:, 0:nkw + 1], AF.Exp, bias=cl[:, 1:2], scale=1.0, accum=cl[:, 3:4])

                def stage2(it_):
                    h, nk, i3 = it_["h"], it_["nk"], it_["i3"]
                    nkw = nk * 128
                    e_ = Eb[i3]
                    cl = cols[i3]
                    P.op("vector", lambda e, o=cl[:, 5:6], i_=cl[:, 3:4]: e.reciprocal(out=o.ap, in_=i_.ap), [cl], [cl])
                    pTb = PS().bc(BF16)
                    for kb in range(nk):
                        TR(pTb[:, kb * 128:(kb + 1) * 128], e_[:, kb * 128:(kb + 1) * 128])
                    ACT(ETb[i3][:, 0:nkw], pTb[:, 0:nkw], AF.Copy)

                def stage3(it_):
                    h, nk, bi, i3, ot = it_["h"], it_["nk"], it_["bi"], it_["i3"], it_["ot"]
                    g, j = h // 4, h % 4
                    et = ETb[i3]
                    cl = cols[i3]
                    pO = banks[g % 2]
                    for kb in range(nk):
                        MM(pO[:, j * 64:(j + 1) * 64], et[:, kb * 128:(kb + 1) * 128],
                           Vt[:, it_["kb0"] + kb, g * 64:(g + 1) * 64], kb == 0, kb == nk - 1)
                    ACT(ot[:, h * 64:(h + 1) * 64], pO[:, j * 64:(j + 1) * 64], AF.Copy, scale=cl[:, 5:6])
                    if h == 15:
                        for c in range(KC):
                            pT = PS().bc(BF16)
                            TR(pT[:, 0:128], ot[:, c * 128:(c + 1) * 128])
                            ACT(OT[:, c, bi * 128:(bi + 1) * 128], pT[:, 0:128], AF.Copy)

                nit = len(items)
                for t in range(nit + 2):
                    if t < nit:
                        stage1(items[t])
                    if 0 <= t - 1 < nit:
                        stage2(items[t - 1])
                    if 0 <= t - 2 < nit:
                        stage3(items[t - 2])
                pool_set(banks)
                for c in range(KC):
                    pa, pg = PS(), PS()
                    for f in range(KC):
                        MM(pa[:, 0:T], Woa[:, f, c * 128:(c + 1) * 128], OT[:, f, 0:T], f == 0, f == KC - 1)
                    for k in range(KC):
                        MM(pg[:, 0:T], Wga[:, k, c * 128:(c + 1) * 128], xb[:, k, 0:T], k == 0, k == KC - 1)
                    sg = sga[c % 2]
                    ACT(sg[:, 0:T], pg[:, 0:T], AF.Sigmoid)
                    TT("vector", gst[:, c, 0:T], sg[:, 0:T], pa[:, 0:T], ALU.mult)
                DMA(TL(fm(G)[:, :, s0 + c0:s0 + c0 + T], dG.b), gst[:, :, 0:T])
        phase_end()

        ob3, g3, x23 = fm(OB), fm(G), fm(X2)
        hg3 = HGs.rearrange("(k p) n -> p k n", p=128)

        def bcast(tl, shape):
            return TL(tl.ap.broadcast_to(list(shape)), tl.b)

        def hgrn_phase(bwd):
            direction = 1 if bwd else 0
            areset()
            Whq = abf([128, KC, 1024])
            Whf = abf([128, KC, 1024])
            Whi = abf([128, KC, 1024])
            Whg = None if bwd else abf([128, KC, 1024])
            XB = [abf([128, KC, 256]) for _ in range(2)]
            Vtk = [abf([128, 2, 1024]) for _ in range(2)]
            Qa, Fa, La, Ba = (af32([128, 8, 256]) for _ in range(4))
            QD = [abf([128, 8, 256]) for _ in range(2)]
            KI = [abf([128, 8, 256]) for _ in range(2)]
            KTt = [abf([128, 2, 8, 128]) for _ in range(2)]
            AS = [abf([128, 2, 8, 128]) for _ in range(2)]
            DEC = [af32([128, 8, 4]) for _ in range(2)]
            S32 = af32([128, 8, 128])
            Sbf = abf([128, 8, 128])
            tmpS = af32([128, 8, 128])
            if bwd:
                OBst = [af32([128, 8, 256]) for _ in range(2)]
            else:
                OBa, Oa, SQa, SILa = (af32([128, 8, 256]) for _ in range(4))
                HGo = [abf([128, 8, 256]) for _ in range(2)]
            wload(Whq, win3[:, :, 1536:2560], KC)
            wload(Whf, win3[:, :, 3584:4608] if bwd else win3[:, :, 2560:3584], KC)
            wload(Whi, win3[:, :, 4608:5632], KC)
            if not bwd:
                wload(Whg, win3[:, :, 5632:6656], KC)
            pool_set(banks[4:8])
            pob = banks[0:4]
            lo = 8 * direction
            work = []
            for (s0, L, halo) in segs:
                if halo:
                    tl_ = ([(0, 128)] if not bwd else []) + tiles_of(128, L - 128) + ([(L - 128, 128)] if bwd else [])
                else:
                    tl_ = tiles_of(0, L)
                if bwd:
                    tl_ = tl_[::-1]
                for ti, (c0, T) in enumerate(tl_):
                    work.append(dict(s0=s0, L=L, halo=halo, c0=c0, T=T, first=(ti == 0)))

            def prologue(w, it_g):
                s0, L, halo, c0, T = w["s0"], w["L"], w["halo"], w["c0"], w["T"]
                i2 = it_g % 2
                is_halo_tile = halo and (c0 == 0 or c0 == L - 128)
                xb = XB[i2]
                vt = Vtk[i2]
                nb_ = T // 128
                nch = T // 64
                DMA(xb[:, :, 0:T], TL(x13[:, :, s0 + c0:s0 + c0 + T], dX1.b), eng="gpsimd")
                for bi in range(nb_):
                    for hf in range(2):
                        pv = PS()
                        for k in range(KC):
                            MM(pv[:, 0:512], xb[:, k, bi * 128:(bi + 1) * 128], Whi[:, k, hf * 512:(hf + 1) * 512], k == 0, k == KC - 1)
                        if is_halo_tile:
                            TS("vector", vt[:, bi, hf * 512:(hf + 1) * 512], pv[:, 0:512], flagL if c0 == 0 else flagR, None, ALU.mult)
                        else:
                            ACT(vt[:, bi, hf * 512:(hf + 1) * 512], pv[:, 0:512], AF.Copy)
                for (W_, dst, fn, sc) in ((Whq, Qa, AF.Silu, 1.0), (Whf, Fa, AF.Sigmoid, -1.0)):
                    for m in range(4):
                        pq = PS()
                        for hh in range(2):
                            h = 2 * m + hh
                            for k in range(KC):
                                MM(pq[:, hh * T:(hh + 1) * T], W_[:, k, h * 128:(h + 1) * 128], xb[:, k, 0:T], k == 0, k == KC - 1)
                        ACT(dst[:, 2 * m:2 * m + 2, 0:T], pq[:, 0:2 * T].r("p (a t) -> p a t", a=2), fn, scale=sc)
                Qv, Fv, Lv, Bv = Qa[:, :, 0:T], Fa[:, :, 0:T], La[:, :, 0:T], Ba[:, :, 0:T]
                TT("vector", Fv, Fv, bcast(DV[:, 16 + lo:24 + lo].r("p (h o) -> p h o", o=1), [128, 8, T]), ALU.mult)
                ACT(Lv, Fv, AF.Ln, bias=onec, scale=-1.0)
                for h in range(8):
                    P.op("vector", lambda e, o=Ba[:, h, 0:T], d1=La[:, h, 0:T]: e.tensor_tensor_scan(
                        out=o.ap, data0=onesT.ap[:, 0:T], data1=d1.ap, initial=0.0, op0=ALU.mult, op1=ALU.add),
                        [La, onesT], [Ba])
                if not bwd:
                    B4f = Ba[:, :, 0:T].r("p h (c t) -> p h c t", t=64)
                    for c in range(nch - 1, 0, -1):
                        TT("vector", B4f[:, :, c, :], B4f[:, :, c, :], bcast(B4f[:, :, c - 1, 63:64], [128, 8, 64]), ALU.subtract)
                if bwd:
                    B4 = Bv.r("p h (c t) -> p h c t", t=64)
                    L4 = Lv.r("p h (c t) -> p h c t", t=64)
                    TT("vector", Lv, Lv, Bv, ALU.subtract)
                    TT("vector", L4, L4, bcast(B4[:, :, :, 63:64], [128, 8, nch, 64]), ALU.add)
                    ACT(Bv, Lv, AF.Exp)
                    ACT(Lv, Lv, AF.Exp, scale=-1.0)
                    E1, E2 = Bv, Lv
                else:
                    ACT(Lv, Bv, AF.Exp)
                    ACT(Bv, Bv, AF.Exp, scale=-1.0)
                    E1, E2 = Lv, Bv
                qd, ki, kT, as_, dec = QD[i2], KI[i2], KTt[i2], AS[i2], DEC[i2]
                TT("gpsimd", qd[:, :, 0:T], Qv, E1, ALU.mult)
                TT("vector", ki[:, :, 0:T], Fv, E2, ALU.mult)
                dcol = 0 if bwd else 63
                CP("vector", dec[:, :, 0:nch].r("p h (c o) -> p h c o", o=1),
                   E1.r("p h (c t) -> p h c t", t=64)[:, :, :, dcol:dcol + 1])
                for bi in range(nb_):
                    pT = PS().bc(BF16)
                    for h in range(8):
                        TR(pT[:, h * 128:(h + 1) * 128], ki[:, h, bi * 128:(bi + 1) * 128])
                    ACT(kT[:, bi, :, :], pT[:, 0:1024].r("p (h d) -> p h d", h=8), AF.Copy)
                if not is_halo_tile:
                    tri = triB if bwd else triF
                    tri_b = bcast(tri.r("p (o t) -> p o t", o=1), [128, 4, 128])
                    for bi in range(nb_):
                        for j in range(2):
                            pA = PS()
                            for hh in range(4):
                                h = 4 * j + hh
                                MM(pA[:, hh * 128:(hh + 1) * 128], ki[:, h, bi * 128:(bi + 1) * 128], qd[:, h, bi * 128:(bi + 1) * 128])
                            TT("vector", as_[:, bi, 4 * j:4 * j + 4, :], pA[:, 0:512].r("p (a t) -> p a t", a=4), tri_b, ALU.mult)
                w.update(i2=i2, is_halo_tile=is_halo_tile, nb_=nb_, nch=nch)

            def epilogue(w):
                s0, c0, T, i2, is_halo_tile, nb_ = w["s0"], w["c0"], w["T"], w["i2"], w["is_halo_tile"], w["nb_"]
                xb, vt, qd, kT, as_, dec = XB[i2], Vtk[i2], QD[i2], KTt[i2], AS[i2], DEC[i2]
                if w["first"]:
                    MS("gpsimd", S32, 0.0)
                    MS("gpsimd", Sbf, 0.0)
                if not bwd and not is_halo_tile:
                    DMA(OBa[:, :, 0:T], TL(ob3[:, :, s0 + c0:s0 + c0 + T], dOB.b))
                    for m in range(4):
                        pg = PS()
                        for hh in range(2):
                            h = 2 * m + hh
                            for k in range(KC):
                                MM(pg[:, hh * T:(hh + 1) * T], Whg[:, k, h * 128:(h + 1) * 128], xb[:, k, 0:T], k == 0, k == KC - 1)
                        ACT(SILa[:, 2 * m:2 * m + 2, 0:T], pg[:, 0:2 * T].r("p (a t) -> p a t", a=2), AF.Silu)
                blks = range(nb_ - 1, -1, -1) if bwd else range(nb_)
                for bi in blks:
                    if not is_halo_tile:
                        for h in range(8):
                            MM(pob[h // 2][:, (h % 2) * T + bi * 128:(h % 2) * T + (bi + 1) * 128],
                               vt[:, bi, h * 128:(h + 1) * 128], as_[:, bi, h, :], (bi == blks[0] and h % 2 == 0), False)
                    chs = (1, 0) if bwd else (0, 1)
                    for ci, cc in enumerate(chs):
                        c = bi * 2 + cc
                        pU = [PS(), PS()]
                        for h in range(8):
                            MM(pU[h // 4][:, (h % 4) * 128:(h % 4 + 1) * 128], kT[cc * 64:(cc + 1) * 64, bi, h, :],
                               vt[cc * 64:(cc + 1) * 64, bi, h * 128:(h + 1) * 128], h % 4 == 0, True)
                        if not is_halo_tile:
                            for h in range(8):
                                MM(pob[h // 2][:, (h % 2) * T + c * 64:(h % 2) * T + (c + 1) * 64],
                                   Sbf[:, h, :], qd[:, h, c * 64:(c + 1) * 64], False, ci == 1)
                        for j in range(2):
                            TT("vector", tmpS[:, 4 * j:4 * j + 4, :], pU[j][:, 0:512].r("p (a t) -> p a t", a=4),
                               S32[:, 4 * j:4 * j + 4, :], ALU.add)
                        dec_b = bcast(dec[:, :, c:c + 1], [128, 8, 128])
                        TT("vector", S32, tmpS, dec_b, ALU.mult)
                        ACT(Sbf, S32, AF.Copy)
                if is_halo_tile:
                    return
                if bwd:
                    ost = OBst[i2]
                    for m in range(4):
                        ACT(ost[:, 2 * m:2 * m + 2, 0:T], pob[m][:, 0:2 * T].r("p (a t) -> p a t", a=2), AF.Copy)
                    DMA(TL(ob3[:, :, s0 + c0:s0 + c0 + T], dOB.b), ost[:, :, 0:T])
                else:
                    Ov, SQv, SIv = Oa[:, :, 0:T], SQa[:, :, 0:T], SILa[:, :, 0:T]
                    for m in range(4):
                        TT("vector", Oa[:, 2 * m:2 * m + 2, 0:T], pob[m][:, 0:2 * T].r("p (a t) -> p a t", a=2),
                           OBa[:, 2 * m:2 * m + 2, 0:T], ALU.add)
                    TT("vector", SQv, Ov, Ov, ALU.mult)
                    for m in range(4):
                        pm = PS()
                        for hh in range(2):
                            MM(pm[:, hh * T:(hh + 1) * T], ones128, SQa[:, 2 * m + hh, 0:T])
                        ACT(SQa[:, 2 * m:2 * m + 2, 0:T], pm[:, 0:2 * T].r("p (a t) -> p a t", a=2), AF.Ln, bias=epsRMS, scale=1.0)
                    ACT(SQv, SQv, AF.Exp, scale=-0.5)
                    TT("vector", Ov, Ov, SQv, ALU.mult)
                    ho = HGo[i2]
                    STT("vector", ho[:, :, 0:T], Ov, PR[:, 48:49], SIv, ALU.mult, ALU.mult)
                    DMA(TL(hg3[:, :, s0 + c0:s0 + c0 + T], dHG.b), ho[:, :, 0:T])

            prologue(work[0], 0)
            for wi in range(len(work)):
                if wi + 1 < len(work):
                    prologue(work[wi + 1], wi + 1)
                epilogue(work[wi])
            pool_set(banks)

        hgrn_phase(True)
        phase_end()
        hgrn_phase(False)
        phase_end()

        areset()
        T3 = 512
        Woh = abf([128, KC, 1024])
        Wgh = abf([128, KC, 1024])
        Wo = abf([128, KC, 1024])
        HGt = [abf([128, KC, T3]) for _ in range(2)]
        XB = [abf([128, KC, T3]) for _ in range(2)]
        Gt2 = [af32([128, KC, T3]) for _ in range(2)]
        X1r2 = [af32([128, KC, T3]) for _ in range(2)]
        sgh = [af32([128, T3]) for _ in range(2)]
        mg = abf([128, KC, T3])
        tmp = {"sq": [af32([128, T3]) for _ in range(2)], "mean": af32([128, T3]),
               "m2": af32([128, T3]), "rstd": af32([128, T3])}
        wload(Wgh, win3[:, :, 7680:8704], KC)
        wload(Woh, w_o_hgrn[0].rearrange("(k p) n -> p k n", p=128), KC)
        wload(Wo, w_out[0].rearrange("(k p) n -> p k n", p=128), KC)
        work3 = []
        for (s0, L, halo) in segs:
            for (c0, T) in (tiles_of(128, L - 128, T3) if halo else tiles_of(0, L, T3)):
                work3.append((s0, c0, T))

        def b3_loads(wi):
            s0, c0, T = work3[wi]
            i2 = wi % 2
            cs_ = slice(s0 + c0, s0 + c0 + T)
            DMA(XB[i2][:, :, 0:T], TL(x13[:, :, cs_], dX1.b), eng="gpsimd")
            DMA(HGt[i2][:, :, 0:T], TL(hg3[:, :, cs_], dHG.b))
            DMA(Gt2[i2][:, :, 0:T], TL(g3[:, :, cs_], dG.b))
            DMA(X1r2[i2][:, :, 0:T], TL(x13[:, :, cs_], dX1.b))

        b3_loads(0)
        for wi in range(len(work3)):
            if True:
                s0, c0, T = work3[wi]
                i2 = wi % 2
                if wi + 1 < len(work3):
                    b3_loads(wi + 1)
                xb, hgt = XB[i2], HGt[i2]
                Gt, X1r = Gt2[i2], X1r2[i2]
                cs_ = slice(s0 + c0, s0 + c0 + T)
                for c in range(KC):
                    ph, pg = PS(), PS()
                    for f in range(KC):
                        MM(ph[:, 0:T], Woh[:, f, c * 128:(c + 1) * 128], hgt[:, f, 0:T], f == 0, f == KC - 1)
                    for k in range(KC):
                        MM(pg[:, 0:T], Wgh[:, k, c * 128:(c + 1) * 128], xb[:, k, 0:T], k == 0, k == KC - 1)
                    sg = sgh[c % 2][:, 0:T]
                    ACT(sg, pg[:, 0:T], AF.Sigmoid)
                    TT("vector", sg, sg, ph[:, 0:T], ALU.mult)
                    TT("vector", mg[:, c, 0:T], sg, Gt[:, c, 0:T], ALU.add)
                R = []
                for c in range(KC):
                    px = PS()
                    for f in range(KC):
                        MM(px[:, 0:T], Wo[:, f, c * 128:(c + 1) * 128], mg[:, f, 0:T], f == 0, f == KC - 1)
                    rc = X1r[:, c, 0:T]
                    STT("vector", rc, rc, ALPHA, px[:, 0:T], ALU.mult, ALU.add)
                    R.append(rc)
                layer_norm(R, T, 16, tmp)
                DMA(TL(x23[:, :, cs_], dX2.b), X1r[:, :, 0:T])
        phase_end()

        tl = [(c, c, T) for (c, T) in tiles_of(0, S0, TF)] + [(c, c - 128, T) for (c, T) in tiles_of(S0 + 128, S0 + 128 + Ls, TF)]
        dY = TL(yT)
        ffn_phase(X2, dX2, yT, dY, w_ffn2_in, w_ffn2_out, 32, tl)
        phase_end()
        P.emit(st)
    return nc, P


def _const_tables(core, n_cores):
    ct = np.zeros((128, 1536), np.float32)
    d = np.arange(128)
    dd = d % 64
    perm = np.where(dd < 8, d + 8, np.where(dd < 16, d - 8, d))
    ct[perm, d] = 1.0
    j = np.arange(128)[:, None]
    i = np.arange(128)[None, :]
    same = (j // 64) == (i // 64)
    ct[:, 128:256] = (same & (j <= i)).astype(np.float32)
    ct[:, 256:384] = (same & (j >= i)).astype(np.float32)
    qi = np.arange(128)[:, None]
    kj = np.arange(384)[None, :] - 128
    band = np.abs(kj - qi) <= 128
    mid = np.where(band, 0.0, NEG).astype(np.float32)
    ct[:, 384:768] = mid
    first = mid.copy()
    last = mid.copy()
    if core == 0:
        first[:, 0:128] = NEG
    if core == n_cores - 1:
        last[:, 256:384] = NEG
    ct[:, 768:1152] = first
    ct[:, 1152:1536] = last
    return ct


def _rope_table(pos):
    half = 8
    inv = (500000.0 ** (-np.arange(half, dtype=np.float32) / half)).astype(np.float32)
    ang = pos.astype(np.float32)[:, None] * inv[None, :]
    cos = np.cos(ang).astype(np.float32)
    sin = np.sin(ang).astype(np.float32)
    L = pos.shape[0]
    tab = np.zeros((2, 128, L), np.float32)
    tab[0] = 1.0
    for base in (0, 64):
        tab[0, base:base + 8] = cos.T
        tab[0, base + 8:base + 16] = cos.T
        tab[1, base:base + 8] = -sin.T
        tab[1, base + 8:base + 16] = sin.T
    return tab


def _fm8(v):
    return np.ascontiguousarray(v.reshape(8, 128).T)


def make_in_maps(inp, n_cores, nP, Lp, Ls):
    xp = np.asarray(inp["x_prompt"], np.float32)
    xs = np.asarray(inp["x_sample"], np.float32)[0]
    Lse = Ls + 256
    par = np.zeros((128, 128), np.float32)
    for i, nm in enumerate(["ln1_g", "ln1_b", "ln2_g", "ln2_b", "ln3_g", "ln3_b"]):
        par[:, 8 * i:8 * i + 8] = _fm8(np.asarray(inp[nm], np.float32)[0])
    par[:, 48] = np.asarray(inp["hgrn_norm_g"], np.float32)[0]
    lb = np.asarray(inp["hgrn_lb"], np.float32)
    par[:, 56:64] = _fm8(lb[0, 0])
    par[:, 64:72] = _fm8(lb[0, 1])
    par[:, 72:80] = _fm8(lb[1, 0])
    par[:, 80:88] = _fm8(lb[1, 1])
    par[:, 88:104] = np.asarray(inp["attn_sink"], np.float32)[0][None, :]
    ropeP = _rope_table(np.arange(Lp))
    wnames = ["ffn1_w_in", "ffn1_w_out", "w_in", "w_o_attn", "w_o_hgrn", "w_out", "ffn2_w_in", "ffn2_w_out"]
    wts = {k: np.ascontiguousarray(np.asarray(inp[k], np.float32)) for k in wnames}
    maps = []
    for c in range(n_cores):
        cols = [xp[c * nP + i].T for i in range(nP)]
        ext = np.zeros((Lse, D), np.float32)
        a, b = c * Ls - 128, (c + 1) * Ls + 128
        a2, b2 = max(a, 0), min(b, xs.shape[0])
        ext[a2 - a:b2 - a] = xs[a2:b2]
        cols.append(ext.T)
        p = par.copy()
        p[:, 104] = 1.0 if c > 0 else 0.0
        p[:, 105] = 1.0 if c < n_cores - 1 else 0.0
        m = {"xT": np.ascontiguousarray(np.concatenate(cols, axis=1)),
             "ctab": _const_tables(c, n_cores), "par": p, "ropeP": ropeP,
             "ropeS": _rope_table(np.arange(a, b))}
        m.update(wts)
        maps.append(m)
    return maps


def gather(results, n_cores, nP, Lp, Ls):
    yp = np.zeros((n_cores * nP, Lp, D), np.float32)
    ys = np.zeros((1, n_cores * Ls, D), np.float32)
    for c in range(n_cores):
        y = results[c]["yT"]
        for i in range(nP):
            yp[c * nP + i] = y[:, i * Lp:(i + 1) * Lp].T
        ys[0, c * Ls:(c + 1) * Ls] = y[:, nP * Lp:].T
    return yp, ys


_CACHE = {}


def kernel(**inputs):
    nP, Lp, Ls = 2, 4096, 2048
    if "nc" not in _CACHE:
        _CACHE["nc"] = build(nP, Lp, Ls)[0]
    nc = _CACHE["nc"]
    maps = make_in_maps(inputs, N_CORES, nP, Lp, Ls)
    res = run_bass_kernel_spmd(nc, maps, core_ids=list(range(N_CORES)))
    return gather(res.results, N_CORES, nP, Lp, Ls)
```

```python
import numpy as np
from contextlib import ExitStack
import concourse.bass as bass
import concourse.mybir as mybir
from concourse.bass_utils import run_bass_kernel_spmd

F32 = mybir.dt.float32
BF16 = mybir.dt.bfloat16
AF = mybir.ActivationFunctionType
ALU = mybir.AluOpType
AX = mybir.AxisListType
ENGS = ("tensor", "vector", "scalar", "gpsimd", "sync")

D = 1024
KC = 8
DFF = 2816
FC = 22
ALPHA = float(2.0 ** 0.25)
SCALE = 0.125
NEG = -1.0e30
N_CORES = 8
TF = 512
ARENA_WORDS = 50432
EPOCH = 30000
SAME_ENGINE_SYNC = True
NO_SELF_SYNC = ()
DEPOCH = 1800


class Buf:
    __slots__ = ("last_w", "readers")

    def __init__(self):
        self.last_w = None
        self.readers = {}


class TL:
    __slots__ = ("ap", "b")

    def __init__(self, ap, b=None):
        if type(ap).__name__ != "AP":
            ap = ap[:]
        self.ap = ap
        self.b = b if b is not None else Buf()

    def __getitem__(self, k):
        return TL(self.ap[k], self.b)

    def bc(self, dt):
        return TL(self.ap.bitcast(dt), self.b)

    def r(self, pat, **kw):
        return TL(self.ap.rearrange(pat, **kw), self.b)


class Prog:
    def __init__(self, nc):
        self.nc = nc
        self.streams = {e: [] for e in ENGS}
        self.count = {e: 0 for e in ENGS}
        self.chan_count = {}
        self.waited = {e: {} for e in ENGS}
        self.live = {}
        self.nops = 0

    def op(self, eng, fn, reads=(), writes=(), chan=None):
        deps = {}

        def add(tok):
            if tok is not None and deps.get(tok[0], 0) < tok[1]:
                deps[tok[0]] = tok[1]
        for t in reads:
            add(t.b.last_w)
        for t in writes:
            add(t.b.last_w)
            for kv in t.b.readers.items():
                add(kv)
        waits = []
        w = self.waited[eng]
        for k, v in deps.items():
            if chan is None and k.startswith(eng + "#") and (eng == "tensor" or eng in NO_SELF_SYNC):
                continue
            if w.get(k, 0) >= v:
                continue
            w[k] = v
            waits.append((k, v))
        if chan is None:
            n = self.count[eng]
            self.count[eng] = n + 1
            key = "%s#%d" % (eng, n // EPOCH)
            tok = (key, n % EPOCH + 1)
            inc = (key, 1)
        else:
            n = self.chan_count.get(chan, 0)
            self.chan_count[chan] = n + 1
            key = "dma:%s#%d" % (chan, n // DEPOCH)
            tok = (key, (n % DEPOCH + 1) * 16)
            inc = (key, 16)
        self.live[tok[0]] = tok[1]
        self.streams[eng].append((waits, fn, inc))
        for t in reads:
            if t.b.readers.get(tok[0], 0) < tok[1]:
                t.b.readers[tok[0]] = tok[1]
        for t in writes:
            t.b.last_w = tok
            t.b.readers = {}
        self.nops += 1
        return tok

    def barrier(self):
        allk = list(self.live.items())
        for e in ENGS:
            waits = []
            for k, v in allk:
                if k.startswith(e + "#"):
                    continue
                if self.waited[e].get(k, 0) >= v:
                    continue
                self.waited[e][k] = v
                waits.append((k, v))
            if waits:
                self.streams[e].append((waits, None, None))

    def emit(self, stack):
        nc = self.nc
        keys = set()
        for e in ENGS:
            for waits, fn, inc in self.streams[e]:
                if inc is not None:
                    keys.add(inc[0])
        sems = {k: stack.enter_context(nc.semaphore("s_" + k.replace(":", "_").replace("#", "_"))) for k in sorted(keys)}
        self.nsems = len(sems)
        block = stack.enter_context(nc.Block())

        def make(ename):
            ops = self.streams[ename]

            def body(e):
                for waits, fn, inc in ops:
                    for k, v in waits:
                        e.wait_ge(sems[k], v)
                    if fn is not None:
                        fn(e).then_inc(sems[inc[0]], inc[1])
            return body
        for ename in ENGS:
            if self.streams[ename]:
                getattr(block, ename)(make(ename))


def build(nP, Lp, Ls, debug=False):
    Lse = Ls + 256
    NT = nP * Lp + Lse
    NB = nP * Lp + Ls
    S0 = nP * Lp
    nc = bass.Bass("TRN2", target_bir_lowering=False)
    P = Prog(nc)

    def din(name, shape):
        return nc.dram_tensor(name, list(shape), F32, kind="ExternalInput").ap()

    xT = din("xT", [D, NT])
    ctab = din("ctab", [128, 1536])
    par = din("par", [128, 128])
    ropeP = din("ropeP", [2, 128, Lp])
    ropeS = din("ropeS", [2, 128, Lse])
    w_ffn1_in = din("ffn1_w_in", [1, D, 2 * DFF])
    w_ffn1_out = din("ffn1_w_out", [1, DFF, D])
    w_in = din("w_in", [1, D, 8704])
    w_o_attn = din("w_o_attn", [1, D, D])
    w_o_hgrn = din("w_o_hgrn", [1, D, D])
    w_out = din("w_out", [1, D, D])
    w_ffn2_in = din("ffn2_w_in", [1, D, 2 * DFF])
    w_ffn2_out = din("ffn2_w_out", [1, DFF, D])
    yT = nc.dram_tensor("yT", [D, NB], F32, kind="ExternalOutput").ap()
    skind = "ExternalOutput" if debug else "Internal"
    X1 = nc.dram_tensor("X1s", [D, NT], F32, kind=skind).ap()
    G = nc.dram_tensor("Gs", [D, NT], F32, kind=skind).ap()
    OB = nc.dram_tensor("OBs", [D, NT], F32, kind=skind).ap()
    X2 = nc.dram_tensor("X2s", [D, NT], F32, kind=skind).ap()
    HGs = nc.dram_tensor("HGs", [D, NT], BF16, kind="Internal").ap()
    dX1, dG, dOB, dX2, dHG = TL(X1), TL(G), TL(OB), TL(X2), TL(HGs)

    def fm(ap2d):
        return ap2d.rearrange("(k p) n -> p k n", p=128)

    st = ExitStack()
    with st:
        def sbt(name, shape, dt):
            return st.enter_context(nc.sbuf_tensor(name, list(shape), dt))

        arena = sbt("arena", [128, ARENA_WORDS], F32)
        identb = TL(sbt("identb", [128, 128], BF16))
        identf = TL(sbt("identf", [128, 128], F32))
        onesD = TL(sbt("onesD", [128, 128], F32))
        onesDb = TL(sbt("onesDb", [128, 128], BF16))
        ones128 = TL(sbt("ones128", [128, 128], F32))
        ones1 = TL(sbt("ones1", [128, 64], F32))
        onesT = TL(sbt("onesT", [128, 256], F32))
        cst = TL(sbt("cst", [128, 8], F32))
        CT = TL(sbt("ctabs", [128, 1536], F32))
        PR = TL(sbt("pars", [128, 128], F32))
        DV = TL(sbt("derived", [128, 64], F32))
        banks = [TL(st.enter_context(nc.psum_tensor("pb%d" % i, [128, 512], F32))) for i in range(8)]
        pctr = [0]
        pset = [banks]

        def pool_set(lst):
            pset[0] = list(lst)

        def PS():
            t = pset[0][pctr[0] % len(pset[0])]
            pctr[0] += 1
            return t

        def MM(out, lhsT, rhs, start=True, stop=True):
            P.op("tensor", lambda e: e.matmul(out.ap, lhsT=lhsT.ap, rhs=rhs.ap, start=start, stop=stop),
                 [lhsT, rhs], [out])

        def TR(out, in_):
            P.op("tensor", lambda e: e.transpose(out.ap, in_.ap, identb.ap), [in_, identb], [out])

        def ACT(out, in_, func, bias=None, scale=None, accum=None):
            rd = [in_]
            kw = {}
            wr = [out]
            if accum is not None:
                kw["accum_out"] = accum.ap
                wr.append(accum)
            if bias is not None:
                kw["bias"] = bias.ap
                rd.append(bias)
            if scale is not None:
                if isinstance(scale, TL):
                    kw["scale"] = scale.ap
                    rd.append(scale)
                else:
                    kw["scale"] = float(scale)
            P.op("scalar", lambda e: e.activation(out=out.ap, in_=in_.ap, func=func, **kw), rd, wr)

        def TT(eng, out, in0, in1, op):
            P.op(eng, lambda e: e.tensor_tensor(out=out.ap, in0=in0.ap, in1=in1.ap, op=op), [in0, in1], [out])

        def TS(eng, out, in0, s1, s2, op0, op1=None):
            rd = [in0]
            a1 = s1.ap if isinstance(s1, TL) else float(s1)
            if isinstance(s1, TL):
                rd.append(s1)
            if s2 is None:
                P.op(eng, lambda e: e.tensor_scalar(out=out.ap, in0=in0.ap, scalar1=a1, scalar2=None, op0=op0), rd, [out])
                return
            a2 = s2.ap if isinstance(s2, TL) else float(s2)
            if isinstance(s2, TL):
                rd.append(s2)
            P.op(eng, lambda e: e.tensor_scalar(out=out.ap, in0=in0.ap, scalar1=a1, scalar2=a2, op0=op0, op1=op1), rd, [out])

        def STT(eng, out, in0, s, in1, op0, op1):
            rd = [in0, in1]
            a = s.ap if isinstance(s, TL) else float(s)
            if isinstance(s, TL):
                rd.append(s)
            P.op(eng, lambda e: e.scalar_tensor_tensor(out=out.ap, in0=in0.ap, scalar=a, in1=in1.ap, op0=op0, op1=op1), rd, [out])

        def CP(eng, out, in_):
            P.op(eng, lambda e: e.tensor_copy(out=out.ap, in_=in_.ap), [in_], [out])

        def MS(eng, out, val):
            P.op(eng, lambda e: e.memset(out.ap, val), [], [out])

        chmap = {}

        def DMA(out, in_, eng="sync"):
            ch = chmap.setdefault(id(out.b), "c%d" % len(chmap))
            P.op(eng, lambda e: e.dma_start(out=out.ap, in_=in_.ap), [in_], [out], chan=ch)

        def phase_end():
            P.barrier()
            chmap.clear()

        apos = [0]

        def areset():
            apos[0] = 0

        def af32(shape):
            n = int(np.prod(shape[1:]))
            v = arena[:, apos[0]:apos[0] + n]
            apos[0] += n
            assert apos[0] <= ARENA_WORDS, apos[0]
            if len(shape) == 3:
                v = v.rearrange("p (a b) -> p a b", a=shape[1])
            return TL(v)

        def abf(shape):
            n = int(np.prod(shape[1:]))
            assert n % 2 == 0
            v = arena[:, apos[0]:apos[0] + n // 2].bitcast(BF16)
            apos[0] += n // 2
            assert apos[0] <= ARENA_WORDS, apos[0]
            if len(shape) == 3:
                v = v.rearrange("p (a b) -> p a b", a=shape[1])
            elif len(shape) == 4:
                v = v.rearrange("p (a b c) -> p a b c", a=shape[1], b=shape[2])
            return TL(v)

        def wload(dst, src3, nk):
            for k in range(nk):
                DMA(dst[:, k, :], TL(src3[:, k, :]), eng="gpsimd")

        def tiles_of(c0, c1, T=256):
            out = []
            c = c0
            while c < c1:
                out.append((c, min(T, c1 - c)))
                c += T
            return out

        DMA(CT, TL(ctab))
        DMA(PR, TL(par))
        MS("gpsimd", identf, 0.0)
        P.op("gpsimd", lambda e: e.affine_select(out=identf.ap, in_=identf.ap, pattern=[[-1, 128]],
                                                 compare_op=ALU.not_equal, fill=1.0, base=0, channel_multiplier=1),
             [identf], [identf])
        CP("vector", identb, identf)
        MS("vector", onesD, 1.0 / D)
        MS("vector", onesDb, 1.0 / D)
        MS("vector", ones128, 1.0 / 128)
        MS("vector", ones1, 1.0)
        MS("vector", onesT, 1.0)
        MS("vector", cst[:, 0:1], 1e-5)
        MS("vector", cst[:, 1:2], 1e-6)
        MS("vector", cst[:, 2:3], 1.0)
        MS("vector", cst[:, 3:4], 0.0)
        MS("vector", cst[:, 4:5], 4e-5)
        epsLN, epsRMS, onec = cst[:, 0:1], cst[:, 1:2], cst[:, 2:3]
        TT("vector", DV[:, 0:8], PR[:, 56:64], PR[:, 64:72], ALU.subtract)
        TT("vector", DV[:, 8:16], PR[:, 72:80], PR[:, 80:88], ALU.subtract)
        ACT(DV[:, 0:16], DV[:, 0:16], AF.Sigmoid)
        TS("vector", DV[:, 16:32], DV[:, 0:16], -1.0, 1.0, ALU.mult, ALU.add)
        sinks = PR[:, 88:104]
        flagL, flagR = PR[:, 104:105], PR[:, 105:106]
        permT = CT[:, 0:128]
        triF, triB = CT[:, 128:256], CT[:, 256:384]
        maskMid, maskFirst, maskLast = CT[:, 384:768], CT[:, 768:1152], CT[:, 1152:1536]

        def layer_norm(R, T, gcol, tmp, eps=None, defer=False):
            eps = epsLN if eps is None else eps
            psM, psV = PS(), PS()
            for c in range(KC):
                sq = tmp["sq"][c % 2].bc(BF16)
                ACT(sq[:, 0:T], R[c], AF.Square)
                MM(psM[:, 0:T], onesD, R[c], start=(c == 0), stop=(c == KC - 1))
                MM(psV[:, 0:T], onesDb, sq[:, 0:T], start=(c == 0), stop=(c == KC - 1))
            mean, m2, rstd = tmp["mean"][:, 0:T], tmp["m2"][:, 0:T], tmp["rstd"][:, 0:T]
            ACT(mean, psM[:, 0:T], AF.Copy)
            ACT(m2, psM[:, 0:T], AF.Square)
            TT("vector", m2, psV[:, 0:T], m2, ALU.subtract)
            ACT(m2, m2, AF.Sqrt, bias=eps, scale=1.0)
            P.op("vector", lambda e: e.reciprocal(out=rstd.ap, in_=m2.ap), [m2], [rstd])
            def apply_chunk(c):
                TT("vector", R[c], R[c], mean, ALU.subtract)
                TT("vector", R[c], R[c], rstd, ALU.mult)
                ACT(R[c], R[c], AF.Identity, bias=PR[:, gcol + 8 + c:gcol + 9 + c], scale=PR[:, gcol + c:gcol + c + 1])
            if defer:
                return apply_chunk
            for c in range(KC):
                apply_chunk(c)

        def ffn_phase(src, dsrc, dst, ddst, wi_ap, wo_ap, gcol, tlist):
            areset()
            WI = abf([128, KC, 2 * DFF])
            WO = abf([128, FC, D])
            XB = [abf([128, KC, TF]) for _ in range(2)]
            XR = [af32([128, KC, TF])] * 2
            H = abf([128, FC, TF])
            SG = [af32([128, TF])] * 2
            tmp = {"sq": [af32([128, TF])] * 2, "mean": af32([128, TF]),
                   "m2": af32([128, TF]), "rstd": af32([128, TF])}
            wload(WI, wi_ap[0].rearrange("(k p) n -> p k n", p=128), KC)
            wload(WO, wo_ap[0].rearrange("(k p) n -> p k n", p=128), FC)
            s3 = fm(src)
            d3 = fm(dst)
            pending = None
            for it, (sc, dc, T) in enumerate(tlist):
                xb, xr = XB[it % 2], XR[it % 2]
                DMA(xb[:, :, 0:T], TL(s3[:, :, sc:sc + T], dsrc.b), eng="gpsimd")
                if pending is None:
                    DMA(xr[:, :, 0:T], TL(s3[:, :, sc:sc + T], dsrc.b))
                for j in range(FC):
                    pg, pu = PS(), PS()
                    for k in range(KC):
                        MM(pg[:, 0:T], WI[:, k, j * 128:(j + 1) * 128], xb[:, k, 0:T], k == 0, k == KC - 1)
                    for k in range(KC):
                        MM(pu[:, 0:T], WI[:, k, DFF + j * 128:DFF + (j + 1) * 128], xb[:, k, 0:T], k == 0, k == KC - 1)
                    sg = SG[j % 2]
                    ACT(sg[:, 0:T], pg[:, 0:T], AF.Silu)
                    TT("vector", H[:, j, 0:T], sg[:, 0:T], pu[:, 0:T], ALU.mult)
                    if pending is not None and j < KC:
                        pending[0](j)
                        if j == KC - 1:
                            pending[1]()
                            pending = None
                            DMA(xr[:, :, 0:T], TL(s3[:, :, sc:sc + T], dsrc.b))
                R = []
                for c in range(KC):
                    po = PS()
                    for j in range(FC):
                        MM(po[:, 0:T], WO[:, j, c * 128:(c + 1) * 128], H[:, j, 0:T], j == 0, j == FC - 1)
                    rc = xr[:, c, 0:T]
                    STT("vector", rc, rc, 2.0 * ALPHA, po[:, 0:T], ALU.mult, ALU.add)
                    R.append(rc)
                apply_fn = layer_norm(R, T, gcol, tmp, eps=cst[:, 4:5], defer=True)

                def store_fn(xr=xr, dc=dc, T=T):
                    DMA(TL(d3[:, :, dc:dc + T], ddst.b), xr[:, :, 0:T])
                pending = (apply_fn, store_fn)
            if pending is not None:
                for c in range(KC):
                    pending[0](c)
                pending[1]()

        tl = [(c, c, T) for (c, T) in tiles_of(0, S0, TF)] + [(S0, S0, 128)] + \
             [(c, c, T) for (c, T) in tiles_of(S0 + 128, S0 + 128 + Ls, TF)] + [(S0 + 128 + Ls, S0 + 128 + Ls, 128)]
        ffn_phase(xT, TL(xT), X1, dX1, w_ffn1_in, w_ffn1_out, 0, tl)
        phase_end()

        segs = [(i * Lp, Lp, False) for i in range(nP)] + [(S0, Lse, True)]
        x13 = fm(X1)
        win3 = w_in[0].rearrange("(k p) n -> p k n", p=128)

        areset()
        Wq = abf([128, KC, 1024])
        Wkd = abf([128, KC, 512])
        Wv = abf([128, KC, 256])
        Wga = abf([128, KC, 1024])
        Woa = abf([128, KC, 1024])
        Lmax = max(Lp, Lse)
        KT = abf([128, 4, Lmax])
        Vt = abf([128, Lmax // 128, 256])
        XB = [abf([128, KC, 256]) for _ in range(2)]
        RC = [af32([128, 256]) for _ in range(2)]
        RS = [af32([128, 256]) for _ in range(2)]
        QT = abf([128, KC, 256])
        QTl = [TL(QT.ap[:, c, :]) for c in range(KC)]
        qf = [af32([128, 256]) for _ in range(3)]
        t1 = [af32([128, 256]) for _ in range(2)]
        t2 = [af32([128, 256]) for _ in range(2)]
        sm = [af32([128, 392]) for _ in range(4)]
        Eb = [abf([128, 392]) for _ in range(4)]
        ETb = [abf([128, 384]) for _ in range(4)]
        cols = [af32([128, 8]) for _ in range(4)]
        hq = [0]
        Otok = [abf([128, 1024]) for _ in range(2)]
        OT = abf([128, KC, 256])
        sga = [af32([128, 256]) for _ in range(2)]
        Gst = [af32([128, KC, 256]) for _ in range(2)]
        wload(Wq, win3[:, :, 0:1024], KC)
        for g in range(4):
            for hf in range(2):
                DMA(Wkd[:, :, g * 128 + hf * 64:g * 128 + hf * 64 + 64],
                    TL(win3[:, :, 1024 + g * 64:1024 + (g + 1) * 64]), eng="gpsimd")
        wload(Wv, win3[:, :, 1280:1536], KC)
        wload(Wga, win3[:, :, 6656:7680], KC)
        wload(Woa, w_o_attn[0].rearrange("(k p) n -> p k n", p=128), KC)

        def rope_a1(ps_in, T, i):
            q32 = qf[i % 3]
            ACT(q32[:, 0:T], ps_in, AF.Copy)
            return [q32, None, i]

        def rope_a2(st_, T):
            pp = PS()
            MM(pp[:, 0:T], permT, st_[0][:, 0:T])
            st_[1] = pp

        def rope_pipe(nchunks, T, rc, rs, proj, outs):
            sts = {}
            for t in range(nchunks + 2):
                if t < nchunks:
                    sts[t] = rope_a1(proj(t), T, t)
                if 0 <= t - 1 < nchunks:
                    rope_a2(sts[t - 1], T)
                if 0 <= t - 2 < nchunks:
                    rope_b(sts[t - 2], T, rc, rs, outs[t - 2])

        def rope_b(st_, T, rc, rs, out_bf):
            q32, pp, i = st_
            a, b_ = t1[i % 2], t2[i % 2]
            TT("gpsimd", a[:, 0:T], q32[:, 0:T], rc[:, 0:T], ALU.mult)
            TT("vector", b_[:, 0:T], pp[:, 0:T], rs[:, 0:T], ALU.mult)
            TT("vector", out_bf, a[:, 0:T], b_[:, 0:T], ALU.add)

        gi = [0]
        for (s0, L, halo) in segs:
            rope3 = ropeS if halo else ropeP
            nblk = L // 128
            seg_tiles = ([(0, 128)] + tiles_of(128, L - 128) + [(L - 128, 128)]) if halo else tiles_of(0, L)
            for it, (c0, T) in enumerate(seg_tiles):
                xb = XB[gi[0] % 2]
                rc, rs = RC[gi[0] % 2], RS[gi[0] % 2]
                gi[0] += 1
                DMA(xb[:, :, 0:T], TL(x13[:, :, s0 + c0:s0 + c0 + T], dX1.b), eng="gpsimd")
                DMA(rc[:, 0:T], TL(rope3[0, :, c0:c0 + T]))
                DMA(rs[:, 0:T], TL(rope3[1, :, c0:c0 + T]))
                def proj_k(g, xb=xb, T=T):
                    pk = PS()
                    for k in range(KC):
                        MM(pk[:, 0:T], Wkd[:, k, g * 128:(g + 1) * 128], xb[:, k, 0:T], k == 0, k == KC - 1)
                    return pk[:, 0:T]
                rope_pipe(4, T, rc, rs, proj_k, [KT[:, g, c0:c0 + T] for g in range(4)])
                for bi in range(T // 128):
                    blk = (c0 // 128) + bi
                    pv = PS()
                    for k in range(KC):
                        MM(pv[:, 0:256], xb[:, k, bi * 128:(bi + 1) * 128], Wv[:, k, :], k == 0, k == KC - 1)
                    if halo and blk == 0:
                        TS("vector", Vt[:, blk, :], pv[:, 0:256], flagL, None, ALU.mult)
                    elif halo and blk == nblk - 1:
                        TS("vector", Vt[:, blk, :], pv[:, 0:256], flagR, None, ALU.mult)
                    else:
                        ACT(Vt[:, blk, :], pv[:, 0:256], AF.Copy)
            body_tiles = tiles_of(128, L - 128) if halo else tiles_of(0, L)
            for it, (c0, T) in enumerate(body_tiles):
                xb = XB[gi[0] % 2]
                rc, rs = RC[gi[0] % 2], RS[gi[0] % 2]
                gst = Gst[gi[0] % 2]
                gi[0] += 1
                DMA(xb[:, :, 0:T], TL(x13[:, :, s0 + c0:s0 + c0 + T], dX1.b), eng="gpsimd")
                DMA(rc[:, 0:T], TL(rope3[0, :, c0:c0 + T]))
                DMA(rs[:, 0:T], TL(rope3[1, :, c0:c0 + T]))
                def proj_q(c, xb=xb, T=T):
                    pq = PS()
                    for k in range(KC):
                        MM(pq[:, 0:T], Wq[:, k, c * 128:(c + 1) * 128], xb[:, k, 0:T], k == 0, k == KC - 1)
                    return pq[:, 0:T]
                rope_pipe(KC, T, rc, rs, proj_q, [QTl[c][:, 0:T] for c in range(KC)])
                pool_set(banks[2:8])
                items = []
                for bi in range(T // 128):
                    n = c0 // 128 + bi
                    kb0, kb1 = max(n - 1, 0), min(n + 1, nblk - 1)
                    nk = kb1 - kb0 + 1
                    if halo and n == 1:
                        mask = maskFirst
                    elif halo and n == nblk - 2:
                        mask = maskLast
                    else:
                        mask = maskMid
                    moff = (kb0 - (n - 1)) * 128
                    for h in range(16):
                        items.append(dict(bi=bi, n=n, kb0=kb0, kb1=kb1, nk=nk, mk=mask[:, moff:moff + nk * 128], h=h, ot=Otok[n % 2]))

                def stage1(it_):
                    h, nk, bi = it_["h"], it_["nk"], it_["bi"]
                    g = h // 4
                    hp = (h % 2) * 64
                    i3 = hq[0] % 4
                    hq[0] += 1
                    it_["i3"] = i3
                    nkw = nk * 128
                    pS = PS()
                    MM(pS[:, 0:nkw], QTl[h // 2][hp:hp + 64, bi * 128:(bi + 1) * 128],
                       KT[hp:hp + 64, g, it_["kb0"] * 128:(it_["kb1"] + 1) * 128])
                    s_ = sm[i3]
                    e_ = Eb[i3]
                    cl = cols[i3]
                    CP("gpsimd", s_[:, nkw:nkw + 1], sinks[:, h:h + 1])
                    STT("vector", s_[:, 0:nkw], pS[:, 0:nkw], SCALE, it_["mk"], ALU.mult, ALU.add)
                    P.op("vector", lambda e, o=cl[:, 1:2], i_=s_[:, 0:nkw + 1]: e.reduce_max(out=o.ap, in_=i_.ap, axis=AX.X, negate=True), [s_], [cl])
                    ACT(e_[:, 0:nkw + 1], s_[:, 0:nkw + 1], AF.Exp, bias=cl[:, 1:2], scale=1.0, accum=cl[:, 3:4])

                def stage2(it_):
                    h, nk, i3 = it_["h"], it_["nk"], it_["i3"]
                    nkw = nk * 128
                    e_ = Eb[i3]
                    cl = cols[i3]
                    P.op("vector", lambda e, o=cl[:, 5:6], i_=cl[:, 3:4]: e.reciprocal(out=o.ap, in_=i_.ap), [cl], [cl])
                    pTb = PS().bc(BF16)
                    for kb in range(nk):
                        TR(pTb[:, kb * 128:(kb + 1) * 128], e_[:, kb * 128:(kb + 1) * 128])
                    ACT(ETb[i3][:, 0:nkw], pTb[:, 0:nkw], AF.Copy)

                def stage3(it_):
                    h, nk, bi, i3, ot = it_["h"], it_["nk"], it_["bi"], it_["i3"], it_["ot"]
                    g, j = h // 4, h % 4
                    et = ETb[i3]
                    cl = cols[i3]
                    pO = banks[g % 2]
                    for kb in range(nk):
                        MM(pO[:, j * 64:(j + 1) * 64], et[:, kb * 128:(kb + 1) * 128],
                           Vt[:, it_["kb0"] + kb, g * 64:(g + 1) * 64], kb == 0, kb == nk - 1)
                    ACT(ot[:, h * 64:(h + 1) * 64], pO[:, j * 64:(j + 1) * 64], AF.Copy, scale=cl[:, 5:6])
                    if h == 15:
                        for c in range(KC):
                            pT = PS().bc(BF16)
                            TR(pT[:, 0:128], ot[:, c * 128:(c + 1) * 128])
                            ACT(OT[:, c, bi * 128:(bi + 1) * 128], pT[:, 0:128], AF.Copy)

                nit = len(items)
                for t in range(nit + 2):
                    if t < nit:
                        stage1(items[t])
                    if 0 <= t - 1 < nit:
                        stage2(items[t - 1])
                    if 0 <= t - 2 < nit:
                        stage3(items[t - 2])
                pool_set(banks)
                for c in range(KC):
                    pa, pg = PS(), PS()
                    for f in range(KC):
                        MM(pa[:, 0:T], Woa[:, f, c * 128:(c + 1) * 128], OT[:, f, 0:T], f == 0, f == KC - 1)
                    for k in range(KC):
                        MM(pg[:, 0:T], Wga[:, k, c * 128:(c + 1) * 128], xb[:, k, 0:T], k == 0, k == KC - 1)
                    sg = sga[c % 2]
                    ACT(sg[:, 0:T], pg[:, 0:T], AF.Sigmoid)
                    TT("vector", gst[:, c, 0:T], sg[:, 0:T], pa[:, 0:T], ALU.mult)
                DMA(TL(fm(G)[:, :, s0 + c0:s0 + c0 + T], dG.b), gst[:, :, 0:T])
        phase_end()

        ob3, g3, x23 = fm(OB), fm(G), fm(X2)
        hg3 = HGs.rearrange("(k p) n -> p k n", p=128)

        def bcast(tl, shape):
            return TL(tl.ap.broadcast_to(list(shape)), tl.b)

        def hgrn_phase(bwd):
            direction = 1 if bwd else 0
            areset()
            Whq = abf([128, KC, 1024])
            Whf = abf([128, KC, 1024])
            Whi = abf([128, KC, 1024])
            Whg = None if bwd else abf([128, KC, 1024])
            XB = [abf([128, KC, 256]) for _ in range(2)]
            Vtk = [abf([128, 2, 1024]) for _ in range(2)]
            Qa, Fa, La, Ba = (af32([128, 8, 256]) for _ in range(4))
            QD = [abf([128, 8, 256]) for _ in range(2)]
            KI = [abf([128, 8, 256]) for _ in range(2)]
            KTt = [abf([128, 2, 8, 128]) for _ in range(2)]
            AS = [abf([128, 2, 8, 128]) for _ in range(2)]
            DEC = [af32([128, 8, 4]) for _ in range(2)]
            S32 = af32([128, 8, 128])
            Sbf = abf([128, 8, 128])
            tmpS = af32([128, 8, 128])
            if bwd:
                OBst = [af32([128, 8, 256]) for _ in range(2)]
            else:
                OBa, Oa, SQa, SILa = (af32([128, 8, 256]) for _ in range(4))
                HGo = [abf([128, 8, 256]) for _ in range(2)]
            wload(Whq, win3[:, :, 1536:2560], KC)
            wload(Whf, win3[:, :, 3584:4608] if bwd else win3[:, :, 2560:3584], KC)
            wload(Whi, win3[:, :, 4608:5632], KC)
            if not bwd:
                wload(Whg, win3[:, :, 5632:6656], KC)
            pool_set(banks[4:8])
            pob = banks[0:4]
            lo = 8 * direction
            work = []
            for (s0, L, halo) in segs:
                if halo:
                    tl_ = ([(0, 128)] if not bwd else []) + tiles_of(128, L - 128) + ([(L - 128, 128)] if bwd else [])
                else:
                    tl_ = tiles_of(0, L)
                if bwd:
                    tl_ = tl_[::-1]
                for ti, (c0, T) in enumerate(tl_):
                    work.append(dict(s0=s0, L=L, halo=halo, c0=c0, T=T, first=(ti == 0)))

            def prologue(w, it_g):
                s0, L, halo, c0, T = w["s0"], w["L"], w["halo"], w["c0"], w["T"]
                i2 = it_g % 2
                is_halo_tile = halo and (c0 == 0 or c0 == L - 128)
                xb = XB[i2]
                vt = Vtk[i2]
                nb_ = T // 128
                nch = T // 64
                DMA(xb[:, :, 0:T], TL(x13[:, :, s0 + c0:s0 + c0 + T], dX1.b), eng="gpsimd")
                for bi in range(nb_):
                    for hf in range(2):
                        pv = PS()
                        for k in range(KC):
                            MM(pv[:, 0:512], xb[:, k, bi * 128:(bi + 1) * 128], Whi[:, k, hf * 512:(hf + 1) * 512], k == 0, k == KC - 1)
                        if is_halo_tile:
                            TS("vector", vt[:, bi, hf * 512:(hf + 1) * 512], pv[:, 0:512], flagL if c0 == 0 else flagR, None, ALU.mult)
                        else:
                            ACT(vt[:, bi, hf * 512:(hf + 1) * 512], pv[:, 0:512], AF.Copy)
                for (W_, dst, fn, sc) in ((Whq, Qa, AF.Silu, 1.0), (Whf, Fa, AF.Sigmoid, -1.0)):
                    for m in range(4):
                        pq = PS()
                        for hh in range(2):
                            h = 2 * m + hh
                            for k in range(KC):
                                MM(pq[:, hh * T:(hh + 1) * T], W_[:, k, h * 128:(h + 1) * 128], xb[:, k, 0:T], k == 0, k == KC - 1)
                        ACT(dst[:, 2 * m:2 * m + 2, 0:T], pq[:, 0:2 * T].r("p (a t) -> p a t", a=2), fn, scale=sc)
                Qv, Fv, Lv, Bv = Qa[:, :, 0:T], Fa[:, :, 0:T], La[:, :, 0:T], Ba[:, :, 0:T]
                TT("vector", Fv, Fv, bcast(DV[:, 16 + lo:24 + lo].r("p (h o) -> p h o", o=1), [128, 8, T]), ALU.mult)
                ACT(Lv, Fv, AF.Ln, bias=onec, scale=-1.0)
                for h in range(8):
                    P.op("vector", lambda e, o=Ba[:, h, 0:T], d1=La[:, h, 0:T]: e.tensor_tensor_scan(
                        out=o.ap, data0=onesT.ap[:, 0:T], data1=d1.ap, initial=0.0, op0=ALU.mult, op1=ALU.add),
                        [La, onesT], [Ba])
                if not bwd:
                    B4f = Ba[:, :, 0:T].r("p h (c t) -> p h c t", t=64)
                    for c in range(nch - 1, 0, -1):
                        TT("vector", B4f[:, :, c, :], B4f[:, :, c, :], bcast(B4f[:, :, c - 1, 63:64], [128, 8, 64]), ALU.subtract)
                if bwd:
                    B4 = Bv.r("p h (c t) -> p h c t", t=64)
                    L4 = Lv.r("p h (c t) -> p h c t", t=64)
                    TT("vector", Lv, Lv, Bv, ALU.subtract)
                    TT("vector", L4, L4, bcast(B4[:, :, :, 63:64], [128, 8, nch, 64]), ALU.add)
                    ACT(Bv, Lv, AF.Exp)
                    ACT(Lv, Lv, AF.Exp, scale=-1.0)
                    E1, E2 = Bv, Lv
                else:
                    ACT(Lv, Bv, AF.Exp)
                    ACT(Bv, Bv, AF.Exp, scale=-1.0)
                    E1, E2 = Lv, Bv
                qd, ki, kT, as_, dec = QD[i2], KI[i2], KTt[i2], AS[i2], DEC[i2]
                TT("gpsimd", qd[:, :, 0:T], Qv, E1, ALU.mult)
                TT("vector", ki[:, :, 0:T], Fv, E2, ALU.mult)
                dcol = 0 if bwd else 63
                CP("vector", dec[:, :, 0:nch].r("p h (c o) -> p h c o", o=1),
                   E1.r("p h (c t) -> p h c t", t=64)[:, :, :, dcol:dcol + 1])
                for bi in range(nb_):
                    pT = PS().bc(BF16)
                    for h in range(8):
                        TR(pT[:, h * 128:(h + 1) * 128], ki[:, h, bi * 128:(bi + 1) * 128])
                    ACT(kT[:, bi, :, :], pT[:, 0:1024].r("p (h d) -> p h d", h=8), AF.Copy)
                if not is_halo_tile:
                    tri = triB if bwd else triF
                    tri_b = bcast(tri.r("p (o t) -> p o t", o=1), [128, 4, 128])
                    for bi in range(nb_):
                        for j in range(2):
                            pA = PS()
                            for hh in range(4):
                                h = 4 * j + hh
                                MM(pA[:, hh * 128:(hh + 1) * 128], ki[:, h, bi * 128:(bi + 1) * 128], qd[:, h, bi * 128:(bi + 1) * 128])
                            TT("vector", as_[:, bi, 4 * j:4 * j + 4, :], pA[:, 0:512].r("p (a t) -> p a t", a=4), tri_b, ALU.mult)
                w.update(i2=i2, is_halo_tile=is_halo_tile, nb_=nb_, nch=nch)

            def epilogue(w):
                s0, c0, T, i2, is_halo_tile, nb_ = w["s0"], w["c0"], w["T"], w["i2"], w["is_halo_tile"], w["nb_"]
                xb, vt, qd, kT, as_, dec = XB[i2], Vtk[i2], QD[i2], KTt[i2], AS[i2], DEC[i2]
                if w["first"]:
                    MS("gpsimd", S32, 0.0)
                    MS("gpsimd", Sbf, 0.0)
                if not bwd and not is_halo_tile:
                    DMA(OBa[:, :, 0:T], TL(ob3[:, :, s0 + c0:s0 + c0 + T], dOB.b))
                    for m in range(4):
                        pg = PS()
                        for hh in range(2):
                            h = 2 * m + hh
                            for k in range(KC):
                                MM(pg[:, hh * T:(hh + 1) * T], Whg[:, k, h * 128:(h + 1) * 128], xb[:, k, 0:T], k == 0, k == KC - 1)
                        ACT(SILa[:, 2 * m:2 * m + 2, 0:T], pg[:, 0:2 * T].r("p (a t) -> p a t", a=2), AF.Silu)
                blks = range(nb_ - 1, -1, -1) if bwd else range(nb_)
                for bi in blks:
                    if not is_halo_tile:
                        for h in range(8):
                            MM(pob[h // 2][:, (h % 2) * T + bi * 128:(h % 2) * T + (bi + 1) * 128],
                               vt[:, bi, h * 128:(h + 1) * 128], as_[:, bi, h, :], (bi == blks[0] and h % 2 == 0), False)
                    chs = (1, 0) if bwd else (0, 1)
                    for ci, cc in enumerate(chs):
                        c = bi * 2 + cc
                        pU = [PS(), PS()]
                        for h in range(8):
                            MM(pU[h // 4][:, (h % 4) * 128:(h % 4 + 1) * 128], kT[cc * 64:(cc + 1) * 64, bi, h, :],
                               vt[cc * 64:(cc + 1) * 64, bi, h * 128:(h + 1) * 128], h % 4 == 0, True)
                        if not is_halo_tile:
                            for h in range(8):
                                MM(pob[h // 2][:, (h % 2) * T + c * 64:(h % 2) * T + (c + 1) * 64],
                                   Sbf[:, h, :], qd[:, h, c * 64:(c + 1) * 64], False, ci == 1)
                        for j in range(2):
                            TT("vector", tmpS[:, 4 * j:4 * j + 4, :], pU[j][:, 0:512].r("p (a t) -> p a t", a=4),
                               S32[:, 4 * j:4 * j + 4, :], ALU.add)
                        dec_b = bcast(dec[:, :, c:c + 1], [128, 8, 128])
                        TT("vector", S32, tmpS, dec_b, ALU.mult)
                        ACT(Sbf, S32, AF.Copy)
                if is_halo_tile:
                    return
                if bwd:
                    ost = OBst[i2]
                    for m in range(4):
                        ACT(ost[:, 2 * m:2 * m + 2, 0:T], pob[m][:, 0:2 * T].r("p (a t) -> p a t", a=2), AF.Copy)
                    DMA(TL(ob3[:, :, s0 + c0:s0 + c0 + T], dOB.b), ost[:, :, 0:T])
                else:
                    Ov, SQv, SIv = Oa[:, :, 0:T], SQa[:, :, 0:T], SILa[:, :, 0:T]
                    for m in range(4):
                        TT("vector", Oa[:, 2 * m:2 * m + 2, 0:T], pob[m][:, 0:2 * T].r("p (a t) -> p a t", a=2),
                           OBa[:, 2 * m:2 * m + 2, 0:T], ALU.add)
                    TT("vector", SQv, Ov, Ov, ALU.mult)
                    for m in range(4):
                        pm = PS()
                        for hh in range(2):
                            MM(pm[:, hh * T:(hh + 1) * T], ones128, SQa[:, 2 * m + hh, 0:T])
                        ACT(SQa[:, 2 * m:2 * m + 2, 0:T], pm[:, 0:2 * T].r("p (a t) -> p a t", a=2), AF.Ln, bias=epsRMS, scale=1.0)
                    ACT(SQv, SQv, AF.Exp, scale=-0.5)
                    TT("vector", Ov, Ov, SQv, ALU.mult)
                    ho = HGo[i2]
                    STT("vector", ho[:, :, 0:T], Ov, PR[:, 48:49], SIv, ALU.mult, ALU.mult)
                    DMA(TL(hg3[:, :, s0 + c0:s0 + c0 + T], dHG.b), ho[:, :, 0:T])

            prologue(work[0], 0)
            for wi in range(len(work)):
                if wi + 1 < len(work):
                    prologue(work[wi + 1], wi + 1)
                epilogue(work[wi])
            pool_set(banks)

        hgrn_phase(True)
        phase_end()
        hgrn_phase(False)
        phase_end()

        areset()
        T3 = 512
        Woh = abf([128, KC, 1024])
        Wgh = abf([128, KC, 1024])
        Wo = abf([128, KC, 1024])
        HGt = [abf([128, KC, T3]) for _ in range(2)]
        XB = [abf([128, KC, T3]) for _ in range(2)]
        Gt2 = [af32([128, KC, T3]) for _ in range(2)]
        X1r2 = [af32([128, KC, T3]) for _ in range(2)]
        sgh = [af32([128, T3]) for _ in range(2)]
        mg = abf([128, KC, T3])
        tmp = {"sq": [af32([128, T3]) for _ in range(2)], "mean": af32([128, T3]),
               "m2": af32([128, T3]), "rstd": af32([128, T3])}
        wload(Wgh, win3[:, :, 7680:8704], KC)
        wload(Woh, w_o_hgrn[0].rearrange("(k p) n -> p k n", p=128), KC)
        wload(Wo, w_out[0].rearrange("(k p) n -> p k n", p=128), KC)
        work3 = []
        for (s0, L, halo) in segs:
            for (c0, T) in (tiles_of(128, L - 128, T3) if halo else tiles_of(0, L, T3)):
                work3.append((s0, c0, T))

        def b3_loads(wi):
            s0, c0, T = work3[wi]
            i2 = wi % 2
            cs_ = slice(s0 + c0, s0 + c0 + T)
            DMA(XB[i2][:, :, 0:T], TL(x13[:, :, cs_], dX1.b), eng="gpsimd")
            DMA(HGt[i2][:, :, 0:T], TL(hg3[:, :, cs_], dHG.b))
            DMA(Gt2[i2][:, :, 0:T], TL(g3[:, :, cs_], dG.b))
            DMA(X1r2[i2][:, :, 0:T], TL(x13[:, :, cs_], dX1.b))

        b3_loads(0)
        pending3 = None
        for wi in range(len(work3)):
            if True:
                s0, c0, T = work3[wi]
                i2 = wi % 2
                if wi + 1 < len(work3) and pending3 is None:
                    b3_loads(wi + 1)
                xb, hgt = XB[i2], HGt[i2]
                Gt, X1r = Gt2[i2], X1r2[i2]
                cs_ = slice(s0 + c0, s0 + c0 + T)
                for c in range(KC):
                    ph, pg = PS(), PS()
                    for f in range(KC):
                        MM(ph[:, 0:T], Woh[:, f, c * 128:(c + 1) * 128], hgt[:, f, 0:T], f == 0, f == KC - 1)
                    for k in range(KC):
                        MM(pg[:, 0:T], Wgh[:, k, c * 128:(c + 1) * 128], xb[:, k, 0:T], k == 0, k == KC - 1)
                    sg = sgh[c % 2][:, 0:T]
                    ACT(sg, pg[:, 0:T], AF.Sigmoid)
                    TT("vector", sg, sg, ph[:, 0:T], ALU.mult)
                    TT("vector", mg[:, c, 0:T], sg, Gt[:, c, 0:T], ALU.add)
                    if pending3 is not None:
                        pending3[0](c)
                        if c == KC - 1:
                            pending3[1]()
                            pending3 = None
                            if wi + 1 < len(work3):
                                b3_loads(wi + 1)
                R = []
                for c in range(KC):
                    px = PS()
                    for f in range(KC):
                        MM(px[:, 0:T], Wo[:, f, c * 128:(c + 1) * 128], mg[:, f, 0:T], f == 0, f == KC - 1)
                    rc = X1r[:, c, 0:T]
                    STT("vector", rc, rc, ALPHA, px[:, 0:T], ALU.mult, ALU.add)
                    R.append(rc)
                apply3 = layer_norm(R, T, 16, tmp, defer=True)

                def store3(X1r=X1r, cs_=cs_, T=T):
                    DMA(TL(x23[:, :, cs_], dX2.b), X1r[:, :, 0:T])
                pending3 = (apply3, store3)
        if pending3 is not None:
            for c in range(KC):
                pending3[0](c)
            pending3[1]()
        phase_end()

        tl = [(c, c, T) for (c, T) in tiles_of(0, S0, TF)] + [(c, c - 128, T) for (c, T) in tiles_of(S0 + 128, S0 + 128 + Ls, TF)]
        dY = TL(yT)
        ffn_phase(X2, dX2, yT, dY, w_ffn2_in, w_ffn2_out, 32, tl)
        phase_end()
        P.emit(st)
    return nc, P


def _const_tables(core, n_cores):
    ct = np.zeros((128, 1536), np.float32)
    d = np.arange(128)
    dd = d % 64
    perm = np.where(dd < 8, d + 8, np.where(dd < 16, d - 8, d))
    ct[perm, d] = 1.0
    j = np.arange(128)[:, None]
    i = np.arange(128)[None, :]
    same = (j // 64) == (i // 64)
    ct[:, 128:256] = (same & (j <= i)).astype(np.float32)
    ct[:, 256:384] = (same & (j >= i)).astype(np.float32)
    qi = np.arange(128)[:, None]
    kj = np.arange(384)[None, :] - 128
    band = np.abs(kj - qi) <= 128
    mid = np.where(band, 0.0, NEG).astype(np.float32)
    ct[:, 384:768] = mid
    first = mid.copy()
    last = mid.copy()
    if core == 0:
        first[:, 0:128] = NEG
    if core == n_cores - 1:
        last[:, 256:384] = NEG
    ct[:, 768:1152] = first
    ct[:, 1152:1536] = last
    return ct


def _rope_table(pos):
    half = 8
    inv = (500000.0 ** (-np.arange(half, dtype=np.float32) / half)).astype(np.float32)
    ang = pos.astype(np.float32)[:, None] * inv[None, :]
    cos = np.cos(ang).astype(np.float32)
    sin = np.sin(ang).astype(np.float32)
    L = pos.shape[0]
    tab = np.zeros((2, 128, L), np.float32)
    tab[0] = 1.0
    for base in (0, 64):
        tab[0, base:base + 8] = cos.T
        tab[0, base + 8:base + 16] = cos.T
        tab[1, base:base + 8] = -sin.T
        tab[1, base + 8:base + 16] = sin.T
    return tab


def _fm8(v):
    return np.ascontiguousarray(v.reshape(8, 128).T)


def make_in_maps(inp, n_cores, nP, Lp, Ls):
    xp = np.asarray(inp["x_prompt"], np.float32)
    xs = np.asarray(inp["x_sample"], np.float32)[0]
    Lse = Ls + 256
    par = np.zeros((128, 128), np.float32)
    for i, nm in enumerate(["ln1_g", "ln1_b", "ln2_g", "ln2_b", "ln3_g", "ln3_b"]):
        par[:, 8 * i:8 * i + 8] = _fm8(np.asarray(inp[nm], np.float32)[0])
    par[:, 48] = np.asarray(inp["hgrn_norm_g"], np.float32)[0]
    lb = np.asarray(inp["hgrn_lb"], np.float32)
    par[:, 56:64] = _fm8(lb[0, 0])
    par[:, 64:72] = _fm8(lb[0, 1])
    par[:, 72:80] = _fm8(lb[1, 0])
    par[:, 80:88] = _fm8(lb[1, 1])
    par[:, 88:104] = np.asarray(inp["attn_sink"], np.float32)[0][None, :]
    ropeP = _rope_table(np.arange(Lp))
    wnames = ["ffn1_w_in", "ffn1_w_out", "w_in", "w_o_attn", "w_o_hgrn", "w_out", "ffn2_w_in", "ffn2_w_out"]
    wts = {k: np.ascontiguousarray(np.asarray(inp[k], np.float32)) for k in wnames}
    maps = []
    for c in range(n_cores):
        cols = [xp[c * nP + i].T for i in range(nP)]
        ext = np.zeros((Lse, D), np.float32)
        a, b = c * Ls - 128, (c + 1) * Ls + 128
        a2, b2 = max(a, 0), min(b, xs.shape[0])
        ext[a2 - a:b2 - a] = xs[a2:b2]
        cols.append(ext.T)
        p = par.copy()
        p[:, 104] = 1.0 if c > 0 else 0.0
        p[:, 105] = 1.0 if c < n_cores - 1 else 0.0
        m = {"xT": np.ascontiguousarray(np.concatenate(cols, axis=1)),
             "ctab": _const_tables(c, n_cores), "par": p, "ropeP": ropeP,
             "ropeS": _rope_table(np.arange(a, b))}
        m.update(wts)
        maps.append(m)
    return maps


def gather(results, n_cores, nP, Lp, Ls):
    yp = np.zeros((n_cores * nP, Lp, D), np.float32)
    ys = np.zeros((1, n_cores * Ls, D), np.float32)
    for c in range(n_cores):
        y = results[c]["yT"]
        for i in range(nP):
            yp[c * nP + i] = y[:, i * Lp:(i + 1) * Lp].T
        ys[0, c * Ls:(c + 1) * Ls] = y[:, nP * Lp:].T
    return yp, ys


_CACHE = {}


def kernel(**inputs):
    nP, Lp, Ls = 2, 4096, 2048
    if "nc" not in _CACHE:
        _CACHE["nc"] = build(nP, Lp, Ls)[0]
    nc = _CACHE["nc"]
    maps = make_in_maps(inputs, N_CORES, nP, Lp, Ls)
    res = run_bass_kernel_spmd(nc, maps, core_ids=list(range(N_CORES)))
    return gather(res.results, N_CORES, nP, Lp, Ls)
```

```python
import numpy as np
from contextlib import ExitStack
import concourse.bass as bass
import concourse.mybir as mybir
from concourse.bass_utils import run_bass_kernel_spmd

F32 = mybir.dt.float32
BF16 = mybir.dt.bfloat16
AF = mybir.ActivationFunctionType
ALU = mybir.AluOpType
AX = mybir.AxisListType
ENGS = ("tensor", "vector", "scalar", "gpsimd", "sync")

D = 1024
KC = 8
DFF = 2816
FC = 22
ALPHA = float(2.0 ** 0.25)
SCALE = 0.125
NEG = -1.0e30
N_CORES = 8
TF = 512
ARENA_WORDS = 50432
EPOCH = 30000
SAME_ENGINE_SYNC = True
NO_SELF_SYNC = ()
DEPOCH = 1800


class Buf:
    __slots__ = ("last_w", "readers")

    def __init__(self):
        self.last_w = None
        self.readers = {}


class TL:
    __slots__ = ("ap", "b")

    def __init__(self, ap, b=None):
        if type(ap).__name__ != "AP":
            ap = ap[:]
        self.ap = ap
        self.b = b if b is not None else Buf()

    def __getitem__(self, k):
        return TL(self.ap[k], self.b)

    def bc(self, dt):
        return TL(self.ap.bitcast(dt), self.b)

    def r(self, pat, **kw):
        return TL(self.ap.rearrange(pat, **kw), self.b)


class Prog:
    def __init__(self, nc):
        self.nc = nc
        self.streams = {e: [] for e in ENGS}
        self.count = {e: 0 for e in ENGS}
        self.chan_count = {}
        self.waited = {e: {} for e in ENGS}
        self.live = {}
        self.nops = 0

    def op(self, eng, fn, reads=(), writes=(), chan=None):
        deps = {}

        def add(tok):
            if tok is not None and deps.get(tok[0], 0) < tok[1]:
                deps[tok[0]] = tok[1]
        for t in reads:
            add(t.b.last_w)
        for t in writes:
            add(t.b.last_w)
            for kv in t.b.readers.items():
                add(kv)
        waits = []
        w = self.waited[eng]
        for k, v in deps.items():
            if chan is None and k.startswith(eng + "#") and (eng == "tensor" or eng in NO_SELF_SYNC):
                continue
            if w.get(k, 0) >= v:
                continue
            w[k] = v
            waits.append((k, v))
        if chan is None:
            n = self.count[eng]
            self.count[eng] = n + 1
            key = "%s#%d" % (eng, n // EPOCH)
            tok = (key, n % EPOCH + 1)
            inc = (key, 1)
        else:
            n = self.chan_count.get(chan, 0)
            self.chan_count[chan] = n + 1
            key = "dma:%s#%d" % (chan, n // DEPOCH)
            tok = (key, (n % DEPOCH + 1) * 16)
            inc = (key, 16)
        self.live[tok[0]] = tok[1]
        self.streams[eng].append((waits, fn, inc))
        for t in reads:
            if t.b.readers.get(tok[0], 0) < tok[1]:
                t.b.readers[tok[0]] = tok[1]
        for t in writes:
            t.b.last_w = tok
            t.b.readers = {}
        self.nops += 1
        return tok

    def barrier(self):
        allk = list(self.live.items())
        for e in ENGS:
            waits = []
            for k, v in allk:
                if k.startswith(e + "#"):
                    continue
                if self.waited[e].get(k, 0) >= v:
                    continue
                self.waited[e][k] = v
                waits.append((k, v))
            if waits:
                self.streams[e].append((waits, None, None))

    def emit(self, stack):
        nc = self.nc
        keys = set()
        for e in ENGS:
            for waits, fn, inc in self.streams[e]:
                if inc is not None:
                    keys.add(inc[0])
        sems = {k: stack.enter_context(nc.semaphore("s_" + k.replace(":", "_").replace("#", "_"))) for k in sorted(keys)}
        self.nsems = len(sems)
        block = stack.enter_context(nc.Block())

        def make(ename):
            ops = self.streams[ename]

            def body(e):
                for waits, fn, inc in ops:
                    for k, v in waits:
                        e.wait_ge(sems[k], v)
                    if fn is not None:
                        fn(e).then_inc(sems[inc[0]], inc[1])
            return body
        for ename in ENGS:
            if self.streams[ename]:
                getattr(block, ename)(make(ename))


def build(nP, Lp, Ls, debug=False):
    Lse = Ls + 256
    NT = nP * Lp + Lse
    NB = nP * Lp + Ls
    S0 = nP * Lp
    nc = bass.Bass("TRN2", target_bir_lowering=False)
    P = Prog(nc)

    def din(name, shape):
        return nc.dram_tensor(name, list(shape), F32, kind="ExternalInput").ap()

    xT = din("xT", [D, NT])
    ctab = din("ctab", [128, 1536])
    par = din("par", [128, 128])
    ropeP = din("ropeP", [2, 128, Lp])
    ropeS = din("ropeS", [2, 128, Lse])
    w_ffn1_in = din("ffn1_w_in", [1, D, 2 * DFF])
    w_ffn1_out = din("ffn1_w_out", [1, DFF, D])
    w_in = din("w_in", [1, D, 8704])
    w_o_attn = din("w_o_attn", [1, D, D])
    w_o_hgrn = din("w_o_hgrn", [1, D, D])
    w_out = din("w_out", [1, D, D])
    w_ffn2_in = din("ffn2_w_in", [1, D, 2 * DFF])
    w_ffn2_out = din("ffn2_w_out", [1, DFF, D])
    yT = nc.dram_tensor("yT", [D, NB], F32, kind="ExternalOutput").ap()
    skind = "ExternalOutput" if debug else "Internal"
    X1 = nc.dram_tensor("X1s", [D, NT], F32, kind=skind).ap()
    G = nc.dram_tensor("Gs", [D, NT], F32, kind=skind).ap()
    OB = nc.dram_tensor("OBs", [D, NT], F32, kind=skind).ap()
    X2 = nc.dram_tensor("X2s", [D, NT], F32, kind=skind).ap()
    HGs = nc.dram_tensor("HGs", [D, NT], BF16, kind="Internal").ap()
    dX1, dG, dOB, dX2, dHG = TL(X1), TL(G), TL(OB), TL(X2), TL(HGs)

    def fm(ap2d):
        return ap2d.rearrange("(k p) n -> p k n", p=128)

    st = ExitStack()
    with st:
        def sbt(name, shape, dt):
            return st.enter_context(nc.sbuf_tensor(name, list(shape), dt))

        arena = sbt("arena", [128, ARENA_WORDS], F32)
        identb = TL(sbt("identb", [128, 128], BF16))
        identf = TL(sbt("identf", [128, 128], F32))
        onesD = TL(sbt("onesD", [128, 128], F32))
        onesDb = TL(sbt("onesDb", [128, 128], BF16))
        ones128 = TL(sbt("ones128", [128, 128], F32))
        ones1 = TL(sbt("ones1", [128, 64], F32))
        onesT = TL(sbt("onesT", [128, 256], F32))
        cst = TL(sbt("cst", [128, 8], F32))
        CT = TL(sbt("ctabs", [128, 1536], F32))
        PR = TL(sbt("pars", [128, 128], F32))
        DV = TL(sbt("derived", [128, 64], F32))
        banks = [TL(st.enter_context(nc.psum_tensor("pb%d" % i, [128, 512], F32))) for i in range(8)]
        pctr = [0]
        pset = [banks]

        def pool_set(lst):
            pset[0] = list(lst)

        def PS():
            t = pset[0][pctr[0] % len(pset[0])]
            pctr[0] += 1
            return t

        def MM(out, lhsT, rhs, start=True, stop=True):
            P.op("tensor", lambda e: e.matmul(out.ap, lhsT=lhsT.ap, rhs=rhs.ap, start=start, stop=stop),
                 [lhsT, rhs], [out])

        def TR(out, in_):
            P.op("tensor", lambda e: e.transpose(out.ap, in_.ap, identb.ap), [in_, identb], [out])

        def ACT(out, in_, func, bias=None, scale=None, accum=None):
            rd = [in_]
            kw = {}
            wr = [out]
            if accum is not None:
                kw["accum_out"] = accum.ap
                wr.append(accum)
            if bias is not None:
                kw["bias"] = bias.ap
                rd.append(bias)
            if scale is not None:
                if isinstance(scale, TL):
                    kw["scale"] = scale.ap
                    rd.append(scale)
                else:
                    kw["scale"] = float(scale)
            P.op("scalar", lambda e: e.activation(out=out.ap, in_=in_.ap, func=func, **kw), rd, wr)

        def TT(eng, out, in0, in1, op):
            P.op(eng, lambda e: e.tensor_tensor(out=out.ap, in0=in0.ap, in1=in1.ap, op=op), [in0, in1], [out])

        def TS(eng, out, in0, s1, s2, op0, op1=None):
            rd = [in0]
            a1 = s1.ap if isinstance(s1, TL) else float(s1)
            if isinstance(s1, TL):
                rd.append(s1)
            if s2 is None:
                P.op(eng, lambda e: e.tensor_scalar(out=out.ap, in0=in0.ap, scalar1=a1, scalar2=None, op0=op0), rd, [out])
                return
            a2 = s2.ap if isinstance(s2, TL) else float(s2)
            if isinstance(s2, TL):
                rd.append(s2)
            P.op(eng, lambda e: e.tensor_scalar(out=out.ap, in0=in0.ap, scalar1=a1, scalar2=a2, op0=op0, op1=op1), rd, [out])

        def STT(eng, out, in0, s, in1, op0, op1):
            rd = [in0, in1]
            a = s.ap if isinstance(s, TL) else float(s)
            if isinstance(s, TL):
                rd.append(s)
            P.op(eng, lambda e: e.scalar_tensor_tensor(out=out.ap, in0=in0.ap, scalar=a, in1=in1.ap, op0=op0, op1=op1), rd, [out])

        def CP(eng, out, in_):
            P.op(eng, lambda e: e.tensor_copy(out=out.ap, in_=in_.ap), [in_], [out])

        def MS(eng, out, val):
            P.op(eng, lambda e: e.memset(out.ap, val), [], [out])

        chmap = {}

        def DMA(out, in_, eng="sync"):
            ch = chmap.setdefault((eng, id(out.b)), "%s%d" % (eng[0], len(chmap)))
            P.op(eng, lambda e: e.dma_start(out=out.ap, in_=in_.ap), [in_], [out], chan=ch)

        def phase_end():
            P.barrier()
            chmap.clear()

        apos = [0]

        def areset():
            apos[0] = 0

        def af32(shape):
            n = int(np.prod(shape[1:]))
            v = arena[:, apos[0]:apos[0] + n]
            apos[0] += n
            assert apos[0] <= ARENA_WORDS, apos[0]
            if len(shape) == 3:
                v = v.rearrange("p (a b) -> p a b", a=shape[1])
            return TL(v)

        def abf(shape):
            n = int(np.prod(shape[1:]))
            assert n % 2 == 0
            v = arena[:, apos[0]:apos[0] + n // 2].bitcast(BF16)
            apos[0] += n // 2
            assert apos[0] <= ARENA_WORDS, apos[0]
            if len(shape) == 3:
                v = v.rearrange("p (a b) -> p a b", a=shape[1])
            elif len(shape) == 4:
                v = v.rearrange("p (a b c) -> p a b c", a=shape[1], b=shape[2])
            return TL(v)

        def wload(dst, src3, nk):
            for k in range(nk):
                DMA(dst[:, k, :], TL(src3[:, k, :]), eng="gpsimd")

        def tiles_of(c0, c1, T=256):
            out = []
            c = c0
            while c < c1:
                out.append((c, min(T, c1 - c)))
                c += T
            return out

        DMA(CT, TL(ctab))
        DMA(PR, TL(par))
        MS("gpsimd", identf, 0.0)
        P.op("gpsimd", lambda e: e.affine_select(out=identf.ap, in_=identf.ap, pattern=[[-1, 128]],
                                                 compare_op=ALU.not_equal, fill=1.0, base=0, channel_multiplier=1),
             [identf], [identf])
        CP("vector", identb, identf)
        MS("vector", onesD, 1.0 / D)
        MS("vector", onesDb, 1.0 / D)
        MS("vector", ones128, 1.0 / 128)
        MS("vector", ones1, 1.0)
        MS("vector", onesT, 1.0)
        MS("vector", cst[:, 0:1], 1e-5)
        MS("vector", cst[:, 1:2], 1e-6)
        MS("vector", cst[:, 2:3], 1.0)
        MS("vector", cst[:, 3:4], 0.0)
        MS("vector", cst[:, 4:5], 4e-5)
        epsLN, epsRMS, onec = cst[:, 0:1], cst[:, 1:2], cst[:, 2:3]
        TT("vector", DV[:, 0:8], PR[:, 56:64], PR[:, 64:72], ALU.subtract)
        TT("vector", DV[:, 8:16], PR[:, 72:80], PR[:, 80:88], ALU.subtract)
        ACT(DV[:, 0:16], DV[:, 0:16], AF.Sigmoid)
        TS("vector", DV[:, 16:32], DV[:, 0:16], -1.0, 1.0, ALU.mult, ALU.add)
        sinks = PR[:, 88:104]
        flagL, flagR = PR[:, 104:105], PR[:, 105:106]
        permT = CT[:, 0:128]
        triF, triB = CT[:, 128:256], CT[:, 256:384]
        maskMid, maskFirst, maskLast = CT[:, 384:768], CT[:, 768:1152], CT[:, 1152:1536]

        def layer_norm(R, T, gcol, tmp, eps=None, defer=False):
            eps = epsLN if eps is None else eps
            psM, psV = PS(), PS()
            for c in range(KC):
                sq = tmp["sq"][c % 2].bc(BF16)
                ACT(sq[:, 0:T], R[c], AF.Square)
                MM(psM[:, 0:T], onesD, R[c], start=(c == 0), stop=(c == KC - 1))
                MM(psV[:, 0:T], onesDb, sq[:, 0:T], start=(c == 0), stop=(c == KC - 1))
            mean, m2, rstd = tmp["mean"][:, 0:T], tmp["m2"][:, 0:T], tmp["rstd"][:, 0:T]
            ACT(mean, psM[:, 0:T], AF.Copy)
            ACT(m2, psM[:, 0:T], AF.Square)
            TT("vector", m2, psV[:, 0:T], m2, ALU.subtract)
            ACT(m2, m2, AF.Sqrt, bias=eps, scale=1.0)
            P.op("vector", lambda e: e.reciprocal(out=rstd.ap, in_=m2.ap), [m2], [rstd])
            def apply_chunk(c):
                TT("vector", R[c], R[c], mean, ALU.subtract)
                TT("vector", R[c], R[c], rstd, ALU.mult)
                ACT(R[c], R[c], AF.Identity, bias=PR[:, gcol + 8 + c:gcol + 9 + c], scale=PR[:, gcol + c:gcol + c + 1])
            if defer:
                return apply_chunk
            for c in range(KC):
                apply_chunk(c)

        def ffn_phase(src, dsrc, dst, ddst, wi_ap, wo_ap, gcol, tlist):
            areset()
            WI = abf([128, KC, 2 * DFF])
            WO = abf([128, FC, D])
            XB = [abf([128, KC, TF]) for _ in range(2)]
            XR = [af32([128, KC, TF])] * 2
            H = abf([128, FC, TF])
            SG = [af32([128, TF])] * 2
            tmp = {"sq": [af32([128, TF])] * 2, "mean": af32([128, TF]),
                   "m2": af32([128, TF]), "rstd": af32([128, TF])}
            wload(WI, wi_ap[0].rearrange("(k p) n -> p k n", p=128), KC)
            wload(WO, wo_ap[0].rearrange("(k p) n -> p k n", p=128), FC)
            s3 = fm(src)
            d3 = fm(dst)
            pending = None
            for it, (sc, dc, T) in enumerate(tlist):
                xb, xr = XB[it % 2], XR[it % 2]
                DMA(xb[:, :, 0:T], TL(s3[:, :, sc:sc + T], dsrc.b), eng="gpsimd")
                if pending is None:
                    DMA(xr[:, :, 0:T], TL(s3[:, :, sc:sc + T], dsrc.b))
                for j in range(FC):
                    pg, pu = PS(), PS()
                    for k in range(KC):
                        MM(pg[:, 0:T], WI[:, k, j * 128:(j + 1) * 128], xb[:, k, 0:T], k == 0, k == KC - 1)
                    for k in range(KC):
                        MM(pu[:, 0:T], WI[:, k, DFF + j * 128:DFF + (j + 1) * 128], xb[:, k, 0:T], k == 0, k == KC - 1)
                    sg = SG[j % 2]
                    ACT(sg[:, 0:T], pg[:, 0:T], AF.Silu)
                    TT("vector", H[:, j, 0:T], sg[:, 0:T], pu[:, 0:T], ALU.mult)
                    if pending is not None and j < KC:
                        pending[0](j)
                        if j == KC - 1:
                            pending[1]()
                            pending = None
                            DMA(xr[:, :, 0:T], TL(s3[:, :, sc:sc + T], dsrc.b))
                R = []
                for c in range(KC):
                    po = PS()
                    for j in range(FC):
                        MM(po[:, 0:T], WO[:, j, c * 128:(c + 1) * 128], H[:, j, 0:T], j == 0, j == FC - 1)
                    rc = xr[:, c, 0:T]
                    STT("vector", rc, rc, 2.0 * ALPHA, po[:, 0:T], ALU.mult, ALU.add)
                    R.append(rc)
                apply_fn = layer_norm(R, T, gcol, tmp, eps=cst[:, 4:5], defer=True)

                def store_fn(xr=xr, dc=dc, T=T):
                    DMA(TL(d3[:, :, dc:dc + T], ddst.b), xr[:, :, 0:T])
                pending = (apply_fn, store_fn)
            if pending is not None:
                for c in range(KC):
                    pending[0](c)
                pending[1]()

        tl = [(c, c, T) for (c, T) in tiles_of(0, S0, TF)] + [(S0, S0, 128)] + \
             [(c, c, T) for (c, T) in tiles_of(S0 + 128, S0 + 128 + Ls, TF)] + [(S0 + 128 + Ls, S0 + 128 + Ls, 128)]
        ffn_phase(xT, TL(xT), X1, dX1, w_ffn1_in, w_ffn1_out, 0, tl)
        phase_end()

        segs = [(i * Lp, Lp, False) for i in range(nP)] + [(S0, Lse, True)]
        x13 = fm(X1)
        win3 = w_in[0].rearrange("(k p) n -> p k n", p=128)

        areset()
        Wq = abf([128, KC, 1024])
        Wkd = abf([128, KC, 512])
        Wv = abf([128, KC, 256])
        Wga = abf([128, KC, 1024])
        Woa = abf([128, KC, 1024])
        Lmax = max(Lp, Lse)
        KT = abf([128, 4, Lmax])
        Vt = abf([128, Lmax // 128, 256])
        XB = [abf([128, KC, 256]) for _ in range(2)]
        RC = [af32([128, 256]) for _ in range(2)]
        RS = [af32([128, 256]) for _ in range(2)]
        QT = abf([128, KC, 256])
        QTl = [TL(QT.ap[:, c, :]) for c in range(KC)]
        qf = [af32([128, 256]) for _ in range(3)]
        t1 = [af32([128, 256]) for _ in range(2)]
        t2 = [af32([128, 256]) for _ in range(2)]
        sm = [af32([128, 392]) for _ in range(4)]
        Eb = [abf([128, 392]) for _ in range(4)]
        ETb = [abf([128, 384]) for _ in range(4)]
        cols = [af32([128, 8]) for _ in range(4)]
        hq = [0]
        Otok = [abf([128, 1024]) for _ in range(2)]
        OT = abf([128, KC, 256])
        sga = [af32([128, 256]) for _ in range(2)]
        Gst = [af32([128, KC, 256]) for _ in range(2)]
        wload(Wq, win3[:, :, 0:1024], KC)
        for g in range(4):
            for hf in range(2):
                DMA(Wkd[:, :, g * 128 + hf * 64:g * 128 + hf * 64 + 64],
                    TL(win3[:, :, 1024 + g * 64:1024 + (g + 1) * 64]), eng="gpsimd")
        wload(Wv, win3[:, :, 1280:1536], KC)
        wload(Wga, win3[:, :, 6656:7680], KC)
        wload(Woa, w_o_attn[0].rearrange("(k p) n -> p k n", p=128), KC)

        def rope_a1(ps_in, T, i):
            q32 = qf[i % 3]
            ACT(q32[:, 0:T], ps_in, AF.Copy)
            return [q32, None, i]

        def rope_a2(st_, T):
            pp = PS()
            MM(pp[:, 0:T], permT, st_[0][:, 0:T])
            st_[1] = pp

        def rope_pipe(nchunks, T, rc, rs, proj, outs):
            sts = {}
            for t in range(nchunks + 2):
                if t < nchunks:
                    sts[t] = rope_a1(proj(t), T, t)
                if 0 <= t - 1 < nchunks:
                    rope_a2(sts[t - 1], T)
                if 0 <= t - 2 < nchunks:
                    rope_b(sts[t - 2], T, rc, rs, outs[t - 2])

        def rope_b(st_, T, rc, rs, out_bf):
            q32, pp, i = st_
            a, b_ = t1[i % 2], t2[i % 2]
            TT("gpsimd", a[:, 0:T], q32[:, 0:T], rc[:, 0:T], ALU.mult)
            TT("vector", b_[:, 0:T], pp[:, 0:T], rs[:, 0:T], ALU.mult)
            TT("vector", out_bf, a[:, 0:T], b_[:, 0:T], ALU.add)

        gi = [0]
        for (s0, L, halo) in segs:
            rope3 = ropeS if halo else ropeP
            nblk = L // 128
            seg_tiles = ([(0, 128)] + tiles_of(128, L - 128) + [(L - 128, 128)]) if halo else tiles_of(0, L)
            for it, (c0, T) in enumerate(seg_tiles):
                xb = XB[gi[0] % 2]
                rc, rs = RC[gi[0] % 2], RS[gi[0] % 2]
                gi[0] += 1
                DMA(xb[:, :, 0:T], TL(x13[:, :, s0 + c0:s0 + c0 + T], dX1.b), eng="gpsimd")
                DMA(rc[:, 0:T], TL(rope3[0, :, c0:c0 + T]))
                DMA(rs[:, 0:T], TL(rope3[1, :, c0:c0 + T]))
                def proj_k(g, xb=xb, T=T):
                    pk = PS()
                    for k in range(KC):
                        MM(pk[:, 0:T], Wkd[:, k, g * 128:(g + 1) * 128], xb[:, k, 0:T], k == 0, k == KC - 1)
                    return pk[:, 0:T]
                rope_pipe(4, T, rc, rs, proj_k, [KT[:, g, c0:c0 + T] for g in range(4)])
                for bi in range(T // 128):
                    blk = (c0 // 128) + bi
                    pv = PS()
                    for k in range(KC):
                        MM(pv[:, 0:256], xb[:, k, bi * 128:(bi + 1) * 128], Wv[:, k, :], k == 0, k == KC - 1)
                    if halo and blk == 0:
                        TS("vector", Vt[:, blk, :], pv[:, 0:256], flagL, None, ALU.mult)
                    elif halo and blk == nblk - 1:
                        TS("vector", Vt[:, blk, :], pv[:, 0:256], flagR, None, ALU.mult)
                    else:
                        ACT(Vt[:, blk, :], pv[:, 0:256], AF.Copy)
            body_tiles = tiles_of(128, L - 128) if halo else tiles_of(0, L)
            for it, (c0, T) in enumerate(body_tiles):
                xb = XB[gi[0] % 2]
                rc, rs = RC[gi[0] % 2], RS[gi[0] % 2]
                gst = Gst[gi[0] % 2]
                gi[0] += 1
                DMA(xb[:, :, 0:T], TL(x13[:, :, s0 + c0:s0 + c0 + T], dX1.b), eng="gpsimd")
                DMA(rc[:, 0:T], TL(rope3[0, :, c0:c0 + T]))
                DMA(rs[:, 0:T], TL(rope3[1, :, c0:c0 + T]))
                def proj_q(c, xb=xb, T=T):
                    pq = PS()
                    for k in range(KC):
                        MM(pq[:, 0:T], Wq[:, k, c * 128:(c + 1) * 128], xb[:, k, 0:T], k == 0, k == KC - 1)
                    return pq[:, 0:T]
                rope_pipe(KC, T, rc, rs, proj_q, [QTl[c][:, 0:T] for c in range(KC)])
                pool_set(banks[2:8])
                items = []
                for bi in range(T // 128):
                    n = c0 // 128 + bi
                    kb0, kb1 = max(n - 1, 0), min(n + 1, nblk - 1)
                    nk = kb1 - kb0 + 1
                    if halo and n == 1:
                        mask = maskFirst
                    elif halo and n == nblk - 2:
                        mask = maskLast
                    else:
                        mask = maskMid
                    moff = (kb0 - (n - 1)) * 128
                    for h in range(16):
                        items.append(dict(bi=bi, n=n, kb0=kb0, kb1=kb1, nk=nk, mk=mask[:, moff:moff + nk * 128], h=h, ot=Otok[n % 2]))

                def stage1(it_):
                    h, nk, bi = it_["h"], it_["nk"], it_["bi"]
                    g = h // 4
                    hp = (h % 2) * 64
                    i3 = hq[0] % 4
                    hq[0] += 1
                    it_["i3"] = i3
                    nkw = nk * 128
                    pS = PS()
                    MM(pS[:, 0:nkw], QTl[h // 2][hp:hp + 64, bi * 128:(bi + 1) * 128],
                       KT[hp:hp + 64, g, it_["kb0"] * 128:(it_["kb1"] + 1) * 128])
                    s_ = sm[i3]
                    e_ = Eb[i3]
                    cl = cols[i3]
                    CP("gpsimd", s_[:, nkw:nkw + 1], sinks[:, h:h + 1])
                    STT("vector", s_[:, 0:nkw], pS[:, 0:nkw], SCALE, it_["mk"], ALU.mult, ALU.add)
                    P.op("vector", lambda e, o=cl[:, 1:2], i_=s_[:, 0:nkw + 1]: e.reduce_max(out=o.ap, in_=i_.ap, axis=AX.X, negate=True), [s_], [cl])
                    ACT(e_[:, 0:nkw + 1], s_[:, 0:nkw + 1], AF.Exp, bias=cl[:, 1:2], scale=1.0, accum=cl[:, 3:4])

                def stage2(it_):
                    h, nk, i3 = it_["h"], it_["nk"], it_["i3"]
                    nkw = nk * 128
                    e_ = Eb[i3]
                    cl = cols[i3]
                    P.op("vector", lambda e, o=cl[:, 5:6], i_=cl[:, 3:4]: e.reciprocal(out=o.ap, in_=i_.ap), [cl], [cl])
                    pTb = PS().bc(BF16)
                    for kb in range(nk):
                        TR(pTb[:, kb * 128:(kb + 1) * 128], e_[:, kb * 128:(kb + 1) * 128])
                    ACT(ETb[i3][:, 0:nkw], pTb[:, 0:nkw], AF.Copy)

                def stage3(it_):
                    h, nk, bi, i3, ot = it_["h"], it_["nk"], it_["bi"], it_["i3"], it_["ot"]
                    g, j = h // 4, h % 4
                    et = ETb[i3]
                    cl = cols[i3]
                    pO = banks[g % 2]
                    for kb in range(nk):
                        MM(pO[:, j * 64:(j + 1) * 64], et[:, kb * 128:(kb + 1) * 128],
                           Vt[:, it_["kb0"] + kb, g * 64:(g + 1) * 64], kb == 0, kb == nk - 1)
                    ACT(ot[:, h * 64:(h + 1) * 64], pO[:, j * 64:(j + 1) * 64], AF.Copy, scale=cl[:, 5:6])
                    if h == 15:
                        for c in range(KC):
                            pT = PS().bc(BF16)
                            TR(pT[:, 0:128], ot[:, c * 128:(c + 1) * 128])
                            ACT(OT[:, c, bi * 128:(bi + 1) * 128], pT[:, 0:128], AF.Copy)

                nit = len(items)
                for t in range(nit + 2):
                    if t < nit:
                        stage1(items[t])
                    if 0 <= t - 1 < nit:
                        stage2(items[t - 1])
                    if 0 <= t - 2 < nit:
                        stage3(items[t - 2])
                pool_set(banks)
                for c in range(KC):
                    pa, pg = PS(), PS()
                    for f in range(KC):
                        MM(pa[:, 0:T], Woa[:, f, c * 128:(c + 1) * 128], OT[:, f, 0:T], f == 0, f == KC - 1)
                    for k in range(KC):
                        MM(pg[:, 0:T], Wga[:, k, c * 128:(c + 1) * 128], xb[:, k, 0:T], k == 0, k == KC - 1)
                    sg = sga[c % 2]
                    ACT(sg[:, 0:T], pg[:, 0:T], AF.Sigmoid)
                    TT("vector", gst[:, c, 0:T], sg[:, 0:T], pa[:, 0:T], ALU.mult)
                DMA(TL(fm(G)[:, :, s0 + c0:s0 + c0 + T], dG.b), gst[:, :, 0:T])
        phase_end()

        ob3, g3, x23 = fm(OB), fm(G), fm(X2)
        hg3 = HGs.rearrange("(k p) n -> p k n", p=128)

        def bcast(tl, shape):
            return TL(tl.ap.broadcast_to(list(shape)), tl.b)

        def hgrn_phase(bwd):
            direction = 1 if bwd else 0
            areset()
            Whq = abf([128, KC, 1024])
            Whf = abf([128, KC, 1024])
            Whi = abf([128, KC, 1024])
            Whg = None if bwd else abf([128, KC, 1024])
            XB = [abf([128, KC, 256]) for _ in range(2)]
            Vtk = [abf([128, 2, 1024]) for _ in range(2)]
            Qa, Fa, La, Ba = (af32([128, 8, 256]) for _ in range(4))
            QD = [abf([128, 8, 256]) for _ in range(2)]
            KI = [abf([128, 8, 256]) for _ in range(2)]
            KTt = [abf([128, 2, 8, 128]) for _ in range(2)]
            AS = [abf([128, 2, 8, 128]) for _ in range(2)]
            DEC = [af32([128, 8, 4]) for _ in range(2)]
            S32 = af32([128, 8, 128])
            Sbf = abf([128, 8, 128])
            tmpS = af32([128, 8, 128])
            if bwd:
                OBst = [af32([128, 8, 256]) for _ in range(2)]
            else:
                OBa, Oa, SQa, SILa = (af32([128, 8, 256]) for _ in range(4))
                HGo = [abf([128, 8, 256]) for _ in range(2)]
            wload(Whq, win3[:, :, 1536:2560], KC)
            wload(Whf, win3[:, :, 3584:4608] if bwd else win3[:, :, 2560:3584], KC)
            wload(Whi, win3[:, :, 4608:5632], KC)
            if not bwd:
                wload(Whg, win3[:, :, 5632:6656], KC)
            pool_set(banks[4:8])
            pob = banks[0:4]
            lo = 8 * direction
            work = []
            for (s0, L, halo) in segs:
                if halo:
                    tl_ = ([(0, 128)] if not bwd else []) + tiles_of(128, L - 128) + ([(L - 128, 128)] if bwd else [])
                else:
                    tl_ = tiles_of(0, L)
                if bwd:
                    tl_ = tl_[::-1]
                for ti, (c0, T) in enumerate(tl_):
                    work.append(dict(s0=s0, L=L, halo=halo, c0=c0, T=T, first=(ti == 0)))

            def prologue(w, it_g):
                s0, L, halo, c0, T = w["s0"], w["L"], w["halo"], w["c0"], w["T"]
                i2 = it_g % 2
                is_halo_tile = halo and (c0 == 0 or c0 == L - 128)
                xb = XB[i2]
                vt = Vtk[i2]
                nb_ = T // 128
                nch = T // 64
                DMA(xb[:, :, 0:T], TL(x13[:, :, s0 + c0:s0 + c0 + T], dX1.b), eng="gpsimd")
                for bi in range(nb_):
                    for hf in range(2):
                        pv = PS()
                        for k in range(KC):
                            MM(pv[:, 0:512], xb[:, k, bi * 128:(bi + 1) * 128], Whi[:, k, hf * 512:(hf + 1) * 512], k == 0, k == KC - 1)
                        if is_halo_tile:
                            TS("vector", vt[:, bi, hf * 512:(hf + 1) * 512], pv[:, 0:512], flagL if c0 == 0 else flagR, None, ALU.mult)
                        else:
                            ACT(vt[:, bi, hf * 512:(hf + 1) * 512], pv[:, 0:512], AF.Copy)
                for (W_, dst, fn, sc) in ((Whq, Qa, AF.Silu, 1.0), (Whf, Fa, AF.Sigmoid, -1.0)):
                    for m in range(4):
                        pq = PS()
                        for hh in range(2):
                            h = 2 * m + hh
                            for k in range(KC):
                                MM(pq[:, hh * T:(hh + 1) * T], W_[:, k, h * 128:(h + 1) * 128], xb[:, k, 0:T], k == 0, k == KC - 1)
                        ACT(dst[:, 2 * m:2 * m + 2, 0:T], pq[:, 0:2 * T].r("p (a t) -> p a t", a=2), fn, scale=sc)
                Qv, Fv, Lv, Bv = Qa[:, :, 0:T], Fa[:, :, 0:T], La[:, :, 0:T], Ba[:, :, 0:T]
                TT("vector", Fv, Fv, bcast(DV[:, 16 + lo:24 + lo].r("p (h o) -> p h o", o=1), [128, 8, T]), ALU.mult)
                ACT(Lv, Fv, AF.Ln, bias=onec, scale=-1.0)
                for h in range(8):
                    P.op("vector", lambda e, o=Ba[:, h, 0:T], d1=La[:, h, 0:T]: e.tensor_tensor_scan(
                        out=o.ap, data0=onesT.ap[:, 0:T], data1=d1.ap, initial=0.0, op0=ALU.mult, op1=ALU.add),
                        [La, onesT], [Ba])
                if not bwd:
                    B4f = Ba[:, :, 0:T].r("p h (c t) -> p h c t", t=64)
                    for c in range(nch - 1, 0, -1):
                        TT("vector", B4f[:, :, c, :], B4f[:, :, c, :], bcast(B4f[:, :, c - 1, 63:64], [128, 8, 64]), ALU.subtract)
                if bwd:
                    B4 = Bv.r("p h (c t) -> p h c t", t=64)
                    L4 = Lv.r("p h (c t) -> p h c t", t=64)
                    TT("vector", Lv, Lv, Bv, ALU.subtract)
                    TT("vector", L4, L4, bcast(B4[:, :, :, 63:64], [128, 8, nch, 64]), ALU.add)
                    ACT(Bv, Lv, AF.Exp)
                    ACT(Lv, Lv, AF.Exp, scale=-1.0)
                    E1, E2 = Bv, Lv
                else:
                    ACT(Lv, Bv, AF.Exp)
                    ACT(Bv, Bv, AF.Exp, scale=-1.0)
                    E1, E2 = Lv, Bv
                qd, ki, kT, as_, dec = QD[i2], KI[i2], KTt[i2], AS[i2], DEC[i2]
                TT("gpsimd", qd[:, :, 0:T], Qv, E1, ALU.mult)
                TT("vector", ki[:, :, 0:T], Fv, E2, ALU.mult)
                dcol = 0 if bwd else 63
                CP("vector", dec[:, :, 0:nch].r("p h (c o) -> p h c o", o=1),
                   E1.r("p h (c t) -> p h c t", t=64)[:, :, :, dcol:dcol + 1])
                for bi in range(nb_):
                    pT = PS().bc(BF16)
                    for h in range(8):
                        TR(pT[:, h * 128:(h + 1) * 128], ki[:, h, bi * 128:(bi + 1) * 128])
                    ACT(kT[:, bi, :, :], pT[:, 0:1024].r("p (h d) -> p h d", h=8), AF.Copy)
                if not is_halo_tile:
                    tri = triB if bwd else triF
                    tri_b = bcast(tri.r("p (o t) -> p o t", o=1), [128, 4, 128])
                    for bi in range(nb_):
                        for j in range(2):
                            pA = PS()
                            for hh in range(4):
                                h = 4 * j + hh
                                MM(pA[:, hh * 128:(hh + 1) * 128], ki[:, h, bi * 128:(bi + 1) * 128], qd[:, h, bi * 128:(bi + 1) * 128])
                            TT("vector", as_[:, bi, 4 * j:4 * j + 4, :], pA[:, 0:512].r("p (a t) -> p a t", a=4), tri_b, ALU.mult)
                w.update(i2=i2, is_halo_tile=is_halo_tile, nb_=nb_, nch=nch)

            def epilogue(w):
                s0, c0, T, i2, is_halo_tile, nb_ = w["s0"], w["c0"], w["T"], w["i2"], w["is_halo_tile"], w["nb_"]
                xb, vt, qd, kT, as_, dec = XB[i2], Vtk[i2], QD[i2], KTt[i2], AS[i2], DEC[i2]
                if w["first"]:
                    MS("gpsimd", S32, 0.0)
                    MS("gpsimd", Sbf, 0.0)
                if not bwd and not is_halo_tile:
                    DMA(OBa[:, :, 0:T], TL(ob3[:, :, s0 + c0:s0 + c0 + T], dOB.b))
                    for m in range(4):
                        pg = PS()
                        for hh in range(2):
                            h = 2 * m + hh
                            for k in range(KC):
                                MM(pg[:, hh * T:(hh + 1) * T], Whg[:, k, h * 128:(h + 1) * 128], xb[:, k, 0:T], k == 0, k == KC - 1)
                        ACT(SILa[:, 2 * m:2 * m + 2, 0:T], pg[:, 0:2 * T].r("p (a t) -> p a t", a=2), AF.Silu)
                blks = range(nb_ - 1, -1, -1) if bwd else range(nb_)
                for bi in blks:
                    if not is_halo_tile:
                        for h in range(8):
                            MM(pob[h // 2][:, (h % 2) * T + bi * 128:(h % 2) * T + (bi + 1) * 128],
                               vt[:, bi, h * 128:(h + 1) * 128], as_[:, bi, h, :], (bi == blks[0] and h % 2 == 0), False)
                    chs = (1, 0) if bwd else (0, 1)
                    for ci, cc in enumerate(chs):
                        c = bi * 2 + cc
                        pU = [PS(), PS()]
                        for h in range(8):
                            MM(pU[h // 4][:, (h % 4) * 128:(h % 4 + 1) * 128], kT[cc * 64:(cc + 1) * 64, bi, h, :],
                               vt[cc * 64:(cc + 1) * 64, bi, h * 128:(h + 1) * 128], h % 4 == 0, True)
                        if not is_halo_tile:
                            for h in range(8):
                                MM(pob[h // 2][:, (h % 2) * T + c * 64:(h % 2) * T + (c + 1) * 64],
                                   Sbf[:, h, :], qd[:, h, c * 64:(c + 1) * 64], False, ci == 1)
                        for j in range(2):
                            TT("vector", tmpS[:, 4 * j:4 * j + 4, :], pU[j][:, 0:512].r("p (a t) -> p a t", a=4),
                               S32[:, 4 * j:4 * j + 4, :], ALU.add)
                        dec_b = bcast(dec[:, :, c:c + 1], [128, 8, 128])
                        TT("vector", S32, tmpS, dec_b, ALU.mult)
                        ACT(Sbf, S32, AF.Copy)
                if is_halo_tile:
                    return
                if bwd:
                    ost = OBst[i2]
                    for m in range(4):
                        ACT(ost[:, 2 * m:2 * m + 2, 0:T], pob[m][:, 0:2 * T].r("p (a t) -> p a t", a=2), AF.Copy)
                    DMA(TL(ob3[:, :, s0 + c0:s0 + c0 + T], dOB.b), ost[:, :, 0:T])
                else:
                    Ov, SQv, SIv = Oa[:, :, 0:T], SQa[:, :, 0:T], SILa[:, :, 0:T]
                    for m in range(4):
                        TT("vector", Oa[:, 2 * m:2 * m + 2, 0:T], pob[m][:, 0:2 * T].r("p (a t) -> p a t", a=2),
                           OBa[:, 2 * m:2 * m + 2, 0:T], ALU.add)
                    TT("vector", SQv, Ov, Ov, ALU.mult)
                    for m in range(4):
                        pm = PS()
                        for hh in range(2):
                            MM(pm[:, hh * T:(hh + 1) * T], ones128, SQa[:, 2 * m + hh, 0:T])
                        ACT(SQa[:, 2 * m:2 * m + 2, 0:T], pm[:, 0:2 * T].r("p (a t) -> p a t", a=2), AF.Ln, bias=epsRMS, scale=1.0)
                    ACT(SQv, SQv, AF.Exp, scale=-0.5)
                    TT("vector", Ov, Ov, SQv, ALU.mult)
                    ho = HGo[i2]
                    STT("vector", ho[:, :, 0:T], Ov, PR[:, 48:49], SIv, ALU.mult, ALU.mult)
                    DMA(TL(hg3[:, :, s0 + c0:s0 + c0 + T], dHG.b), ho[:, :, 0:T])

            prologue(work[0], 0)
            for wi in range(len(work)):
                if wi + 1 < len(work):
                    prologue(work[wi + 1], wi + 1)
                epilogue(work[wi])
            pool_set(banks)

        hgrn_phase(True)
        phase_end()
        hgrn_phase(False)
        phase_end()

        areset()
        T3 = 512
        Woh = abf([128, KC, 1024])
        Wgh = abf([128, KC, 1024])
        Wo = abf([128, KC, 1024])
        HGt = [abf([128, KC, T3]) for _ in range(2)]
        XB = [abf([128, KC, T3]) for _ in range(2)]
        Gt2 = [af32([128, KC, T3]) for _ in range(2)]
        X1r2 = [af32([128, KC, T3]) for _ in range(2)]
        sgh = [af32([128, T3]) for _ in range(2)]
        mg = abf([128, KC, T3])
        tmp = {"sq": [af32([128, T3]) for _ in range(2)], "mean": af32([128, T3]),
               "m2": af32([128, T3]), "rstd": af32([128, T3])}
        wload(Wgh, win3[:, :, 7680:8704], KC)
        wload(Woh, w_o_hgrn[0].rearrange("(k p) n -> p k n", p=128), KC)
        wload(Wo, w_out[0].rearrange("(k p) n -> p k n", p=128), KC)
        work3 = []
        for (s0, L, halo) in segs:
            for (c0, T) in (tiles_of(128, L - 128, T3) if halo else tiles_of(0, L, T3)):
                work3.append((s0, c0, T))

        def b3_loads(wi):
            s0, c0, T = work3[wi]
            i2 = wi % 2
            cs_ = slice(s0 + c0, s0 + c0 + T)
            DMA(XB[i2][:, :, 0:T], TL(x13[:, :, cs_], dX1.b), eng="gpsimd")
            DMA(HGt[i2][:, :, 0:T], TL(hg3[:, :, cs_], dHG.b))
            DMA(Gt2[i2][:, :, 0:T], TL(g3[:, :, cs_], dG.b))
            DMA(X1r2[i2][:, :, 0:T], TL(x13[:, :, cs_], dX1.b))

        b3_loads(0)
        pending3 = None
        for wi in range(len(work3)):
            if True:
                s0, c0, T = work3[wi]
                i2 = wi % 2
                if wi + 1 < len(work3) and pending3 is None:
                    b3_loads(wi + 1)
                xb, hgt = XB[i2], HGt[i2]
                Gt, X1r = Gt2[i2], X1r2[i2]
                cs_ = slice(s0 + c0, s0 + c0 + T)
                for c in range(KC):
                    ph, pg = PS(), PS()
                    for f in range(KC):
                        MM(ph[:, 0:T], Woh[:, f, c * 128:(c + 1) * 128], hgt[:, f, 0:T], f == 0, f == KC - 1)
                    for k in range(KC):
                        MM(pg[:, 0:T], Wgh[:, k, c * 128:(c + 1) * 128], xb[:, k, 0:T], k == 0, k == KC - 1)
                    sg = sgh[c % 2][:, 0:T]
                    ACT(sg, pg[:, 0:T], AF.Sigmoid)
                    TT("vector", sg, sg, ph[:, 0:T], ALU.mult)
                    TT("vector", mg[:, c, 0:T], sg, Gt[:, c, 0:T], ALU.add)
                    if pending3 is not None:
                        pending3[0](c)
                        if c == KC - 1:
                            pending3[1]()
                            pending3 = None
                            if wi + 1 < len(work3):
                                b3_loads(wi + 1)
                R = []
                for c in range(KC):
                    px = PS()
                    for f in range(KC):
                        MM(px[:, 0:T], Wo[:, f, c * 128:(c + 1) * 128], mg[:, f, 0:T], f == 0, f == KC - 1)
                    rc = X1r[:, c, 0:T]
                    STT("vector", rc, rc, ALPHA, px[:, 0:T], ALU.mult, ALU.add)
                    R.append(rc)
                apply3 = layer_norm(R, T, 16, tmp, defer=True)

                def store3(X1r=X1r, cs_=cs_, T=T):
                    DMA(TL(x23[:, :, cs_], dX2.b), X1r[:, :, 0:T])
                pending3 = (apply3, store3)
        if pending3 is not None:
            for c in range(KC):
                pending3[0](c)
            pending3[1]()
        phase_end()

        tl = [(c, c, T) for (c, T) in tiles_of(0, S0, TF)] + [(c, c - 128, T) for (c, T) in tiles_of(S0 + 128, S0 + 128 + Ls, TF)]
        dY = TL(yT)
        ffn_phase(X2, dX2, yT, dY, w_ffn2_in, w_ffn2_out, 32, tl)
        phase_end()
        P.emit(st)
    return nc, P


def _const_tables(core, n_cores):
    ct = np.zeros((128, 1536), np.float32)
    d = np.arange(128)
    dd = d % 64
    perm = np.where(dd < 8, d + 8, np.where(dd < 16, d - 8, d))
    ct[perm, d] = 1.0
    j = np.arange(128)[:, None]
    i = np.arange(128)[None, :]
    same = (j // 64) == (i // 64)
    ct[:, 128:256] = (same & (j <= i)).astype(np.float32)
    ct[:, 256:384] = (same & (j >= i)).astype(np.float32)
    qi = np.arange(128)[:, None]
    kj = np.arange(384)[None, :] - 128
    band = np.abs(kj - qi) <= 128
    mid = np.where(band, 0.0, NEG).astype(np.float32)
    ct[:, 384:768] = mid
    first = mid.copy()
    last = mid.copy()
    if core == 0:
        first[:, 0:128] = NEG
    if core == n_cores - 1:
        last[:, 256:384] = NEG
    ct[:, 768:1152] = first
    ct[:, 1152:1536] = last
    return ct


def _rope_table(pos):
    half = 8
    inv = (500000.0 ** (-np.arange(half, dtype=np.float32) / half)).astype(np.float32)
    ang = pos.astype(np.float32)[:, None] * inv[None, :]
    cos = np.cos(ang).astype(np.float32)
    sin = np.sin(ang).astype(np.float32)
    L = pos.shape[0]
    tab = np.zeros((2, 128, L), np.float32)
    tab[0] = 1.0
    for base in (0, 64):
        tab[0, base:base + 8] = cos.T
        tab[0, base + 8:base + 16] = cos.T
        tab[1, base:base + 8] = -sin.T
        tab[1, base + 8:base + 16] = sin.T
    return tab


def _fm8(v):
    return np.ascontiguousarray(v.reshape(8, 128).T)


def make_in_maps(inp, n_cores, nP, Lp, Ls):
    xp = np.asarray(inp["x_prompt"], np.float32)
    xs = np.asarray(inp["x_sample"], np.float32)[0]
    Lse = Ls + 256
    par = np.zeros((128, 128), np.float32)
    for i, nm in enumerate(["ln1_g", "ln1_b", "ln2_g", "ln2_b", "ln3_g", "ln3_b"]):
        par[:, 8 * i:8 * i + 8] = _fm8(np.asarray(inp[nm], np.float32)[0])
    par[:, 48] = np.asarray(inp["hgrn_norm_g"], np.float32)[0]
    lb = np.asarray(inp["hgrn_lb"], np.float32)
    par[:, 56:64] = _fm8(lb[0, 0])
    par[:, 64:72] = _fm8(lb[0, 1])
    par[:, 72:80] = _fm8(lb[1, 0])
    par[:, 80:88] = _fm8(lb[1, 1])
    par[:, 88:104] = np.asarray(inp["attn_sink"], np.float32)[0][None, :]
    ropeP = _rope_table(np.arange(Lp))
    wnames = ["ffn1_w_in", "ffn1_w_out", "w_in", "w_o_attn", "w_o_hgrn", "w_out", "ffn2_w_in", "ffn2_w_out"]
    wts = {k: np.ascontiguousarray(np.asarray(inp[k], np.float32)) for k in wnames}
    maps = []
    for c in range(n_cores):
        cols = [xp[c * nP + i].T for i in range(nP)]
        ext = np.zeros((Lse, D), np.float32)
        a, b = c * Ls - 128, (c + 1) * Ls + 128
        a2, b2 = max(a, 0), min(b, xs.shape[0])
        ext[a2 - a:b2 - a] = xs[a2:b2]
        cols.append(ext.T)
        p = par.copy()
        p[:, 104] = 1.0 if c > 0 else 0.0
        p[:, 105] = 1.0 if c < n_cores - 1 else 0.0
        m = {"xT": np.ascontiguousarray(np.concatenate(cols, axis=1)),
             "ctab": _const_tables(c, n_cores), "par": p, "ropeP": ropeP,
             "ropeS": _rope_table(np.arange(a, b))}
        m.update(wts)
        maps.append(m)
    return maps


def gather(results, n_cores, nP, Lp, Ls):
    yp = np.zeros((n_cores * nP, Lp, D), np.float32)
    ys = np.zeros((1, n_cores * Ls, D), np.float32)
    for c in range(n_cores):
        y = results[c]["yT"]
        for i in range(nP):
            yp[c * nP + i] = y[:, i * Lp:(i + 1) * Lp].T
        ys[0, c * Ls:(c + 1) * Ls] = y[:, nP * Lp:].T
    return yp, ys


_CACHE = {}


def kernel(**inputs):
    nP, Lp, Ls = 2, 4096, 2048
    if "nc" not in _CACHE:
        _CACHE["nc"] = build(nP, Lp, Ls)[0]
    nc = _CACHE["nc"]
    maps = make_in_maps(inputs, N_CORES, nP, Lp, Ls)
    res = run_bass_kernel_spmd(nc, maps, core_ids=list(range(N_CORES)))
    return gather(res.results, N_CORES, nP, Lp, Ls)
```

```python
import numpy as np
from contextlib import ExitStack
import concourse.bass as bass
import concourse.mybir as mybir
from concourse.bass_utils import run_bass_kernel_spmd

F32 = mybir.dt.float32
BF16 = mybir.dt.bfloat16
AF = mybir.ActivationFunctionType
ALU = mybir.AluOpType
AX = mybir.AxisListType
ENGS = ("tensor", "vector", "scalar", "gpsimd", "sync")

D = 1024
KC = 8
DFF = 2816
FC = 22
ALPHA = float(2.0 ** 0.25)
SCALE = 0.125
NEG = -1.0e30
N_CORES = 8
TF = 512
ARENA_WORDS = 50432
EPOCH = 30000
SAME_ENGINE_SYNC = True
NO_SELF_SYNC = ()
DEPOCH = 1800


class Buf:
    __slots__ = ("last_w", "readers")

    def __init__(self):
        self.last_w = None
        self.readers = {}


class TL:
    __slots__ = ("ap", "b")

    def __init__(self, ap, b=None):
        if type(ap).__name__ != "AP":
            ap = ap[:]
        self.ap = ap
        self.b = b if b is not None else Buf()

    def __getitem__(self, k):
        return TL(self.ap[k], self.b)

    def bc(self, dt):
        return TL(self.ap.bitcast(dt), self.b)

    def r(self, pat, **kw):
        return TL(self.ap.rearrange(pat, **kw), self.b)


class Prog:
    def __init__(self, nc):
        self.nc = nc
        self.streams = {e: [] for e in ENGS}
        self.count = {e: 0 for e in ENGS}
        self.chan_count = {}
        self.waited = {e: {} for e in ENGS}
        self.live = {}
        self.nops = 0

    def op(self, eng, fn, reads=(), writes=(), chan=None):
        deps = {}

        def add(tok):
            if tok is not None and deps.get(tok[0], 0) < tok[1]:
                deps[tok[0]] = tok[1]
        for t in reads:
            add(t.b.last_w)
        for t in writes:
            add(t.b.last_w)
            for kv in t.b.readers.items():
                add(kv)
        waits = []
        w = self.waited[eng]
        for k, v in deps.items():
            if chan is None and k.startswith(eng + "#") and (eng == "tensor" or eng in NO_SELF_SYNC):
                continue
            if w.get(k, 0) >= v:
                continue
            w[k] = v
            waits.append((k, v))
        if chan is None:
            n = self.count[eng]
            self.count[eng] = n + 1
            key = "%s#%d" % (eng, n // EPOCH)
            tok = (key, n % EPOCH + 1)
            inc = (key, 1)
        else:
            n = self.chan_count.get(chan, 0)
            self.chan_count[chan] = n + 1
            key = "dma:%s#%d" % (chan, n // DEPOCH)
            tok = (key, (n % DEPOCH + 1) * 16)
            inc = (key, 16)
        self.live[tok[0]] = tok[1]
        self.streams[eng].append((waits, fn, inc))
        for t in reads:
            if t.b.readers.get(tok[0], 0) < tok[1]:
                t.b.readers[tok[0]] = tok[1]
        for t in writes:
            t.b.last_w = tok
            t.b.readers = {}
        self.nops += 1
        return tok

    def barrier(self):
        allk = list(self.live.items())
        for e in ENGS:
            waits = []
            for k, v in allk:
                if k.startswith(e + "#"):
                    continue
                if self.waited[e].get(k, 0) >= v:
                    continue
                self.waited[e][k] = v
                waits.append((k, v))
            if waits:
                self.streams[e].append((waits, None, None))

    def emit(self, stack):
        nc = self.nc
        keys = set()
        for e in ENGS:
            for waits, fn, inc in self.streams[e]:
                if inc is not None:
                    keys.add(inc[0])
        sems = {k: stack.enter_context(nc.semaphore("s_" + k.replace(":", "_").replace("#", "_"))) for k in sorted(keys)}
        self.nsems = len(sems)
        block = stack.enter_context(nc.Block())

        def make(ename):
            ops = self.streams[ename]

            def body(e):
                for waits, fn, inc in ops:
                    for k, v in waits:
                        e.wait_ge(sems[k], v)
                    if fn is not None:
                        fn(e).then_inc(sems[inc[0]], inc[1])
            return body
        for ename in ENGS:
            if self.streams[ename]:
                getattr(block, ename)(make(ename))


def build(nP, Lp, Ls, debug=False):
    Lse = Ls + 256
    NT = nP * Lp + Lse
    NB = nP * Lp + Ls
    S0 = nP * Lp
    nc = bass.Bass("TRN2", target_bir_lowering=False)
    P = Prog(nc)

    def din(name, shape):
        return nc.dram_tensor(name, list(shape), F32, kind="ExternalInput").ap()

    xT = din("xT", [D, NT])
    ctab = din("ctab", [128, 1536])
    par = din("par", [128, 128])
    ropeP = din("ropeP", [2, 128, Lp])
    ropeS = din("ropeS", [2, 128, Lse])
    w_ffn1_in = din("ffn1_w_in", [1, D, 2 * DFF])
    w_ffn1_out = din("ffn1_w_out", [1, DFF, D])
    w_in = din("w_in", [1, D, 8704])
    w_o_attn = din("w_o_attn", [1, D, D])
    w_o_hgrn = din("w_o_hgrn", [1, D, D])
    w_out = din("w_out", [1, D, D])
    w_ffn2_in = din("ffn2_w_in", [1, D, 2 * DFF])
    w_ffn2_out = din("ffn2_w_out", [1, DFF, D])
    yT = nc.dram_tensor("yT", [D, NB], F32, kind="ExternalOutput").ap()
    skind = "ExternalOutput" if debug else "Internal"
    X1 = nc.dram_tensor("X1s", [D, NT], F32, kind=skind).ap()
    G = nc.dram_tensor("Gs", [D, NT], F32, kind=skind).ap()
    OB = nc.dram_tensor("OBs", [D, NT], F32, kind=skind).ap()
    X2 = nc.dram_tensor("X2s", [D, NT], F32, kind=skind).ap()
    HGs = nc.dram_tensor("HGs", [D, NT], BF16, kind="Internal").ap()
    dX1, dG, dOB, dX2, dHG = TL(X1), TL(G), TL(OB), TL(X2), TL(HGs)

    def fm(ap2d):
        return ap2d.rearrange("(k p) n -> p k n", p=128)

    st = ExitStack()
    with st:
        def sbt(name, shape, dt):
            return st.enter_context(nc.sbuf_tensor(name, list(shape), dt))

        arena = sbt("arena", [128, ARENA_WORDS], F32)
        identb = TL(sbt("identb", [128, 128], BF16))
        identf = TL(sbt("identf", [128, 128], F32))
        onesD = TL(sbt("onesD", [128, 128], F32))
        onesDb = TL(sbt("onesDb", [128, 128], BF16))
        ones128 = TL(sbt("ones128", [128, 128], F32))
        ones1 = TL(sbt("ones1", [128, 64], F32))
        onesT = TL(sbt("onesT", [128, 256], F32))
        cst = TL(sbt("cst", [128, 8], F32))
        CT = TL(sbt("ctabs", [128, 1536], F32))
        PR = TL(sbt("pars", [128, 128], F32))
        DV = TL(sbt("derived", [128, 64], F32))
        banks = [TL(st.enter_context(nc.psum_tensor("pb%d" % i, [128, 512], F32))) for i in range(8)]
        pctr = [0]
        pset = [banks]

        def pool_set(lst):
            pset[0] = list(lst)

        def PS():
            t = pset[0][pctr[0] % len(pset[0])]
            pctr[0] += 1
            return t

        def MM(out, lhsT, rhs, start=True, stop=True):
            P.op("tensor", lambda e: e.matmul(out.ap, lhsT=lhsT.ap, rhs=rhs.ap, start=start, stop=stop),
                 [lhsT, rhs], [out])

        def TR(out, in_):
            P.op("tensor", lambda e: e.transpose(out.ap, in_.ap, identb.ap), [in_, identb], [out])

        def ACT(out, in_, func, bias=None, scale=None, accum=None):
            rd = [in_]
            kw = {}
            wr = [out]
            if accum is not None:
                kw["accum_out"] = accum.ap
                wr.append(accum)
            if bias is not None:
                kw["bias"] = bias.ap
                rd.append(bias)
            if scale is not None:
                if isinstance(scale, TL):
                    kw["scale"] = scale.ap
                    rd.append(scale)
                else:
                    kw["scale"] = float(scale)
            P.op("scalar", lambda e: e.activation(out=out.ap, in_=in_.ap, func=func, **kw), rd, wr)

        def TT(eng, out, in0, in1, op):
            P.op(eng, lambda e: e.tensor_tensor(out=out.ap, in0=in0.ap, in1=in1.ap, op=op), [in0, in1], [out])

        def TS(eng, out, in0, s1, s2, op0, op1=None):
            rd = [in0]
            a1 = s1.ap if isinstance(s1, TL) else float(s1)
            if isinstance(s1, TL):
                rd.append(s1)
            if s2 is None:
                P.op(eng, lambda e: e.tensor_scalar(out=out.ap, in0=in0.ap, scalar1=a1, scalar2=None, op0=op0), rd, [out])
                return
            a2 = s2.ap if isinstance(s2, TL) else float(s2)
            if isinstance(s2, TL):
                rd.append(s2)
            P.op(eng, lambda e: e.tensor_scalar(out=out.ap, in0=in0.ap, scalar1=a1, scalar2=a2, op0=op0, op1=op1), rd, [out])

        def STT(eng, out, in0, s, in1, op0, op1):
            rd = [in0, in1]
            a = s.ap if isinstance(s, TL) else float(s)
            if isinstance(s, TL):
                rd.append(s)
            P.op(eng, lambda e: e.scalar_tensor_tensor(out=out.ap, in0=in0.ap, scalar=a, in1=in1.ap, op0=op0, op1=op1), rd, [out])

        def CP(eng, out, in_):
            P.op(eng, lambda e: e.tensor_copy(out=out.ap, in_=in_.ap), [in_], [out])

        def MS(eng, out, val):
            P.op(eng, lambda e: e.memset(out.ap, val), [], [out])

        chmap = {}

        def DMA(out, in_, eng="sync"):
            ch = chmap.setdefault((eng, id(out.b)), "%s%d" % (eng[0], len(chmap)))
            P.op(eng, lambda e: e.dma_start(out=out.ap, in_=in_.ap), [in_], [out], chan=ch)

        def phase_end():
            P.barrier()
            chmap.clear()

        apos = [0]

        def areset():
            apos[0] = 0

        def af32(shape):
            n = int(np.prod(shape[1:]))
            v = arena[:, apos[0]:apos[0] + n]
            apos[0] += n
            assert apos[0] <= ARENA_WORDS, apos[0]
            if len(shape) == 3:
                v = v.rearrange("p (a b) -> p a b", a=shape[1])
            return TL(v)

        def abf(shape):
            n = int(np.prod(shape[1:]))
            assert n % 2 == 0
            v = arena[:, apos[0]:apos[0] + n // 2].bitcast(BF16)
            apos[0] += n // 2
            assert apos[0] <= ARENA_WORDS, apos[0]
            if len(shape) == 3:
                v = v.rearrange("p (a b) -> p a b", a=shape[1])
            elif len(shape) == 4:
                v = v.rearrange("p (a b c) -> p a b c", a=shape[1], b=shape[2])
            return TL(v)

        def wload(dst, src3, nk, dsl=None):
            ch = chmap.setdefault(("gpsimd", id(dst.b)), "g%d" % len(chmap))
            tok = None
            for k in range(nk):
                o_ap = dst.ap[:, k, :] if dsl is None else dst.ap[:, k, dsl]
                i_ap = src3[:, k, :]
                tok = P.op("gpsimd", lambda e, o_ap=o_ap, i_ap=i_ap: e.dma_start(out=o_ap, in_=i_ap), [], [TL(o_ap)], chan=ch)
            if dst.b.last_w is None or dst.b.last_w[0] != tok[0] or dst.b.last_w[1] < tok[1]:
                dst.b.last_w = tok
            dst.b.readers = {}

        def tiles_of(c0, c1, T=256):
            out = []
            c = c0
            while c < c1:
                out.append((c, min(T, c1 - c)))
                c += T
            return out

        DMA(CT, TL(ctab))
        DMA(PR, TL(par))
        MS("gpsimd", identf, 0.0)
        P.op("gpsimd", lambda e: e.affine_select(out=identf.ap, in_=identf.ap, pattern=[[-1, 128]],
                                                 compare_op=ALU.not_equal, fill=1.0, base=0, channel_multiplier=1),
             [identf], [identf])
        CP("vector", identb, identf)
        MS("vector", onesD, 1.0 / D)
        MS("vector", onesDb, 1.0 / D)
        MS("vector", ones128, 1.0 / 128)
        MS("vector", ones1, 1.0)
        MS("vector", onesT, 1.0)
        MS("vector", cst[:, 0:1], 1e-5)
        MS("vector", cst[:, 1:2], 1e-6)
        MS("vector", cst[:, 2:3], 1.0)
        MS("vector", cst[:, 3:4], 0.0)
        MS("vector", cst[:, 4:5], 4e-5)
        epsLN, epsRMS, onec = cst[:, 0:1], cst[:, 1:2], cst[:, 2:3]
        TT("vector", DV[:, 0:8], PR[:, 56:64], PR[:, 64:72], ALU.subtract)
        TT("vector", DV[:, 8:16], PR[:, 72:80], PR[:, 80:88], ALU.subtract)
        ACT(DV[:, 0:16], DV[:, 0:16], AF.Sigmoid)
        TS("vector", DV[:, 16:32], DV[:, 0:16], -1.0, 1.0, ALU.mult, ALU.add)
        sinks = PR[:, 88:104]
        flagL, flagR = PR[:, 104:105], PR[:, 105:106]
        permT = CT[:, 0:128]
        triF, triB = CT[:, 128:256], CT[:, 256:384]
        maskMid, maskFirst, maskLast = CT[:, 384:768], CT[:, 768:1152], CT[:, 1152:1536]

        def layer_norm(R, T, gcol, tmp, eps=None, defer=False):
            eps = epsLN if eps is None else eps
            psM, psV = PS(), PS()
            for c in range(KC):
                sq = tmp["sq"][c % 2].bc(BF16)
                ACT(sq[:, 0:T], R[c], AF.Square)
                MM(psM[:, 0:T], onesD, R[c], start=(c == 0), stop=(c == KC - 1))
                MM(psV[:, 0:T], onesDb, sq[:, 0:T], start=(c == 0), stop=(c == KC - 1))
            mean, m2, rstd = tmp["mean"][:, 0:T], tmp["m2"][:, 0:T], tmp["rstd"][:, 0:T]
            ACT(mean, psM[:, 0:T], AF.Copy)
            ACT(m2, psM[:, 0:T], AF.Square)
            TT("vector", m2, psV[:, 0:T], m2, ALU.subtract)
            ACT(m2, m2, AF.Sqrt, bias=eps, scale=1.0)
            P.op("vector", lambda e: e.reciprocal(out=rstd.ap, in_=m2.ap), [m2], [rstd])
            def apply_chunk(c):
                TT("vector", R[c], R[c], mean, ALU.subtract)
                TT("vector", R[c], R[c], rstd, ALU.mult)
                ACT(R[c], R[c], AF.Identity, bias=PR[:, gcol + 8 + c:gcol + 9 + c], scale=PR[:, gcol + c:gcol + c + 1])
            if defer:
                return apply_chunk
            for c in range(KC):
                apply_chunk(c)

        def ffn_phase(src, dsrc, dst, ddst, wi_ap, wo_ap, gcol, tlist):
            areset()
            WI = abf([128, KC, 2 * DFF])
            WO = abf([128, FC, D])
            XB = [abf([128, KC, TF]) for _ in range(2)]
            XR = [af32([128, KC, TF])] * 2
            H = abf([128, FC, TF])
            SG = [af32([128, TF])] * 2
            tmp = {"sq": [af32([128, TF])] * 2, "mean": af32([128, TF]),
                   "m2": af32([128, TF]), "rstd": af32([128, TF])}
            wload(WI, wi_ap[0].rearrange("(k p) n -> p k n", p=128), KC)
            wload(WO, wo_ap[0].rearrange("(k p) n -> p k n", p=128), FC)
            s3 = fm(src)
            d3 = fm(dst)
            pending = None
            for it, (sc, dc, T) in enumerate(tlist):
                xb, xr = XB[it % 2], XR[it % 2]
                DMA(xb[:, :, 0:T], TL(s3[:, :, sc:sc + T], dsrc.b), eng="gpsimd")
                if pending is None:
                    DMA(xr[:, :, 0:T], TL(s3[:, :, sc:sc + T], dsrc.b))
                for j in range(FC):
                    pg, pu = PS(), PS()
                    for k in range(KC):
                        MM(pg[:, 0:T], WI[:, k, j * 128:(j + 1) * 128], xb[:, k, 0:T], k == 0, k == KC - 1)
                    for k in range(KC):
                        MM(pu[:, 0:T], WI[:, k, DFF + j * 128:DFF + (j + 1) * 128], xb[:, k, 0:T], k == 0, k == KC - 1)
                    sg = SG[j % 2]
                    ACT(sg[:, 0:T], pg[:, 0:T], AF.Silu)
                    TT("vector", H[:, j, 0:T], sg[:, 0:T], pu[:, 0:T], ALU.mult)
                    if pending is not None and j < KC:
                        pending[0](j)
                        if j == KC - 1:
                            pending[1]()
                            pending = None
                            DMA(xr[:, :, 0:T], TL(s3[:, :, sc:sc + T], dsrc.b))
                R = []
                for c in range(KC):
                    po = PS()
                    for j in range(FC):
                        MM(po[:, 0:T], WO[:, j, c * 128:(c + 1) * 128], H[:, j, 0:T], j == 0, j == FC - 1)
                    rc = xr[:, c, 0:T]
                    STT("vector", rc, rc, 2.0 * ALPHA, po[:, 0:T], ALU.mult, ALU.add)
                    R.append(rc)
                apply_fn = layer_norm(R, T, gcol, tmp, eps=cst[:, 4:5], defer=True)

                def store_fn(xr=xr, dc=dc, T=T):
                    DMA(TL(d3[:, :, dc:dc + T], ddst.b), xr[:, :, 0:T])
                pending = (apply_fn, store_fn)
            if pending is not None:
                for c in range(KC):
                    pending[0](c)
                pending[1]()

        tl = [(c, c, T) for (c, T) in tiles_of(0, S0, TF)] + [(S0, S0, 128)] + \
             [(c, c, T) for (c, T) in tiles_of(S0 + 128, S0 + 128 + Ls, TF)] + [(S0 + 128 + Ls, S0 + 128 + Ls, 128)]
        ffn_phase(xT, TL(xT), X1, dX1, w_ffn1_in, w_ffn1_out, 0, tl)
        phase_end()

        segs = [(i * Lp, Lp, False) for i in range(nP)] + [(S0, Lse, True)]
        x13 = fm(X1)
        win3 = w_in[0].rearrange("(k p) n -> p k n", p=128)

        areset()
        Wq = abf([128, KC, 1024])
        Wkd = abf([128, KC, 512])
        Wv = abf([128, KC, 256])
        Wga = abf([128, KC, 1024])
        Woa = abf([128, KC, 1024])
        Lmax = max(Lp, Lse)
        KT = abf([128, 4, Lmax])
        Vt = abf([128, Lmax // 128, 256])
        XB = [abf([128, KC, 256]) for _ in range(2)]
        RC = [af32([128, 256]) for _ in range(2)]
        RS = [af32([128, 256]) for _ in range(2)]
        QT = abf([128, KC, 256])
        QTl = [TL(QT.ap[:, c, :]) for c in range(KC)]
        qf = [af32([128, 256]) for _ in range(3)]
        t1 = [af32([128, 256]) for _ in range(2)]
        t2 = [af32([128, 256]) for _ in range(2)]
        sm = [af32([128, 392]) for _ in range(4)]
        Eb = [abf([128, 392]) for _ in range(4)]
        ETb = [abf([128, 384]) for _ in range(4)]
        cols = [af32([128, 8]) for _ in range(4)]
        hq = [0]
        Otok = [abf([128, 1024]) for _ in range(2)]
        OT = abf([128, KC, 256])
        sga = [af32([128, 256]) for _ in range(2)]
        Gst = [af32([128, KC, 256]) for _ in range(2)]
        wload(Wq, win3[:, :, 0:1024], KC)
        for g in range(4):
            for hf in range(2):
                wload(Wkd, win3[:, :, 1024 + g * 64:1024 + (g + 1) * 64], KC,
                      dsl=slice(g * 128 + hf * 64, g * 128 + hf * 64 + 64))
        wload(Wv, win3[:, :, 1280:1536], KC)
        wload(Wga, win3[:, :, 6656:7680], KC)
        wload(Woa, w_o_attn[0].rearrange("(k p) n -> p k n", p=128), KC)

        def rope_a1(ps_in, T, i):
            q32 = qf[i % 3]
            ACT(q32[:, 0:T], ps_in, AF.Copy)
            return [q32, None, i]

        def rope_a2(st_, T):
            pp = PS()
            MM(pp[:, 0:T], permT, st_[0][:, 0:T])
            st_[1] = pp

        def rope_pipe(nchunks, T, rc, rs, proj, outs):
            sts = {}
            for t in range(nchunks + 2):
                if t < nchunks:
                    sts[t] = rope_a1(proj(t), T, t)
                if 0 <= t - 1 < nchunks:
                    rope_a2(sts[t - 1], T)
                if 0 <= t - 2 < nchunks:
                    rope_b(sts[t - 2], T, rc, rs, outs[t - 2])

        def rope_b(st_, T, rc, rs, out_bf):
            q32, pp, i = st_
            a, b_ = t1[i % 2], t2[i % 2]
            TT("gpsimd", a[:, 0:T], q32[:, 0:T], rc[:, 0:T], ALU.mult)
            TT("vector", b_[:, 0:T], pp[:, 0:T], rs[:, 0:T], ALU.mult)
            TT("vector", out_bf, a[:, 0:T], b_[:, 0:T], ALU.add)

        gi = [0]
        for (s0, L, halo) in segs:
            rope3 = ropeS if halo else ropeP
            nblk = L // 128
            seg_tiles = ([(0, 128)] + tiles_of(128, L - 128) + [(L - 128, 128)]) if halo else tiles_of(0, L)
            for it, (c0, T) in enumerate(seg_tiles):
                xb = XB[gi[0] % 2]
                rc, rs = RC[gi[0] % 2], RS[gi[0] % 2]
                gi[0] += 1
                DMA(xb[:, :, 0:T], TL(x13[:, :, s0 + c0:s0 + c0 + T], dX1.b), eng="gpsimd")
                DMA(rc[:, 0:T], TL(rope3[0, :, c0:c0 + T]))
                DMA(rs[:, 0:T], TL(rope3[1, :, c0:c0 + T]))
                def proj_k(g, xb=xb, T=T):
                    pk = PS()
                    for k in range(KC):
                        MM(pk[:, 0:T], Wkd[:, k, g * 128:(g + 1) * 128], xb[:, k, 0:T], k == 0, k == KC - 1)
                    return pk[:, 0:T]
                rope_pipe(4, T, rc, rs, proj_k, [KT[:, g, c0:c0 + T] for g in range(4)])
                for bi in range(T // 128):
                    blk = (c0 // 128) + bi
                    pv = PS()
                    for k in range(KC):
                        MM(pv[:, 0:256], xb[:, k, bi * 128:(bi + 1) * 128], Wv[:, k, :], k == 0, k == KC - 1)
                    if halo and blk == 0:
                        TS("vector", Vt[:, blk, :], pv[:, 0:256], flagL, None, ALU.mult)
                    elif halo and blk == nblk - 1:
                        TS("vector", Vt[:, blk, :], pv[:, 0:256], flagR, None, ALU.mult)
                    else:
                        ACT(Vt[:, blk, :], pv[:, 0:256], AF.Copy)
            body_tiles = tiles_of(128, L - 128) if halo else tiles_of(0, L)
            for it, (c0, T) in enumerate(body_tiles):
                xb = XB[gi[0] % 2]
                rc, rs = RC[gi[0] % 2], RS[gi[0] % 2]
                gst = Gst[gi[0] % 2]
                gi[0] += 1
                DMA(xb[:, :, 0:T], TL(x13[:, :, s0 + c0:s0 + c0 + T], dX1.b), eng="gpsimd")
                DMA(rc[:, 0:T], TL(rope3[0, :, c0:c0 + T]))
                DMA(rs[:, 0:T], TL(rope3[1, :, c0:c0 + T]))
                def proj_q(c, xb=xb, T=T):
                    pq = PS()
                    for k in range(KC):
                        MM(pq[:, 0:T], Wq[:, k, c * 128:(c + 1) * 128], xb[:, k, 0:T], k == 0, k == KC - 1)
                    return pq[:, 0:T]
                rope_pipe(KC, T, rc, rs, proj_q, [QTl[c][:, 0:T] for c in range(KC)])
                pool_set(banks[2:8])
                items = []
                for bi in range(T // 128):
                    n = c0 // 128 + bi
                    kb0, kb1 = max(n - 1, 0), min(n + 1, nblk - 1)
                    nk = kb1 - kb0 + 1
                    if halo and n == 1:
                        mask = maskFirst
                    elif halo and n == nblk - 2:
                        mask = maskLast
                    else:
                        mask = maskMid
                    moff = (kb0 - (n - 1)) * 128
                    for h in range(16):
                        items.append(dict(bi=bi, n=n, kb0=kb0, kb1=kb1, nk=nk, mk=mask[:, moff:moff + nk * 128], h=h, ot=Otok[n % 2]))

                def stage1(it_):
                    h, nk, bi = it_["h"], it_["nk"], it_["bi"]
                    g = h // 4
                    hp = (h % 2) * 64
                    i3 = hq[0] % 4
                    hq[0] += 1
                    it_["i3"] = i3
                    nkw = nk * 128
                    pS = PS()
                    MM(pS[:, 0:nkw], QTl[h // 2][hp:hp + 64, bi * 128:(bi + 1) * 128],
                       KT[hp:hp + 64, g, it_["kb0"] * 128:(it_["kb1"] + 1) * 128])
                    s_ = sm[i3]
                    e_ = Eb[i3]
                    cl = cols[i3]
                    CP("gpsimd", s_[:, nkw:nkw + 1], sinks[:, h:h + 1])
                    STT("vector", s_[:, 0:nkw], pS[:, 0:nkw], SCALE, it_["mk"], ALU.mult, ALU.add)
                    P.op("vector", lambda e, o=cl[:, 1:2], i_=s_[:, 0:nkw + 1]: e.reduce_max(out=o.ap, in_=i_.ap, axis=AX.X, negate=True), [s_], [cl])
                    ACT(e_[:, 0:nkw + 1], s_[:, 0:nkw + 1], AF.Exp, bias=cl[:, 1:2], scale=1.0, accum=cl[:, 3:4])

                def stage2(it_):
                    h, nk, i3 = it_["h"], it_["nk"], it_["i3"]
                    nkw = nk * 128
                    e_ = Eb[i3]
                    cl = cols[i3]
                    P.op("vector", lambda e, o=cl[:, 5:6], i_=cl[:, 3:4]: e.reciprocal(out=o.ap, in_=i_.ap), [cl], [cl])
                    pTb = PS().bc(BF16)
                    for kb in range(nk):
                        TR(pTb[:, kb * 128:(kb + 1) * 128], e_[:, kb * 128:(kb + 1) * 128])
                    ACT(ETb[i3][:, 0:nkw], pTb[:, 0:nkw], AF.Copy)

                def stage3(it_):
                    h, nk, bi, i3, ot = it_["h"], it_["nk"], it_["bi"], it_["i3"], it_["ot"]
                    g, j = h // 4, h % 4
                    et = ETb[i3]
                    cl = cols[i3]
                    pO = banks[g % 2]
                    for kb in range(nk):
                        MM(pO[:, j * 64:(j + 1) * 64], et[:, kb * 128:(kb + 1) * 128],
                           Vt[:, it_["kb0"] + kb, g * 64:(g + 1) * 64], kb == 0, kb == nk - 1)
                    ACT(ot[:, h * 64:(h + 1) * 64], pO[:, j * 64:(j + 1) * 64], AF.Copy, scale=cl[:, 5:6])
                    if h == 15:
                        for c in range(KC):
                            pT = PS().bc(BF16)
                            TR(pT[:, 0:128], ot[:, c * 128:(c + 1) * 128])
                            ACT(OT[:, c, bi * 128:(bi + 1) * 128], pT[:, 0:128], AF.Copy)

                nit = len(items)
                for t in range(nit + 2):
                    if t < nit:
                        stage1(items[t])
                    if 0 <= t - 1 < nit:
                        stage2(items[t - 1])
                    if 0 <= t - 2 < nit:
                        stage3(items[t - 2])
                pool_set(banks)
                for c in range(KC):
                    pa, pg = PS(), PS()
                    for f in range(KC):
                        MM(pa[:, 0:T], Woa[:, f, c * 128:(c + 1) * 128], OT[:, f, 0:T], f == 0, f == KC - 1)
                    for k in range(KC):
                        MM(pg[:, 0:T], Wga[:, k, c * 128:(c + 1) * 128], xb[:, k, 0:T], k == 0, k == KC - 1)
                    sg = sga[c % 2]
                    ACT(sg[:, 0:T], pg[:, 0:T], AF.Sigmoid)
                    TT("vector", gst[:, c, 0:T], sg[:, 0:T], pa[:, 0:T], ALU.mult)
                DMA(TL(fm(G)[:, :, s0 + c0:s0 + c0 + T], dG.b), gst[:, :, 0:T])
        phase_end()

        ob3, g3, x23 = fm(OB), fm(G), fm(X2)
        hg3 = HGs.rearrange("(k p) n -> p k n", p=128)

        def bcast(tl, shape):
            return TL(tl.ap.broadcast_to(list(shape)), tl.b)

        def hgrn_phase(bwd):
            direction = 1 if bwd else 0
            areset()
            Whq = abf([128, KC, 1024])
            Whf = abf([128, KC, 1024])
            Whi = abf([128, KC, 1024])
            Whg = None if bwd else abf([128, KC, 1024])
            XB = [abf([128, KC, 256]) for _ in range(2)]
            Vtk = [abf([128, 2, 1024]) for _ in range(2)]
            Qa, Fa, La, Ba = (af32([128, 8, 256]) for _ in range(4))
            QD = [abf([128, 8, 256]) for _ in range(2)]
            KI = [abf([128, 8, 256]) for _ in range(2)]
            KTt = [abf([128, 2, 8, 128]) for _ in range(2)]
            AS = [abf([128, 2, 8, 128]) for _ in range(2)]
            DEC = [af32([128, 8, 4]) for _ in range(2)]
            S32 = af32([128, 8, 128])
            Sbf = abf([128, 8, 128])
            tmpS = af32([128, 8, 128])
            if bwd:
                OBst = [af32([128, 8, 256]) for _ in range(2)]
            else:
                OBa, Oa, SQa, SILa = (af32([128, 8, 256]) for _ in range(4))
                HGo = [abf([128, 8, 256]) for _ in range(2)]
            wload(Whq, win3[:, :, 1536:2560], KC)
            wload(Whf, win3[:, :, 3584:4608] if bwd else win3[:, :, 2560:3584], KC)
            wload(Whi, win3[:, :, 4608:5632], KC)
            if not bwd:
                wload(Whg, win3[:, :, 5632:6656], KC)
            pool_set(banks[4:8])
            pob = banks[0:4]
            lo = 8 * direction
            work = []
            for (s0, L, halo) in segs:
                if halo:
                    tl_ = ([(0, 128)] if not bwd else []) + tiles_of(128, L - 128) + ([(L - 128, 128)] if bwd else [])
                else:
                    tl_ = tiles_of(0, L)
                if bwd:
                    tl_ = tl_[::-1]
                for ti, (c0, T) in enumerate(tl_):
                    work.append(dict(s0=s0, L=L, halo=halo, c0=c0, T=T, first=(ti == 0)))

            def prologue(w, it_g):
                s0, L, halo, c0, T = w["s0"], w["L"], w["halo"], w["c0"], w["T"]
                i2 = it_g % 2
                is_halo_tile = halo and (c0 == 0 or c0 == L - 128)
                xb = XB[i2]
                vt = Vtk[i2]
                nb_ = T // 128
                nch = T // 64
                DMA(xb[:, :, 0:T], TL(x13[:, :, s0 + c0:s0 + c0 + T], dX1.b), eng="gpsimd")
                for bi in range(nb_):
                    for hf in range(2):
                        pv = PS()
                        for k in range(KC):
                            MM(pv[:, 0:512], xb[:, k, bi * 128:(bi + 1) * 128], Whi[:, k, hf * 512:(hf + 1) * 512], k == 0, k == KC - 1)
                        if is_halo_tile:
                            TS("vector", vt[:, bi, hf * 512:(hf + 1) * 512], pv[:, 0:512], flagL if c0 == 0 else flagR, None, ALU.mult)
                        else:
                            ACT(vt[:, bi, hf * 512:(hf + 1) * 512], pv[:, 0:512], AF.Copy)
                for (W_, dst, fn, sc) in ((Whq, Qa, AF.Silu, 1.0), (Whf, Fa, AF.Sigmoid, -1.0)):
                    for m in range(4):
                        pq = PS()
                        for hh in range(2):
                            h = 2 * m + hh
                            for k in range(KC):
                                MM(pq[:, hh * T:(hh + 1) * T], W_[:, k, h * 128:(h + 1) * 128], xb[:, k, 0:T], k == 0, k == KC - 1)
                        ACT(dst[:, 2 * m:2 * m + 2, 0:T], pq[:, 0:2 * T].r("p (a t) -> p a t", a=2), fn, scale=sc)
                Qv, Fv, Lv, Bv = Qa[:, :, 0:T], Fa[:, :, 0:T], La[:, :, 0:T], Ba[:, :, 0:T]
                TT("vector", Fv, Fv, bcast(DV[:, 16 + lo:24 + lo].r("p (h o) -> p h o", o=1), [128, 8, T]), ALU.mult)
                ACT(Lv, Fv, AF.Ln, bias=onec, scale=-1.0)
                for h in range(8):
                    P.op("vector", lambda e, o=Ba[:, h, 0:T], d1=La[:, h, 0:T]: e.tensor_tensor_scan(
                        out=o.ap, data0=onesT.ap[:, 0:T], data1=d1.ap, initial=0.0, op0=ALU.mult, op1=ALU.add),
                        [La, onesT], [Ba])
                if not bwd:
                    B4f = Ba[:, :, 0:T].r("p h (c t) -> p h c t", t=64)
                    for c in range(nch - 1, 0, -1):
                        TT("vector", B4f[:, :, c, :], B4f[:, :, c, :], bcast(B4f[:, :, c - 1, 63:64], [128, 8, 64]), ALU.subtract)
                if bwd:
                    B4 = Bv.r("p h (c t) -> p h c t", t=64)
                    L4 = Lv.r("p h (c t) -> p h c t", t=64)
                    TT("vector", Lv, Lv, Bv, ALU.subtract)
                    TT("vector", L4, L4, bcast(B4[:, :, :, 63:64], [128, 8, nch, 64]), ALU.add)
                    ACT(Bv, Lv, AF.Exp)
                    ACT(Lv, Lv, AF.Exp, scale=-1.0)
                    E1, E2 = Bv, Lv
                else:
                    ACT(Lv, Bv, AF.Exp)
                    ACT(Bv, Bv, AF.Exp, scale=-1.0)
                    E1, E2 = Lv, Bv
                qd, ki, kT, as_, dec = QD[i2], KI[i2], KTt[i2], AS[i2], DEC[i2]
                TT("gpsimd", qd[:, :, 0:T], Qv, E1, ALU.mult)
                TT("vector", ki[:, :, 0:T], Fv, E2, ALU.mult)
                dcol = 0 if bwd else 63
                CP("vector", dec[:, :, 0:nch].r("p h (c o) -> p h c o", o=1),
                   E1.r("p h (c t) -> p h c t", t=64)[:, :, :, dcol:dcol + 1])
                for bi in range(nb_):
                    pT = PS().bc(BF16)
                    for h in range(8):
                        TR(pT[:, h * 128:(h + 1) * 128], ki[:, h, bi * 128:(bi + 1) * 128])
                    ACT(kT[:, bi, :, :], pT[:, 0:1024].r("p (h d) -> p h d", h=8), AF.Copy)
                if not is_halo_tile:
                    tri = triB if bwd else triF
                    tri_b = bcast(tri.r("p (o t) -> p o t", o=1), [128, 4, 128])
                    for bi in range(nb_):
                        for j in range(2):
                            pA = PS()
                            for hh in range(4):
                                h = 4 * j + hh
                                MM(pA[:, hh * 128:(hh + 1) * 128], ki[:, h, bi * 128:(bi + 1) * 128], qd[:, h, bi * 128:(bi + 1) * 128])
                            TT("vector", as_[:, bi, 4 * j:4 * j + 4, :], pA[:, 0:512].r("p (a t) -> p a t", a=4), tri_b, ALU.mult)
                w.update(i2=i2, is_halo_tile=is_halo_tile, nb_=nb_, nch=nch)

            def epilogue(w):
                s0, c0, T, i2, is_halo_tile, nb_ = w["s0"], w["c0"], w["T"], w["i2"], w["is_halo_tile"], w["nb_"]
                xb, vt, qd, kT, as_, dec = XB[i2], Vtk[i2], QD[i2], KTt[i2], AS[i2], DEC[i2]
                if w["first"]:
                    MS("gpsimd", S32, 0.0)
                    MS("gpsimd", Sbf, 0.0)
                if not bwd and not is_halo_tile:
                    DMA(OBa[:, :, 0:T], TL(ob3[:, :, s0 + c0:s0 + c0 + T], dOB.b))
                    for m in range(4):
                        pg = PS()
                        for hh in range(2):
                            h = 2 * m + hh
                            for k in range(KC):
                                MM(pg[:, hh * T:(hh + 1) * T], Whg[:, k, h * 128:(h + 1) * 128], xb[:, k, 0:T], k == 0, k == KC - 1)
                        ACT(SILa[:, 2 * m:2 * m + 2, 0:T], pg[:, 0:2 * T].r("p (a t) -> p a t", a=2), AF.Silu)
                blks = range(nb_ - 1, -1, -1) if bwd else range(nb_)
                for bi in blks:
                    if not is_halo_tile:
                        for h in range(8):
                            MM(pob[h // 2][:, (h % 2) * T + bi * 128:(h % 2) * T + (bi + 1) * 128],
                               vt[:, bi, h * 128:(h + 1) * 128], as_[:, bi, h, :], (bi == blks[0] and h % 2 == 0), False)
                    chs = (1, 0) if bwd else (0, 1)
                    for ci, cc in enumerate(chs):
                        c = bi * 2 + cc
                        pU = [PS(), PS()]
                        for h in range(8):
                            MM(pU[h // 4][:, (h % 4) * 128:(h % 4 + 1) * 128], kT[cc * 64:(cc + 1) * 64, bi, h, :],
                               vt[cc * 64:(cc + 1) * 64, bi, h * 128:(h + 1) * 128], h % 4 == 0, True)
                        if not is_halo_tile:
                            for h in range(8):
                                MM(pob[h // 2][:, (h % 2) * T + c * 64:(h % 2) * T + (c + 1) * 64],
                                   Sbf[:, h, :], qd[:, h, c * 64:(c + 1) * 64], False, ci == 1)
                        for j in range(2):
                            TT("vector", tmpS[:, 4 * j:4 * j + 4, :], pU[j][:, 0:512].r("p (a t) -> p a t", a=4),
                               S32[:, 4 * j:4 * j + 4, :], ALU.add)
                        dec_b = bcast(dec[:, :, c:c + 1], [128, 8, 128])
                        TT("vector", S32, tmpS, dec_b, ALU.mult)
                        ACT(Sbf, S32, AF.Copy)
                if is_halo_tile:
                    return
                if bwd:
                    ost = OBst[i2]
                    for m in range(4):
                        ACT(ost[:, 2 * m:2 * m + 2, 0:T], pob[m][:, 0:2 * T].r("p (a t) -> p a t", a=2), AF.Copy)
                    DMA(TL(ob3[:, :, s0 + c0:s0 + c0 + T], dOB.b), ost[:, :, 0:T])
                else:
                    Ov, SQv, SIv = Oa[:, :, 0:T], SQa[:, :, 0:T], SILa[:, :, 0:T]
                    for m in range(4):
                        TT("vector", Oa[:, 2 * m:2 * m + 2, 0:T], pob[m][:, 0:2 * T].r("p (a t) -> p a t", a=2),
                           OBa[:, 2 * m:2 * m + 2, 0:T], ALU.add)
                    TT("vector", SQv, Ov, Ov, ALU.mult)
                    for m in range(4):
                        pm = PS()
                        for hh in range(2):
                            MM(pm[:, hh * T:(hh + 1) * T], ones128, SQa[:, 2 * m + hh, 0:T])
                        ACT(SQa[:, 2 * m:2 * m + 2, 0:T], pm[:, 0:2 * T].r("p (a t) -> p a t", a=2), AF.Ln, bias=epsRMS, scale=1.0)
                    ACT(SQv, SQv, AF.Exp, scale=-0.5)
                    TT("vector", Ov, Ov, SQv, ALU.mult)
                    ho = HGo[i2]
                    STT("vector", ho[:, :, 0:T], Ov, PR[:, 48:49], SIv, ALU.mult, ALU.mult)
                    DMA(TL(hg3[:, :, s0 + c0:s0 + c0 + T], dHG.b), ho[:, :, 0:T])

            prologue(work[0], 0)
            for wi in range(len(work)):
                if wi + 1 < len(work):
                    prologue(work[wi + 1], wi + 1)
                epilogue(work[wi])
            pool_set(banks)

        hgrn_phase(True)
        phase_end()
        hgrn_phase(False)
        phase_end()

        areset()
        T3 = 512
        Woh = abf([128, KC, 1024])
        Wgh = abf([128, KC, 1024])
        Wo = abf([128, KC, 1024])
        HGt = [abf([128, KC, T3]) for _ in range(2)]
        XB = [abf([128, KC, T3]) for _ in range(2)]
        Gt2 = [af32([128, KC, T3]) for _ in range(2)]
        X1r2 = [af32([128, KC, T3]) for _ in range(2)]
        sgh = [af32([128, T3]) for _ in range(2)]
        mg = abf([128, KC, T3])
        tmp = {"sq": [af32([128, T3]) for _ in range(2)], "mean": af32([128, T3]),
               "m2": af32([128, T3]), "rstd": af32([128, T3])}
        wload(Wgh, win3[:, :, 7680:8704], KC)
        wload(Woh, w_o_hgrn[0].rearrange("(k p) n -> p k n", p=128), KC)
        wload(Wo, w_out[0].rearrange("(k p) n -> p k n", p=128), KC)
        work3 = []
        for (s0, L, halo) in segs:
            for (c0, T) in (tiles_of(128, L - 128, T3) if halo else tiles_of(0, L, T3)):
                work3.append((s0, c0, T))

        def b3_loads(wi):
            s0, c0, T = work3[wi]
            i2 = wi % 2
            cs_ = slice(s0 + c0, s0 + c0 + T)
            DMA(XB[i2][:, :, 0:T], TL(x13[:, :, cs_], dX1.b), eng="gpsimd")
            DMA(HGt[i2][:, :, 0:T], TL(hg3[:, :, cs_], dHG.b))
            DMA(Gt2[i2][:, :, 0:T], TL(g3[:, :, cs_], dG.b))
            DMA(X1r2[i2][:, :, 0:T], TL(x13[:, :, cs_], dX1.b))

        b3_loads(0)
        pending3 = None
        for wi in range(len(work3)):
            if True:
                s0, c0, T = work3[wi]
                i2 = wi % 2
                if wi + 1 < len(work3) and pending3 is None:
                    b3_loads(wi + 1)
                xb, hgt = XB[i2], HGt[i2]
                Gt, X1r = Gt2[i2], X1r2[i2]
                cs_ = slice(s0 + c0, s0 + c0 + T)
                for c in range(KC):
                    ph, pg = PS(), PS()
                    for f in range(KC):
                        MM(ph[:, 0:T], Woh[:, f, c * 128:(c + 1) * 128], hgt[:, f, 0:T], f == 0, f == KC - 1)
                    for k in range(KC):
                        MM(pg[:, 0:T], Wgh[:, k, c * 128:(c + 1) * 128], xb[:, k, 0:T], k == 0, k == KC - 1)
                    sg = sgh[c % 2][:, 0:T]
                    ACT(sg, pg[:, 0:T], AF.Sigmoid)
                    TT("vector", sg, sg, ph[:, 0:T], ALU.mult)
                    TT("vector", mg[:, c, 0:T], sg, Gt[:, c, 0:T], ALU.add)
                    if pending3 is not None:
                        pending3[0](c)
                        if c == KC - 1:
                            pending3[1]()
                            pending3 = None
                            if wi + 1 < len(work3):
                                b3_loads(wi + 1)
                R = []
                for c in range(KC):
                    px = PS()
                    for f in range(KC):
                        MM(px[:, 0:T], Wo[:, f, c * 128:(c + 1) * 128], mg[:, f, 0:T], f == 0, f == KC - 1)
                    rc = X1r[:, c, 0:T]
                    STT("vector", rc, rc, ALPHA, px[:, 0:T], ALU.mult, ALU.add)
                    R.append(rc)
                apply3 = layer_norm(R, T, 16, tmp, defer=True)

                def store3(X1r=X1r, cs_=cs_, T=T):
                    DMA(TL(x23[:, :, cs_], dX2.b), X1r[:, :, 0:T])
                pending3 = (apply3, store3)
        if pending3 is not None:
            for c in range(KC):
                pending3[0](c)
            pending3[1]()
        phase_end()

        tl = [(c, c, T) for (c, T) in tiles_of(0, S0, TF)] + [(c, c - 128, T) for (c, T) in tiles_of(S0 + 128, S0 + 128 + Ls, TF)]
        dY = TL(yT)
        ffn_phase(X2, dX2, yT, dY, w_ffn2_in, w_ffn2_out, 32, tl)
        phase_end()
        P.emit(st)
    return nc, P


def _const_tables(core, n_cores):
    ct = np.zeros((128, 1536), np.float32)
    d = np.arange(128)
    dd = d % 64
    perm = np.where(dd < 8, d + 8, np.where(dd < 16, d - 8, d))
    ct[perm, d] = 1.0
    j = np.arange(128)[:, None]
    i = np.arange(128)[None, :]
    same = (j // 64) == (i // 64)
    ct[:, 128:256] = (same & (j <= i)).astype(np.float32)
    ct[:, 256:384] = (same & (j >= i)).astype(np.float32)
    qi = np.arange(128)[:, None]
    kj = np.arange(384)[None, :] - 128
    band = np.abs(kj - qi) <= 128
    mid = np.where(band, 0.0, NEG).astype(np.float32)
    ct[:, 384:768] = mid
    first = mid.copy()
    last = mid.copy()
    if core == 0:
        first[:, 0:128] = NEG
    if core == n_cores - 1:
        last[:, 256:384] = NEG
    ct[:, 768:1152] = first
    ct[:, 1152:1536] = last
    return ct


def _rope_table(pos):
    half = 8
    inv = (500000.0 ** (-np.arange(half, dtype=np.float32) / half)).astype(np.float32)
    ang = pos.astype(np.float32)[:, None] * inv[None, :]
    cos = np.cos(ang).astype(np.float32)
    sin = np.sin(ang).astype(np.float32)
    L = pos.shape[0]
    tab = np.zeros((2, 128, L), np.float32)
    tab[0] = 1.0
    for base in (0, 64):
        tab[0, base:base + 8] = cos.T
        tab[0, base + 8:base + 16] = cos.T
        tab[1, base:base + 8] = -sin.T
        tab[1, base + 8:base + 16] = sin.T
    return tab


def _fm8(v):
    return np.ascontiguousarray(v.reshape(8, 128).T)


def make_in_maps(inp, n_cores, nP, Lp, Ls):
    xp = np.asarray(inp["x_prompt"], np.float32)
    xs = np.asarray(inp["x_sample"], np.float32)[0]
    Lse = Ls + 256
    par = np.zeros((128, 128), np.float32)
    for i, nm in enumerate(["ln1_g", "ln1_b", "ln2_g", "ln2_b", "ln3_g", "ln3_b"]):
        par[:, 8 * i:8 * i + 8] = _fm8(np.asarray(inp[nm], np.float32)[0])
    par[:, 48] = np.asarray(inp["hgrn_norm_g"], np.float32)[0]
    lb = np.asarray(inp["hgrn_lb"], np.float32)
    par[:, 56:64] = _fm8(lb[0, 0])
    par[:, 64:72] = _fm8(lb[0, 1])
    par[:, 72:80] = _fm8(lb[1, 0])
    par[:, 80:88] = _fm8(lb[1, 1])
    par[:, 88:104] = np.asarray(inp["attn_sink"], np.float32)[0][None, :]
    ropeP = _rope_table(np.arange(Lp))
    wnames = ["ffn1_w_in", "ffn1_w_out", "w_in", "w_o_attn", "w_o_hgrn", "w_out", "ffn2_w_in", "ffn2_w_out"]
    wts = {k: np.ascontiguousarray(np.asarray(inp[k], np.float32)) for k in wnames}
    maps = []
    for c in range(n_cores):
        cols = [xp[c * nP + i].T for i in range(nP)]
        ext = np.zeros((Lse, D), np.float32)
        a, b = c * Ls - 128, (c + 1) * Ls + 128
        a2, b2 = max(a, 0), min(b, xs.shape[0])
        ext[a2 - a:b2 - a] = xs[a2:b2]
        cols.append(ext.T)
        p = par.copy()
        p[:, 104] = 1.0 if c > 0 else 0.0
        p[:, 105] = 1.0 if c < n_cores - 1 else 0.0
        m = {"xT": np.ascontiguousarray(np.concatenate(cols, axis=1)),
             "ctab": _const_tables(c, n_cores), "par": p, "ropeP": ropeP,
             "ropeS": _rope_table(np.arange(a, b))}
        m.update(wts)
        maps.append(m)
    return maps


def gather(results, n_cores, nP, Lp, Ls):
    yp = np.zeros((n_cores * nP, Lp, D), np.float32)
    ys = np.zeros((1, n_cores * Ls, D), np.float32)
    for c in range(n_cores):
        y = results[c]["yT"]
        for i in range(nP):
            yp[c * nP + i] = y[:, i * Lp:(i + 1) * Lp].T
        ys[0, c * Ls:(c + 1) * Ls] = y[:, nP * Lp:].T
    return yp, ys


_CACHE = {}


def kernel(**inputs):
    nP, Lp, Ls = 2, 4096, 2048
    if "nc" not in _CACHE:
        _CACHE["nc"] = build(nP, Lp, Ls)[0]
    nc = _CACHE["nc"]
    maps = make_in_maps(inputs, N_CORES, nP, Lp, Ls)
    res = run_bass_kernel_spmd(nc, maps, core_ids=list(range(N_CORES)))
    return gather(res.results, N_CORES, nP, Lp, Ls)
```
